# Optimizing a Trainium2 kernel written in Bass

```python
import math
import jax
import jax.numpy as jnp
from jax import lax
import numpy as np

D_MODEL = 1024
BATCH = 8
SEQ = 4096
DEPTH = 4

GRID_W = 64
EPS = 1e-6

S5_WIDTH = D_MODEL // 2
S5_GROUP = 16
S5_GROUPS = S5_WIDTH // S5_GROUP
S5_STATE = 64
S5_MAX_RE = -1e-4

HEAD_DIM = 64
N_Q_HEADS = (D_MODEL // 2) // HEAD_DIM
N_KV_HEADS = 2
Q_PER_KV = N_Q_HEADS // N_KV_HEADS
ATTN_WIDTH = N_Q_HEADS * HEAD_DIM
KV_WIDTH = N_KV_HEADS * HEAD_DIM
ROPE_AXIS_DIM = HEAD_DIM // 2
ROPE_FREQS = ROPE_AXIS_DIM // 2
ROPE_BASE = 10000.0
Q_BLOCK = 128

HYB_IN = S5_WIDTH + ATTN_WIDTH + 2 * KV_WIDTH
HYB_OUT = S5_WIDTH + ATTN_WIDTH

LRU_WIDTH = D_MODEL
LRU_HEADS = 16
LRU_BLOCK = LRU_WIDTH // LRU_HEADS
CONV_WIDTH = 4
CONV_LEFT = 2
LRU_C = 8.0

D_FF = 4 * D_MODEL

N_EVEN = (DEPTH + 1) // 2
N_ODD = DEPTH // 2

kernel_name = "hybrid_s5_gqa_rglru_encoder"


def rms_norm(x, w):
    xf = x.astype(jnp.float32)
    y = xf * lax.rsqrt(jnp.mean(xf * xf, axis=-1, keepdims=True) + EPS)
    return (y * w.astype(jnp.float32)).astype(x.dtype)


def _linear_combine(e1, e2):
    a1, b1 = e1
    a2, b2 = e2
    return a1 * a2, a2 * b1 + b2


def _complex_linear_combine(e1, e2):
    ar1, ai1, br1, bi1 = e1
    ar2, ai2, br2, bi2 = e2
    return (ar1 * ar2 - ai1 * ai2,
            ar1 * ai2 + ai1 * ar2,
            ar2 * br1 - ai2 * bi1 + br2,
            ar2 * bi1 + ai2 * br1 + bi2)


def s5_mixer(u, lam_re, lam_im, log_dt, b_re, b_im, c_re, c_im, d, glu_w, glu_b):
    bsz, seq, _ = u.shape
    f32 = jnp.float32
    uf = u.astype(f32).reshape(bsz, seq, S5_GROUPS, S5_GROUP)
    y = uf * d.astype(f32).reshape(S5_GROUPS, S5_GROUP)
    for direction in range(2):
        lr = jnp.minimum(lam_re[direction].astype(f32), S5_MAX_RE)
        li = lam_im[direction].astype(f32)
        dt = jnp.exp(log_dt[direction].astype(f32))[:, None]
        mag = jnp.exp(lr * dt)
        abar_re = mag * jnp.cos(li * dt)
        abar_im = mag * jnp.sin(li * dt)
        den = lr * lr + li * li
        nr = abar_re - 1.0
        f_re = (nr * lr + abar_im * li) / den
        f_im = (abar_im * lr - nr * li) / den
        br = b_re[direction].astype(f32)
        bi = b_im[direction].astype(f32)
        bb_re = f_re[..., None] * br - f_im[..., None] * bi
        bb_im = f_re[..., None] * bi + f_im[..., None] * br
        bu_re = jnp.einsum("bsgc,gpc->bsgp", uf, bb_re)
        bu_im = jnp.einsum("bsgc,gpc->bsgp", uf, bb_im)
        a_shape = (1, seq, S5_GROUPS, S5_STATE)
        _, _, h_re, h_im = lax.associative_scan(
            _complex_linear_combine,
            (jnp.broadcast_to(abar_re, a_shape), jnp.broadcast_to(abar_im, a_shape), bu_re, bu_im),
            reverse=(direction == 1), axis=1)
        y = y + (jnp.einsum("bsgp,gcp->bsgc", h_re, c_re[direction].astype(f32))
                 - jnp.einsum("bsgp,gcp->bsgc", h_im, c_im[direction].astype(f32)))
    y = y.reshape(bsz, seq, S5_WIDTH)
    g = jax.nn.gelu(y)
    out = g * jax.nn.sigmoid(g @ glu_w.astype(f32) + glu_b.astype(f32))
    return out.astype(u.dtype)


def axial_rope_tables(seq):
    rows = seq // GRID_W
    row_idx = jnp.repeat(jnp.arange(rows, dtype=jnp.float32), GRID_W)
    col_idx = jnp.tile(jnp.arange(GRID_W, dtype=jnp.float32), rows)
    inv_freq = ROPE_BASE ** (-jnp.arange(ROPE_FREQS, dtype=jnp.float32) / ROPE_FREQS)
    ang = jnp.stack([row_idx[:, None] * inv_freq, col_idx[:, None] * inv_freq], axis=1)
    return jnp.cos(ang), jnp.sin(ang)


def apply_axial_rope(x, cos, sin):
    shp = x.shape
    xs = x.reshape(shp[0], shp[1], shp[2], 2, 2, ROPE_FREQS)
    x1, x2 = xs[..., 0, :], xs[..., 1, :]
    c = cos[None, :, None]
    s = sin[None, :, None]
    return jnp.stack([x1 * c - x2 * s, x2 * c + x1 * s], axis=-2).reshape(shp)


def axial_gqa(q, k, v, q_norm, k_norm, cos, sin):
    bsz, seq, _ = q.shape
    f32 = jnp.float32
    qf = rms_norm(q.astype(f32).reshape(bsz, seq, N_Q_HEADS, HEAD_DIM), q_norm)
    kf = rms_norm(k.astype(f32).reshape(bsz, seq, N_KV_HEADS, HEAD_DIM), k_norm)
    qf = apply_axial_rope(qf, cos, sin) * (HEAD_DIM ** -0.5)
    kf = apply_axial_rope(kf, cos, sin)
    vf = v.astype(f32).reshape(bsz, seq, N_KV_HEADS, HEAD_DIM)
    n_blk = seq // Q_BLOCK
    qb = qf.reshape(bsz, n_blk, Q_BLOCK, N_KV_HEADS, Q_PER_KV, HEAD_DIM).transpose(1, 0, 2, 3, 4, 5)

    def block(q_blk):
        s = jnp.einsum("bqkgd,bskd->bkgqs", q_blk, kf)
        p = jax.nn.softmax(s, axis=-1)
        return jnp.einsum("bkgqs,bskd->bqkgd", p, vf)

    o = lax.map(block, qb)
    o = o.transpose(1, 0, 2, 3, 4, 5).reshape(bsz, seq, ATTN_WIDTH)
    return o.astype(q.dtype)


def rglru_mixer(z, conv_w, conv_b, ra_w, ra_b, ix_w, ix_b, lam):
    bsz, seq, _ = z.shape
    f32 = jnp.float32
    zf = z.astype(f32)
    gate, xr = zf[..., :LRU_WIDTH], zf[..., LRU_WIDTH:]
    xc = lax.conv_general_dilated(
        xr, conv_w.astype(f32)[:, None, :], window_strides=(1,),
        padding=[(CONV_LEFT, CONV_WIDTH - 1 - CONV_LEFT)],
        dimension_numbers=("NWC", "WIO", "NWC"),
        feature_group_count=LRU_WIDTH) + conv_b.astype(f32)
    xb = xc.reshape(bsz, seq, LRU_HEADS, LRU_BLOCK)
    hs = []
    for direction in range(2):
        r = jax.nn.sigmoid(jnp.einsum("bshi,hij->bshj", xb, ra_w[direction].astype(f32)).reshape(bsz, seq, LRU_WIDTH)
                           + ra_b[direction].astype(f32))
        i = jax.nn.sigmoid(jnp.einsum("bshi,hij->bshj", xb, ix_w[direction].astype(f32)).reshape(bsz, seq, LRU_WIDTH)
                           + ix_b[direction].astype(f32))
        log_a = -LRU_C * r * jax.nn.softplus(-lam[direction].astype(f32))
        a = jnp.exp(log_a)
        b = jnp.sqrt(-jnp.expm1(2.0 * log_a)) * (i * xc)
        _, h = lax.associative_scan(_linear_combine, (a, b), reverse=(direction == 1), axis=1)
        hs.append(h)
    y = hs[0] + hs[1]
    return (y * jax.nn.gelu(gate)).astype(z.dtype)


def setup_inputs(seed: int = 0) -> dict:
    key = jax.random.key(seed)
    keys = iter(jax.random.split(key, 48))

    def normal(shape, std):
        return std * jax.random.normal(next(keys), shape, jnp.float32)

    def uniform(shape, lo, hi):
        return jax.random.uniform(next(keys), shape, jnp.float32, lo, hi)

    D = D_MODEL
    x = normal((BATCH, SEQ, D), 1.0)
    c = normal((BATCH, D), 1.0)
    norm_w = 1.0 + normal((DEPTH, 2, D), 0.02)
    ada_w = normal((DEPTH, D, 6 * D), 0.1 * D ** -0.5)
    gate_offset = jnp.tile(jnp.repeat(jnp.array([0.0, 0.0, 1.0], jnp.float32), D), 2)
    ada_b = normal((DEPTH, 6 * D), 0.02) + gate_offset
    mlp_w1 = normal((DEPTH, D, D_FF), D ** -0.5)
    mlp_w2 = normal((DEPTH, D_FF, D), D_FF ** -0.5)
    final_norm_w = 1.0 + normal((D,), 0.02)

    hyb_w_in = normal((N_EVEN, D, HYB_IN), D ** -0.5)
    s5_lam_re = -0.5 + normal((N_EVEN, 2, S5_GROUPS, S5_STATE), 0.01)
    s5_lam_im = jnp.pi * jnp.arange(S5_STATE, dtype=jnp.float32) + normal((N_EVEN, 2, S5_GROUPS, S5_STATE), 0.01)
    s5_log_dt = uniform((N_EVEN, 2, S5_GROUPS), math.log(1e-3), math.log(1e-1))
    s5_b_re = normal((N_EVEN, 2, S5_GROUPS, S5_STATE, S5_GROUP), (2 * S5_GROUP) ** -0.5)
    s5_b_im = normal((N_EVEN, 2, S5_GROUPS, S5_STATE, S5_GROUP), (2 * S5_GROUP) ** -0.5)
    s5_c_re = normal((N_EVEN, 2, S5_GROUPS, S5_GROUP, S5_STATE), S5_STATE ** -0.5)
    s5_c_im = normal((N_EVEN, 2, S5_GROUPS, S5_GROUP, S5_STATE), S5_STATE ** -0.5)
    s5_d = normal((N_EVEN, S5_WIDTH), 1.0)
    s5_glu_w = normal((N_EVEN, S5_WIDTH, S5_WIDTH), S5_WIDTH ** -0.5)
    s5_glu_b = normal((N_EVEN, S5_WIDTH), 0.01)
    attn_q_norm = 1.0 + normal((N_EVEN, HEAD_DIM), 0.02)
    attn_k_norm = 1.0 + normal((N_EVEN, HEAD_DIM), 0.02)
    hyb_w_out = normal((N_EVEN, HYB_OUT, D), HYB_OUT ** -0.5)

    rec_w_in = normal((N_ODD, D, 2 * LRU_WIDTH), D ** -0.5)
    rec_conv_w = normal((N_ODD, CONV_WIDTH, LRU_WIDTH), CONV_WIDTH ** -0.5)
    rec_conv_b = normal((N_ODD, LRU_WIDTH), 0.01)
    rec_ra_w = normal((N_ODD, 2, LRU_HEADS, LRU_BLOCK, LRU_BLOCK), LRU_BLOCK ** -0.5)
    rec_ra_b = normal((N_ODD, 2, LRU_WIDTH), 0.01)
    rec_ix_w = normal((N_ODD, 2, LRU_HEADS, LRU_BLOCK, LRU_BLOCK), LRU_BLOCK ** -0.5)
    rec_ix_b = normal((N_ODD, 2, LRU_WIDTH), 0.01)
    a_c = uniform((N_ODD, 2, LRU_WIDTH), 0.9, 0.999)
    s = a_c ** (1.0 / LRU_C)
    rec_lam = jnp.log(s) - jnp.log1p(-s)
    rec_w_out = normal((N_ODD, LRU_WIDTH, D), LRU_WIDTH ** -0.5)

    return {"x": x, "c": c, "norm_w": norm_w, "ada_w": ada_w, "ada_b": ada_b,
            "mlp_w1": mlp_w1, "mlp_w2": mlp_w2, "final_norm_w": final_norm_w,
            "hyb_w_in": hyb_w_in, "s5_lam_re": s5_lam_re, "s5_lam_im": s5_lam_im,
            "s5_log_dt": s5_log_dt, "s5_b_re": s5_b_re, "s5_b_im": s5_b_im,
            "s5_c_re": s5_c_re, "s5_c_im": s5_c_im, "s5_d": s5_d,
            "s5_glu_w": s5_glu_w, "s5_glu_b": s5_glu_b,
            "attn_q_norm": attn_q_norm, "attn_k_norm": attn_k_norm, "hyb_w_out": hyb_w_out,
            "rec_w_in": rec_w_in, "rec_conv_w": rec_conv_w, "rec_conv_b": rec_conv_b,
            "rec_ra_w": rec_ra_w, "rec_ra_b": rec_ra_b, "rec_ix_w": rec_ix_w,
            "rec_ix_b": rec_ix_b, "rec_lam": rec_lam, "rec_w_out": rec_w_out}


def reference(x, c, norm_w, ada_w, ada_b, mlp_w1, mlp_w2, final_norm_w,
              hyb_w_in, s5_lam_re, s5_lam_im, s5_log_dt, s5_b_re, s5_b_im,
              s5_c_re, s5_c_im, s5_d, s5_glu_w, s5_glu_b,
              attn_q_norm, attn_k_norm, hyb_w_out,
              rec_w_in, rec_conv_w, rec_conv_b, rec_ra_w, rec_ra_b, rec_ix_w,
              rec_ix_b, rec_lam, rec_w_out):
    seq = x.shape[1]
    cos, sin = axial_rope_tables(seq)
    c_act = jax.nn.silu(c)
    h = x
    for layer in range(DEPTH):
        mod = (c_act @ ada_w[layer] + ada_b[layer])[:, None, :]
        sh1, sc1, g1, sh2, sc2, g2 = jnp.split(mod, 6, axis=-1)
        u = rms_norm(h, norm_w[layer, 0]) * (1.0 + sc1) + sh1
        if layer % 2 == 0:
            e = layer // 2
            z = u @ hyb_w_in[e]
            z_s5 = z[..., :S5_WIDTH]
            z_q = z[..., S5_WIDTH:S5_WIDTH + ATTN_WIDTH]
            z_k = z[..., S5_WIDTH + ATTN_WIDTH:S5_WIDTH + ATTN_WIDTH + KV_WIDTH]
            z_v = z[..., S5_WIDTH + ATTN_WIDTH + KV_WIDTH:]
            y_s5 = s5_mixer(z_s5, s5_lam_re[e], s5_lam_im[e], s5_log_dt[e], s5_b_re[e], s5_b_im[e],
                            s5_c_re[e], s5_c_im[e], s5_d[e], s5_glu_w[e], s5_glu_b[e])
            y_att = axial_gqa(z_q, z_k, z_v, attn_q_norm[e], attn_k_norm[e], cos, sin)
            mix = jnp.concatenate([y_s5, y_att], axis=-1) @ hyb_w_out[e]
        else:
            o = layer // 2
            z = u @ rec_w_in[o]
            mix = rglru_mixer(z, rec_conv_w[o], rec_conv_b[o], rec_ra_w[o], rec_ra_b[o],
                              rec_ix_w[o], rec_ix_b[o], rec_lam[o]) @ rec_w_out[o]
        h = h + g1 * mix
        u = rms_norm(h, norm_w[layer, 1]) * (1.0 + sc2) + sh2
        ff = jnp.square(jax.nn.relu(u @ mlp_w1[layer])) @ mlp_w2[layer]
        h = h + g2 * ff
    return rms_norm(h, final_norm_w)
```

```python
import math
import numpy as np
from contextlib import ExitStack
import concourse.bass as bass
import concourse.mybir as mybir
from concourse.bass_utils import run_bass_kernel_spmd

F32 = mybir.dt.float32
BF16 = mybir.dt.bfloat16
I32 = mybir.dt.int32
AF = mybir.ActivationFunctionType
ALU = mybir.AluOpType

S = 4096
D = 1024
EPS = 1e-6
NT = 8
TW = 512
PI = math.pi


class Tok:
    __slots__ = ("w", "r", "x")

    def __init__(self, x=False):
        self.w = None
        self.r = {}
        self.x = x


def toks(n):
    return [Tok() for _ in range(n)]


class Prog:
    ENGS = ("pe", "act", "dve", "pool", "sp")

    def __init__(self, nc, n_dma_sems=12):
        self.nc = nc
        self.ops = {e: [] for e in self.ENGS}
        self.cnt = {}
        self.sems = {}
        self.waited = {e: {} for e in self.ENGS}
        self.n_dma_sems = n_dma_sems
        self.dma_rr = {}
        self.ninst = 0

    def alloc_sems(self, stack):
        for e in ("pe", "act", "dve", "pool"):
            self.sems[e] = stack.enter_context(self.nc.semaphore("s_" + e))
            self.cnt[e] = 0
        for q in ("sp", "pool"):
            self.dma_rr[q] = 0
            for i in range(self.n_dma_sems):
                k = "d_%s_%d" % (q, i)
                self.sems[k] = stack.enter_context(self.nc.semaphore(k))
                self.cnt[k] = 0

    def _wait(self, E, key, val):
        if val <= 0 or self.waited[E].get(key, 0) >= val:
            return
        self.waited[E][key] = val
        sem = self.sems[key]
        self.ops[E].append(lambda eng, sem=sem, val=val: eng.wait_ge(sem, val))
        self.ninst += 1

    @staticmethod
    def _deps(reads, writes):
        need = {}
        for t in reads:
            if t.w is not None:
                k, v = t.w
                if need.get(k, 0) < v:
                    need[k] = v
        for t in writes:
            if t.w is not None:
                k, v = t.w
                if need.get(k, 0) < v:
                    need[k] = v
            for k, v in t.r.items():
                if need.get(k, 0) < v:
                    need[k] = v
        return need

    def op(self, E, meth, *args, R=(), W=(), **kw):
        if any(t.x for t in R):
            W = list(W) + [t for t in R if t.x]
            R = [t for t in R if not t.x]
        need = self._deps(R, W)
        for k, v in need.items():
            if k == E and E == "pe":
                continue
            self._wait(E, k, v)
        self.cnt[E] += 1
        idx = self.cnt[E]
        sem = self.sems[E]
        self.ops[E].append(lambda eng, meth=meth, args=args, kw=kw, sem=sem:
                           getattr(eng, meth)(*args, **kw).then_inc(sem, 1))
        self.ninst += 1
        for t in R:
            t.r[E] = idx
        for t in W:
            t.w = (E, idx)
            t.r = {}

    def dma(self, Q, out, in_, R=(), W=()):
        need = self._deps(R, W)
        for k, v in need.items():
            self._wait(Q, k, v)
        i = self.dma_rr[Q]
        self.dma_rr[Q] = (i + 1) % self.n_dma_sems
        key = "d_%s_%d" % (Q, i)
        self._wait(Q, key, self.cnt[key])
        self.cnt[key] += 16
        val = self.cnt[key]
        sem = self.sems[key]
        self.ops[Q].append(lambda eng, out=out, in_=in_, sem=sem: eng.dma_start(out=out, in_=in_).then_inc(sem, 16))
        self.ninst += 1
        for t in R:
            t.r[key] = val
        for t in W:
            t.w = (key, val)
            t.r = {}

    def barrier(self):
        for E in self.ENGS:
            for key, v in self.cnt.items():
                if key != E:
                    self._wait(E, key, v)

    def emit(self):
        nc = self.nc
        ops = self.ops
        self.ops = {e: [] for e in self.ENGS}
        with nc.Block() as block:
            @block.tensor
            def _(eng):
                for f in ops["pe"]:
                    f(eng)

            @block.scalar
            def _(eng):
                for f in ops["act"]:
                    f(eng)

            @block.vector
            def _(eng):
                for f in ops["dve"]:
                    f(eng)

            @block.gpsimd
            def _(eng):
                for f in ops["pool"]:
                    f(eng)

            @block.sync
            def _(eng):
                for f in ops["sp"]:
                    f(eng)


def build_program(n_layers=4):
    nc = bass.Bass("TRN2", target_bir_lowering=False)

    def din(name, shape):
        return nc.dram_tensor(name, list(shape), F32, kind="ExternalInput").ap()

    x_d = din("x", [S, D])
    c_d = din("c_fm", [128, 8])
    adaw_d = din("ada_w", [4, 1024, 6144])
    adab_d = din("ada_b_fm", [128, 192])
    normw_d = din("norm_w_fm", [128, 64])
    fnw_d = din("fnw_fm", [128, 8])
    w1_d = din("mlp_w1", [4, 1024, 4096])
    w2_d = din("mlp_w2", [4, 4096, 1024])
    hin_d = din("hyb_w_in_ext", [2, 1024, 2176])
    hout_d = din("hyb_w_out", [2, 1024, 1024])
    lamre_d = din("s5_lamre", [2, 128, 64])
    lamim_d = din("s5_lamim", [2, 128, 64])
    logdt_d = din("s5_logdt", [2, 128, 64])
    sb1_d = din("s5_b1", [2, 128, 1024])
    sb2_d = din("s5_b2", [2, 128, 1024])
    sc1_d = din("s5_c1", [2, 128, 1024])
    sc2_d = din("s5_c2", [2, 128, 1024])
    s5d_d = din("s5_d_fm", [2, 128, 4])
    gluw_d = din("s5_glu_w", [2, 512, 512])
    glub_d = din("s5_glu_b_fm", [2, 128, 4])
    qkn_d = din("qkn_fm", [2, 128, 4])
    rin_d = din("rec_w_in", [2, 1024, 2048])
    rout_d = din("rec_w_out", [2, 1024, 1024])
    convw_d = din("rec_conv_fm", [2, 128, 32])
    convb_d = din("rec_convb_fm", [2, 128, 8])
    rabd_d = din("rec_ra_bd", [2, 128, 2048])
    ixbd_d = din("rec_ix_bd", [2, 128, 2048])
    rab_d = din("rec_rab_fm", [2, 128, 16])
    ixb_d = din("rec_ixb_fm", [2, 128, 16])
    rlam_d = din("rec_lam_fm", [2, 128, 16])
    rope_c_d = din("rope_cos", [128, S])
    rope_s_d = din("rope_sin", [128, S])
    const_d = din("consts", [128, 400])
    out_d = nc.dram_tensor("out", [S, D], F32, kind="ExternalOutput").ap()

    h_d = nc.dram_tensor("h_scr", [128, 8, S], F32, kind="Internal").ap()
    mix_d = nc.dram_tensor("mix_scr", [128, 8, S], BF16, kind="Internal").ap()
    gate_d = nc.dram_tensor("gate_scr", [128, 8, S], BF16, kind="Internal").ap()

    with ExitStack() as g:
        P = Prog(nc)
        P.alloc_sems(g)

        def gsb(name, shape, dt=F32):
            return g.enter_context(nc.sbuf_tensor(name, list(shape), dt))

        PS = [g.enter_context(nc.psum_tensor("ps%d" % i, [128, 512], F32)) for i in range(8)]
        PSt = [Tok(x=True) for _ in range(8)]

        cst = gsb("cst", [128, 400])
        t_cst = Tok()
        P.dma("sp", cst[:], const_d, W=[t_cst])
        ident = cst[:, 0:128]
        sgn = cst[:, 384:385]
        nsgn = cst[:, 385:386]
        gmask = cst[:, 386:394]
        ones_f = gsb("ones_f", [128, 128])
        ones_b = gsb("ones_b", [128, 128], BF16)
        bones_b = gsb("bones_b", [128, 128], BF16)
        eps_c = gsb("eps_c", [128, 1])
        one_c = gsb("one_c", [128, 1])
        t_c2 = Tok()
        P.op("pool", "memset", ones_f[:], 1.0, W=[t_c2])
        P.op("pool", "memset", ones_b[:], 1.0, W=[t_c2])
        P.op("pool", "memset", eps_c[:], EPS, W=[t_c2])
        P.op("pool", "memset", one_c[:], 1.0, W=[t_c2])
        P.op("pool", "tensor_copy", out=bones_b[:], in_=cst[:, 128:256], R=[t_cst], W=[t_c2])
        CT = [t_cst, t_c2]

        modall = gsb("modall", [128, 192])
        normw = gsb("normw", [128, 64])
        fnw = gsb("fnw", [128, 8])
        A1 = gsb("A1", [128, 32])
        A2 = gsb("A2", [128, 32])
        t_mod = Tok()

        uid = [0]

        def sbuf(st, name, shape, dt=F32):
            uid[0] += 1
            return st.enter_context(nc.sbuf_tensor("%s_u%d" % (name, uid[0]), list(shape), dt))

        with ExitStack() as st:
            cf = sbuf(st, "cf", [128, 8])
            cb = sbuf(st, "cb", [128, 8], BF16)
            adab = sbuf(st, "adab", [128, 192])
            t_cf, t_cb, t_ab, t_nw = toks(4)
            P.dma("sp", cf[:], c_d, W=[t_cf])
            P.dma("sp", adab[:], adab_d, W=[t_ab])
            P.dma("sp", normw[:], normw_d, W=[t_nw])
            P.dma("sp", fnw[:], fnw_d, W=[t_nw])
            P.op("act", "activation", out=cb[:], in_=cf[:], func=AF.Silu, R=[t_cf], W=[t_cb])
            NB = 3
            wt = [sbuf(st, "adaw%d" % i, [128, 8, 512], BF16) for i in range(NB)]
            twt = toks(NB)
            n = 0
            for l in range(4):
                src_l = adaw_d[l].rearrange("(k p) n -> p k n", p=128)
                for blk in range(12):
                    i = n % NB
                    n += 1
                    P.dma("pool", wt[i][:], src_l[:, :, blk * 512:(blk + 1) * 512], W=[twt[i]])
                    for j in range(4):
                        col = l * 48 + blk * 4 + j
                        for k in range(8):
                            P.op("pe", "matmul", PS[0][:, col:col + 1], lhsT=wt[i][:, k, j * 128:(j + 1) * 128],
                                 rhs=cb[:, k:k + 1], start=(k == 0), stop=(k == 7), R=[twt[i], t_cb], W=[PSt[0]])
            P.op("dve", "tensor_tensor", out=modall[:], in0=PS[0][:, 0:192], in1=adab[:], op=ALU.add,
                 R=[PSt[0], t_ab], W=[t_mod])
            for l in range(4):
                for (A, sc0, nw0) in ((A1, 8, 0), (A2, 32, 8)):
                    P.op("dve", "tensor_scalar", out=A[:, l * 8:(l + 1) * 8], in0=modall[:, l * 48 + sc0:l * 48 + sc0 + 8],
                         scalar1=1.0, scalar2=None, op0=ALU.add, R=[t_mod], W=[t_mod])
                    P.op("dve", "tensor_tensor", out=A[:, l * 8:(l + 1) * 8], in0=A[:, l * 8:(l + 1) * 8],
                         in1=normw[:, l * 16 + nw0:l * 16 + nw0 + 8], op=ALU.mult, R=[t_mod, t_nw], W=[t_mod])
            P.barrier()
            P.emit()

        def modcol(l, which, k):
            c = l * 48 + which * 8 + k
            return modall[:, c:c + 1]

        def norm_tile(st_tiles, h, th, N, u, tu, Acol, Bcol, pss, tpss):
            sq, tsq, srt, tsrt, rstd, trstd, tmp, ttmp = st_tiles
            for k in range(8):
                P.op("act", "activation", out=sq[k % 2][:, :N], in_=h[:, k, :N], func=AF.Square, R=[th], W=[tsq[k % 2]])
                P.op("pe", "matmul", pss[:, :N], lhsT=ones_b[:], rhs=sq[k % 2][:, :N], start=(k == 0), stop=(k == 7),
                     R=[tsq[k % 2]] + CT, W=[tpss])
            P.op("act", "activation", out=srt[:, :N], in_=pss[:, :N], func=AF.Sqrt, scale=1.0 / D, bias=eps_c[:],
                 R=[tpss] + CT, W=[tsrt])
            P.op("dve", "reciprocal", out=rstd[:, :N], in_=srt[:, :N], R=[tsrt], W=[trstd])
            for k in range(8):
                P.op("dve", "tensor_tensor", out=tmp[k % 2][:, :N], in0=h[:, k, :N], in1=rstd[:, :N], op=ALU.mult,
                     R=[th, trstd], W=[ttmp[k % 2]])
                if Bcol is None:
                    P.op("act", "activation", out=u[:, k, :N], in_=tmp[k % 2][:, :N], func=AF.Identity, scale=Acol(k),
                         R=[ttmp[k % 2], t_mod], W=[tu])
                else:
                    P.op("act", "activation", out=u[:, k, :N], in_=tmp[k % 2][:, :N], func=AF.Identity, scale=Acol(k),
                         bias=Bcol(k), R=[ttmp[k % 2], t_mod], W=[tu])

        def norm_scratch(st, N):
            sq = [sbuf(st, "n_sq%d" % i, [128, N], BF16) for i in range(2)]
            srt = sbuf(st, "n_srt", [128, N])
            rstd = sbuf(st, "n_rstd", [128, N])
            tmp = [sbuf(st, "n_tmp%d" % i, [128, N]) for i in range(2)]
            return (sq, toks(2), srt, Tok(), rstd, Tok(), tmp, toks(2))

        with ExitStack() as st:
            xt = [sbuf(st, "xt%d" % i, [128, D]) for i in range(2)]
            txt = toks(2)
            stg = [sbuf(st, "stg%d" % i, [128, 8, TW]) for i in range(2)]
            tstg = toks(2)
            for tt in range(NT):
                sg, tsg = stg[tt % 2], tstg[tt % 2]
                for j in range(4):
                    i = tt * 4 + j
                    xx, txx = xt[i % 2], txt[i % 2]
                    P.dma("sp", xx[:], x_d[i * 128:(i + 1) * 128, :], W=[txx])
                    for b in range(2):
                        bank = 2 * (i % 2) + b
                        for kk in range(4):
                            k = 4 * b + kk
                            P.op("pe", "transpose", out=PS[bank][:, kk * 128:(kk + 1) * 128], in_=xx[:, k * 128:(k + 1) * 128],
                                 identity=ident, R=[txx] + CT, W=[PSt[bank]])
                        P.op("act" if b == 0 else "dve", *(("activation",) if b == 0 else ("tensor_copy",)),
                             out=sg[:, 4 * b:4 * b + 4, j * 128:(j + 1) * 128],
                             in_=PS[bank][:, :].rearrange("p (k t) -> p k t", t=128),
                             **({"func": AF.Copy} if b == 0 else {}), R=[PSt[bank]], W=[tsg])
                P.dma("sp", h_d[:, :, tt * TW:(tt + 1) * TW], sg[:], R=[tsg], W=[])
            P.barrier()
            P.emit()

        def phase_tail(l, wout_src):
            TT = 256
            with ExitStack() as st:
                wo = sbuf(st, "wo", [128, 8, 1024], BF16)
                w1 = sbuf(st, "w1", [128, 8, 4096], BF16)
                w2 = sbuf(st, "w2", [128, 32, 1024], BF16)
                t_wo, t_w1, t_w2 = toks(3)
                P.dma("pool", wo[:], wout_src.rearrange("(k p) n -> p k n", p=128), W=[t_wo])
                w1s = w1_d[l].rearrange("(k p) n -> p k n", p=128)
                for q4 in range(4):
                    P.dma("pool", w1[:, :, q4 * 1024:(q4 + 1) * 1024], w1s[:, :, q4 * 1024:(q4 + 1) * 1024], W=[t_w1])
                w2s = w2_d[l].rearrange("(k p) n -> p k n", p=128)
                for q4 in range(4):
                    P.dma("pool", w2[:, q4 * 8:(q4 + 1) * 8, :], w2s[:, q4 * 8:(q4 + 1) * 8, :], W=[t_w2])
                mx = [sbuf(st, "mx%d" % i, [128, 8, TT], BF16) for i in range(2)]
                tmx = toks(2)
                hh = [sbuf(st, "hh%d" % i, [128, 8, TT]) for i in range(2)]
                thh = toks(2)
                u2 = sbuf(st, "u2", [128, 8, TT], BF16)
                tu2 = Tok()
                ff = sbuf(st, "ff", [128, 32, TT], BF16)
                tff = toks(32)
                rl = [sbuf(st, "rl%d" % i, [128, TT], BF16) for i in range(3)]
                trl = toks(3)
                nsc = norm_scratch(st, TT)
                for it in range(S // TT):
                    t0 = it * TT
                    m, tm = mx[it % 2], tmx[it % 2]
                    h, th = hh[it % 2], thh[it % 2]
                    P.dma("sp", m[:], mix_d[:, :, t0:t0 + TT], W=[tm])
                    P.dma("sp", h[:], h_d[:, :, t0:t0 + TT], W=[th])
                    for mo in range(8):
                        b = 1 + (mo % 2)
                        for k in range(8):
                            P.op("pe", "matmul", PS[b][:, :TT], lhsT=wo[:, k, mo * 128:(mo + 1) * 128], rhs=m[:, k, :],
                                 start=(k == 0), stop=(k == 7), R=[t_wo, tm], W=[PSt[b]])
                        P.op("dve", "scalar_tensor_tensor", out=h[:, mo, :], in0=PS[b][:, :TT], scalar=modcol(l, 2, mo),
                             in1=h[:, mo, :], op0=ALU.mult, op1=ALU.add, R=[PSt[b], t_mod, th], W=[th])
                    norm_tile(nsc, h, th, TT, u2, tu2, lambda k: A2[:, l * 8 + k:l * 8 + k + 1], lambda k: modcol(l, 3, k),
                              PS[0], PSt[0])
                    for f in range(32):
                        b = 3 + (f % 3)
                        for k in range(8):
                            P.op("pe", "matmul", PS[b][:, :TT], lhsT=w1[:, k, f * 128:(f + 1) * 128], rhs=u2[:, k, :],
                                 start=(k == 0), stop=(k == 7), R=[t_w1, tu2], W=[PSt[b]])
                        r, tr = rl[f % 3], trl[f % 3]
                        P.op("act", "activation", out=r[:], in_=PS[b][:, :TT], func=AF.Relu, R=[PSt[b]], W=[tr])
                        P.op("pool", "tensor_tensor", out=ff[:, f, :], in0=r[:], in1=r[:], op=ALU.mult, R=[tr], W=[tff[f]])
                    for mo in range(8):
                        b = 6 + (mo % 2)
                        for f in range(32):
                            P.op("pe", "matmul", PS[b][:, :TT], lhsT=w2[:, f, mo * 128:(mo + 1) * 128], rhs=ff[:, f, :],
                                 start=(f == 0), stop=(f == 31), R=[t_w2, tff[f]], W=[PSt[b]])
                        P.op("dve", "scalar_tensor_tensor", out=h[:, mo, :], in0=PS[b][:, :TT], scalar=modcol(l, 5, mo),
                             in1=h[:, mo, :], op0=ALU.mult, op1=ALU.add, R=[PSt[b], t_mod, th], W=[th])
                    P.dma("sp", h_d[:, :, t0:t0 + TT], h[:], R=[th], W=[])
                P.barrier()
                P.emit()

        def phase_even(l):
            e = l // 2
            with ExitStack() as lst:
                zs5 = [sbuf(lst, "zs5_%d" % i, [128, S], BF16) for i in range(4)]
                tzs5 = [toks(NT) for _ in range(4)]
                with ExitStack() as ast:
                    qb = [sbuf(ast, "q%d" % i, [128, S], BF16) for i in range(4)]
                    tq = [toks(NT) for _ in range(4)]
                    kd = [sbuf(ast, "kd%d" % i, [128, S], BF16) for i in range(2)]
                    tkd = [Tok() for _ in range(2)]
                    Vx = sbuf(ast, "Vx", [128, 32, 2, 192], BF16)
                    tVx = Tok()
                    with ExitStack() as st:
                        win = sbuf(st, "win", [128, 8, 2176], BF16)
                        t_win = Tok()
                        wsrc = hin_d[e].rearrange("(k p) n -> p k n", p=128)
                        for q4 in range(4):
                            P.dma("pool", win[:, :, q4 * 544:(q4 + 1) * 544], wsrc[:, :, q4 * 544:(q4 + 1) * 544], W=[t_win])
                        qkn = sbuf(st, "qkn", [128, 4])
                        t_qkn = Tok()
                        P.dma("sp", qkn[:], qkn_d[e], W=[t_qkn])
                        P.op("pool", "memset", Vx[:], 0.0, W=[tVx])
                        P.op("pool", "memset", Vx[:, :, :, 64:65], 1.0, W=[tVx])
                        hh = sbuf(st, "hh", [128, 8, TW])
                        thh = Tok()
                        u = sbuf(st, "u", [128, 8, TW], BF16)
                        tu = Tok()
                        rc = [sbuf(st, "rc%d" % i, [128, TW]) for i in range(2)]
                        rs = [sbuf(st, "rs%d" % i, [128, TW]) for i in range(2)]
                        trc = toks(2)
                        sqh = sbuf(st, "sqh", [128, TW], BF16)
                        srt = sbuf(st, "srt", [128, TW])
                        rstd = sbuf(st, "rstd", [128, TW])
                        t1 = sbuf(st, "t1", [128, TW])
                        t2 = sbuf(st, "t2", [128, TW])
                        tsqh, tsrt, trstd, tt1, tt2 = toks(5)
                        nsc = norm_scratch(st, TW)
                        for tt in range(NT):
                            ts_ = slice(tt * TW, (tt + 1) * TW)
                            P.dma("sp", hh[:], h_d[:, :, ts_], W=[thh])
                            P.dma("sp", rc[tt % 2][:], rope_c_d[:, ts_], W=[trc[tt % 2]])
                            P.dma("sp", rs[tt % 2][:], rope_s_d[:, ts_], W=[trc[tt % 2]])
                            norm_tile(nsc, hh, thh, TW, u, tu, lambda k: A1[:, l * 8 + k:l * 8 + k + 1],
                                      lambda k: modcol(l, 0, k), PS[0], PSt[0])

                            def proj(bank, c0):
                                for k in range(8):
                                    P.op("pe", "matmul", PS[bank][:, :], lhsT=win[:, k, c0:c0 + 128], rhs=u[:, k, :],
                                         start=(k == 0), stop=(k == 7), R=[t_win, tu], W=[PSt[bank]])
                            for cs in range(4):
                                b = 1 + (cs % 2)
                                proj(b, cs * 128)
                                P.op("act", "activation", out=zs5[cs][:, ts_], in_=PS[b][:, :], func=AF.Copy,
                                     R=[PSt[b]], W=[tzs5[cs][tt]])
                            items = [(512 + 128 * j, 1024 + 128 * j, qb[j], tq[j][tt], 0, 1) for j in range(4)]
                            items += [(1536 + 128 * j, 1792 + 128 * j, kd[j], tkd[j], 2, 3) for j in range(2)]
                            for n_, (cz, cp, dst, tdst, wc, wpc) in enumerate(items):
                                bz = 3 + (n_ % 2)
                                bp = 5 + (n_ % 2)
                                proj(bz, cz)
                                proj(bp, cp)
                                P.op("act", "activation", out=sqh[:], in_=PS[bz][:, :], func=AF.Square, R=[PSt[bz]], W=[tsqh])
                                P.op("pe", "matmul", PS[7][:, :], lhsT=bones_b[:], rhs=sqh[:], start=True, stop=True,
                                     R=[tsqh] + CT, W=[PSt[7]])
                                P.op("act", "activation", out=srt[:], in_=PS[7][:, :], func=AF.Sqrt, scale=1.0 / 64, bias=eps_c[:],
                                     R=[PSt[7]] + CT, W=[tsrt])
                                P.op("dve", "reciprocal", out=rstd[:], in_=srt[:], R=[tsrt], W=[trstd])
                                P.op("dve", "scalar_tensor_tensor", out=t1[:], in0=PS[bz][:, :], scalar=qkn[:, wc:wc + 1],
                                     in1=rc[tt % 2][:], op0=ALU.mult, op1=ALU.mult, R=[PSt[bz], t_qkn, trc[tt % 2]], W=[tt1])
                                P.op("dve", "scalar_tensor_tensor", out=t2[:], in0=PS[bp][:, :], scalar=qkn[:, wpc:wpc + 1],
                                     in1=rs[tt % 2][:], op0=ALU.mult, op1=ALU.mult, R=[PSt[bp], t_qkn, trc[tt % 2]], W=[tt2])
                                P.op("pool", "tensor_tensor", out=t1[:], in0=t1[:], in1=t2[:], op=ALU.add, R=[tt1, tt2], W=[tt1])
                                P.op("dve", "tensor_tensor", out=dst[:, ts_], in0=t1[:], in1=rstd[:], op=ALU.mult,
                                     R=[tt1, trstd], W=[tdst])
                            b = 1 + (tt % 2)
                            for j in range(4):
                                for k in range(8):
                                    P.op("pe", "matmul", PS[b][:, j * 128:(j + 1) * 128], lhsT=u[:, k, j * 128:(j + 1) * 128],
                                         rhs=win[:, k, 2048:2176], start=(k == 0), stop=(k == 7), R=[t_win, tu], W=[PSt[b]])
                            src = PS[b][:, :].rearrange("p (j v d) -> p j v d", j=4, v=2)
                            P.op("act", "activation", out=Vx[:, tt * 4:(tt + 1) * 4, :, 0:64], in_=src, func=AF.Copy,
                                 R=[PSt[b]], W=[tVx])
                            P.op("dve", "tensor_copy", out=Vx[:, tt * 4:(tt + 1) * 4, :, 128:192], in_=src, R=[PSt[b]], W=[tVx])
                        P.barrier()
                        P.emit()
                    with ExitStack() as st:
                        pt = [sbuf(st, "pt%d" % i, [128, TW], BF16) for i in range(3)]
                        tpt = toks(3)
                        rec = sbuf(st, "rec", [128, TW])
                        bcs = sbuf(st, "bcs", [128, TW])
                        trec, tbcs = toks(2)
                        oat = [sbuf(st, "oat%d" % i, [128, TW], BF16) for i in range(2)]
                        toat = toks(2)
                        un = 0
                        for j in range(4):
                            kv = j // 2
                            for qt in range(NT):
                                qs = slice(qt * TW, (qt + 1) * TW)
                                o, to = oat[(j * NT + qt) % 2], toat[(j * NT + qt) % 2]
                                for hhf in range(2):
                                    p0 = 64 * hhf
                                    bo = 3 + (un % 2)
                                    un += 1
                                    for kt in range(32):
                                        bs = kt % 3
                                        P.op("pe", "matmul", PS[bs][:, :], lhsT=kd[kv][p0:p0 + 64, kt * 128:(kt + 1) * 128],
                                             rhs=qb[j][p0:p0 + 64, qs], start=True, stop=True, R=[tkd[kv], tq[j][qt]], W=[PSt[bs]])
                                        P.op("act", "activation", out=pt[bs][:], in_=PS[bs][:, :], func=AF.Exp, scale=0.125,
                                             R=[PSt[bs]], W=[tpt[bs]])
                                        if hhf == 0:
                                            P.op("pe", "matmul", PS[bo][0:65, :], lhsT=Vx[:, kt, kv, 0:65], rhs=pt[bs][:],
                                                 start=(kt == 0), stop=(kt == 31), R=[tVx, tpt[bs]], W=[PSt[bo]])
                                        else:
                                            P.op("pe", "matmul", PS[bo][:, :], lhsT=Vx[:, kt, kv, 64:192], rhs=pt[bs][:],
                                                 start=(kt == 0), stop=(kt == 31), R=[tVx, tpt[bs]], W=[PSt[bo]])
                                    if hhf == 0:
                                        P.op("dve", "reciprocal", out=rec[64:65, :], in_=PS[bo][64:65, :], R=[PSt[bo]], W=[trec])
                                        P.op("pe", "matmul", PS[5][0:64, :], lhsT=ones_f[64:65, 0:64], rhs=rec[64:65, :],
                                             start=True, stop=True, R=[trec] + CT, W=[PSt[5]])
                                        P.op("act", "activation", out=bcs[0:64, :], in_=PS[5][0:64, :], func=AF.Copy,
                                             R=[PSt[5]], W=[tbcs])
                                        P.op("dve", "tensor_tensor", out=o[0:64, :], in0=PS[bo][0:64, :], in1=bcs[0:64, :],
                                             op=ALU.mult, R=[PSt[bo], tbcs], W=[to])
                                    else:
                                        P.op("dve", "reciprocal", out=rec[0:1, :], in_=PS[bo][0:1, :], R=[PSt[bo]], W=[trec])
                                        P.op("pe", "matmul", PS[5][:, :], lhsT=ones_f[0:1, :], rhs=rec[0:1, :],
                                             start=True, stop=True, R=[trec] + CT, W=[PSt[5]])
                                        P.op("act", "activation", out=bcs[64:128, :], in_=PS[5][64:128, :], func=AF.Copy,
                                             R=[PSt[5]], W=[tbcs])
                                        P.op("dve", "tensor_tensor", out=o[64:128, :], in0=PS[bo][64:128, :], in1=bcs[64:128, :],
                                             op=ALU.mult, R=[PSt[bo], tbcs], W=[to])
                                P.dma("sp", mix_d[:, 4 + j, qs], o[:], R=[to], W=[])
                        P.barrier()
                        P.emit()
                with ExitStack() as st:
                    NG = 64
                    prm = {}
                    tprm = Tok()

                    def ptile(name, w=NG):
                        prm[name] = sbuf(st, "s5p_" + name, [128, w])
                        return prm[name]
                    lre, lim, ldt = ptile("lre"), ptile("lim"), ptile("ldt")
                    P.dma("sp", lre[:], lamre_d[e], W=[tprm])
                    P.dma("sp", lim[:], lamim_d[e], W=[tprm])
                    P.dma("sp", ldt[:], logdt_d[e], W=[tprm])

                    def dv(meth, **kw):
                        P.op("dve", meth, R=[tprm] + CT, W=[tprm], **kw)

                    def ac(**kw):
                        P.op("act", "activation", R=[tprm] + CT, W=[tprm], **kw)
                    lr, dt, x1, mag, th = ptile("lr"), ptile("dt"), ptile("x1"), ptile("mag"), ptile("th")
                    dv("tensor_scalar", out=lr[:], in0=lre[:], scalar1=-1e-4, scalar2=None, op0=ALU.min)
                    ac(out=dt[:], in_=ldt[:], func=AF.Exp)
                    dv("tensor_tensor", out=x1[:], in0=lr[:], in1=dt[:], op=ALU.mult)
                    ac(out=mag[:], in_=x1[:], func=AF.Exp)
                    dv("tensor_tensor", out=th[:], in0=lim[:], in1=dt[:], op=ALU.mult)
                    ti = sbuf(st, "s5p_ti", [128, NG], I32)
                    tf, red = ptile("tf"), ptile("red")

                    def sin_of(dst, src, shift):
                        dv("tensor_scalar", out=tf[:], in0=src[:], scalar1=shift, scalar2=1.0 / (2 * PI), op0=ALU.add, op1=ALU.mult)
                        dv("tensor_copy", out=ti[:], in_=tf[:])
                        dv("tensor_copy", out=tf[:], in_=ti[:])
                        dv("tensor_scalar", out=red[:], in0=src[:], scalar1=shift, scalar2=None, op0=ALU.add)
                        dv("scalar_tensor_tensor", out=red[:], in0=tf[:], scalar=-2 * PI, in1=red[:], op0=ALU.mult, op1=ALU.add)
                        dv("tensor_scalar", out=red[:], in0=red[:], scalar1=PI, scalar2=-PI, op0=ALU.min, op1=ALU.max)
                        ac(out=dst[:], in_=red[:], func=AF.Sin)
                    sn, cs_ = ptile("sn"), ptile("cs")
                    sin_of(sn, th, 0.0)
                    sin_of(cs_, th, PI / 2)
                    are, aim = ptile("are"), ptile("aim")
                    dv("tensor_tensor", out=are[:], in0=mag[:], in1=cs_[:], op=ALU.mult)
                    dv("tensor_tensor", out=aim[:], in0=mag[:], in1=sn[:], op=ALU.mult)
                    den, t_a, t_b = ptile("den"), ptile("ta"), ptile("tb")
                    dv("tensor_tensor", out=den[:], in0=lr[:], in1=lr[:], op=ALU.mult)
                    dv("tensor_tensor", out=t_a[:], in0=lim[:], in1=lim[:], op=ALU.mult)
                    dv("tensor_tensor", out=den[:], in0=den[:], in1=t_a[:], op=ALU.add)
                    dv("reciprocal", out=den[:], in_=den[:])
                    nr = ptile("nr")
                    dv("tensor_scalar", out=nr[:], in0=are[:], scalar1=-1.0, scalar2=None, op0=ALU.add)
                    fre, fim = ptile("fre"), ptile("fim")
                    dv("tensor_tensor", out=t_a[:], in0=nr[:], in1=lr[:], op=ALU.mult)
                    dv("tensor_tensor", out=t_b[:], in0=aim[:], in1=lim[:], op=ALU.mult)
                    dv("tensor_tensor", out=t_a[:], in0=t_a[:], in1=t_b[:], op=ALU.add)
                    dv("tensor_tensor", out=fre[:], in0=t_a[:], in1=den[:], op=ALU.mult)
                    dv("tensor_tensor", out=t_a[:], in0=aim[:], in1=lr[:], op=ALU.mult)
                    dv("tensor_tensor", out=t_b[:], in0=nr[:], in1=lim[:], op=ALU.mult)
                    dv("tensor_tensor", out=t_a[:], in0=t_a[:], in1=t_b[:], op=ALU.subtract)
                    dv("tensor_tensor", out=fim[:], in0=t_a[:], in1=den[:], op=ALU.mult)
                    F2, F3 = ptile("F2"), ptile("F3")
                    dv("tensor_scalar", out=F2[:], in0=fim[:], scalar1=sgn, scalar2=None, op0=ALU.mult)
                    dv("tensor_scalar", out=F3[:], in0=fre[:], scalar1=nsgn, scalar2=None, op0=ALU.mult)
                    CK = sbuf(st, "s5p_CK", [128, 12, NG])
                    SK = sbuf(st, "s5p_SK", [128, 12, NG])
                    dv("tensor_copy", out=CK[:, 0, :], in_=cs_[:])
                    dv("tensor_copy", out=SK[:, 0, :], in_=sn[:])
                    for k in range(11):
                        dv("tensor_tensor", out=t_a[:], in0=CK[:, k, :], in1=CK[:, k, :], op=ALU.mult)
                        dv("tensor_tensor", out=t_b[:], in0=SK[:, k, :], in1=SK[:, k, :], op=ALU.mult)
                        dv("tensor_tensor", out=CK[:, k + 1, :], in0=t_a[:], in1=t_b[:], op=ALU.subtract)
                        dv("tensor_tensor", out=t_a[:], in0=CK[:, k, :], in1=SK[:, k, :], op=ALU.mult)
                        dv("tensor_scalar", out=SK[:, k + 1, :], in0=t_a[:], scalar1=2.0, scalar2=None, op0=ALU.mult)
                    bbs = sbuf(st, "s5_bbs", [128, NG, 16])
                    bbw = sbuf(st, "s5_bbw", [128, NG, 16])

                    def bc16(a):
                        return a[:].unsqueeze(2).to_broadcast([128, NG, 16])
                    with ExitStack() as sub:
                        b1 = sbuf(sub, "s5_b1", [128, NG, 16])
                        b2 = sbuf(sub, "s5_b2", [128, NG, 16])
                        tmpb = sbuf(sub, "s5_tmpb", [128, NG, 16])
                        P.dma("sp", b1[:], sb1_d[e].rearrange("p (g c) -> p g c", c=16), W=[tprm])
                        P.dma("sp", b2[:], sb2_d[e].rearrange("p (g c) -> p g c", c=16), W=[tprm])
                        dv("tensor_tensor", out=bbs[:], in0=b1[:], in1=bc16(fre), op=ALU.mult)
                        dv("tensor_tensor", out=tmpb[:], in0=b2[:], in1=bc16(F2), op=ALU.mult)
                        dv("tensor_tensor", out=bbs[:], in0=bbs[:], in1=tmpb[:], op=ALU.add)
                        dv("tensor_tensor", out=bbw[:], in0=b2[:], in1=bc16(F3), op=ALU.mult)
                        dv("tensor_tensor", out=tmpb[:], in0=b1[:], in1=bc16(fim), op=ALU.mult)
                        dv("tensor_tensor", out=bbw[:], in0=bbw[:], in1=tmpb[:], op=ALU.add)
                        P.barrier()
                        P.emit()
                    c1 = sbuf(st, "s5_c1", [128, NG, 16])
                    c2 = sbuf(st, "s5_c2", [128, NG, 16])
                    P.dma("sp", c1[:], sc1_d[e].rearrange("p (g c) -> p g c", c=16), W=[tprm])
                    P.dma("sp", c2[:], sc2_d[e].rearrange("p (g c) -> p g c", c=16), W=[tprm])
                    s5d = sbuf(st, "s5_d", [128, 4])
                    glub = sbuf(st, "s5_glub", [128, 4])
                    P.dma("sp", s5d[:], s5d_d[e], W=[tprm])
                    P.dma("sp", glub[:], glub_d[e], W=[tprm])
                    gluw = sbuf(st, "s5_gluw", [128, 4, 512], BF16)
                    tgw = Tok()
                    P.dma("pool", gluw[:], gluw_d[e].rearrange("(k p) n -> p k n", p=128), W=[tgw])

                    bT = sbuf(st, "s5_bT", [128, 128])
                    bTw = sbuf(st, "s5_bTw", [128, 128])
                    tbT = Tok()
                    lb = sbuf(st, "s5_lb", [128, 8, 128], BF16)
                    lbw = sbuf(st, "s5_lbw", [128, 8, 128], BF16)
                    tlb = Tok()
                    w1p = sbuf(st, "s5_w1p", [128, 1152], BF16)
                    w2p = sbuf(st, "s5_w2p", [128, 1152], BF16)
                    twp = Tok()
                    P.op("pool", "memset", w1p[:], 0.0, W=[twp])
                    P.op("pool", "memset", w2p[:], 0.0, W=[twp])
                    COS = sbuf(st, "s5_COS", [128, S])
                    SIN = sbuf(st, "s5_SIN", [128, S])
                    tA = sbuf(st, "s5_tA", [128, S // 2])
                    tB = sbuf(st, "s5_tB", [128, S // 2])
                    tCOS, tSIN, ttA, ttB = toks(4)
                    yacc = sbuf(st, "s5_yacc", [128, S])
                    tya = toks(NT)
                    gl = [sbuf(st, "s5_gl%d" % i, [128, S], BF16) for i in range(4)]
                    tgl = [toks(NT) for _ in range(4)]
                    NB = 3
                    vt = [sbuf(st, "s5_v%d" % i, [128, TW]) for i in range(NB)]
                    tm_ = [sbuf(st, "s5_tm%d" % i, [128, TW]) for i in range(NB)]
                    gt = [sbuf(st, "s5_g%d" % i, [128, TW]) for i in range(NB)]
                    a1 = [sbuf(st, "s5_a1%d" % i, [128, TW], BF16) for i in range(NB)]
                    a2 = [sbuf(st, "s5_a2%d" % i, [128, TW], BF16) for i in range(NB)]
                    tvt, ttm, tgt, ta1, ta2 = toks(NB), toks(NB), toks(NB), toks(NB), toks(NB)
                    it = 0
                    for cs in range(4):
                        for tt in range(NT):
                            ts_ = slice(tt * TW, (tt + 1) * TW)
                            P.op("act", "activation", out=yacc[:, ts_], in_=zs5[cs][:, ts_], func=AF.Identity, scale=s5d[:, cs:cs + 1],
                                 R=[tzs5[cs][tt], tprm], W=[tya[tt]])
                        for dr in range(2):
                            g0 = dr * 32 + cs * 8
                            P.op("pe", "transpose", out=PS[6][:, 0:128], in_=bbs[:, g0:g0 + 8, :].rearrange("p g c -> p (g c)"),
                                 identity=ident, R=[tprm] + CT, W=[PSt[6]])
                            P.op("pe", "transpose", out=PS[6][:, 128:256], in_=bbw[:, g0:g0 + 8, :].rearrange("p g c -> p (g c)"),
                                 identity=ident, R=[tprm] + CT, W=[PSt[6]])
                            P.op("act", "activation", out=bT[:], in_=PS[6][:, 0:128], func=AF.Copy, R=[PSt[6]], W=[tbT])
                            P.op("act", "activation", out=bTw[:], in_=PS[6][:, 128:256], func=AF.Copy, R=[PSt[6]], W=[tbT])
                            for gq in range(8):
                                P.op("dve", "tensor_scalar", out=lb[:, gq, :], in0=bT[:], scalar1=gmask[:, gq:gq + 1], scalar2=None,
                                     op0=ALU.mult, R=[tbT] + CT, W=[tlb])
                                P.op("dve", "tensor_scalar", out=lbw[:, gq, :], in0=bTw[:], scalar1=gmask[:, gq:gq + 1], scalar2=None,
                                     op0=ALU.mult, R=[tbT] + CT, W=[tlb])
                            P.op("dve", "tensor_scalar", out=w1p[:].rearrange("p (g w) -> p g w", w=144)[:, :, 0:16],
                                 in0=c1[:, g0:g0 + 8, :], scalar1=nsgn, scalar2=None, op0=ALU.mult, R=[tprm] + CT, W=[twp])
                            P.op("dve", "tensor_scalar", out=w2p[:].rearrange("p (g w) -> p g w", w=144)[:, :, 0:16],
                                 in0=c2[:, g0:g0 + 8, :], scalar1=-1.0, scalar2=None, op0=ALU.mult, R=[tprm] + CT, W=[twp])
                            for gq in range(8):
                                gd = g0 + gq
                                P.op("pool", "memset", COS[:, 0:1], 1.0, W=[tCOS])
                                P.op("pool", "memset", SIN[:, 0:1], 0.0, W=[tSIN])
                                for k in range(12):
                                    ln = 1 << k
                                    ck = CK[:, k, gd:gd + 1]
                                    sk = SK[:, k, gd:gd + 1]
                                    P.op("act", "activation", out=COS[:, ln:2 * ln], in_=COS[:, 0:ln], func=AF.Identity, scale=ck,
                                         R=[tCOS, tprm], W=[tCOS])
                                    P.op("act", "activation", out=tA[:, 0:ln], in_=SIN[:, 0:ln], func=AF.Identity, scale=sk,
                                         R=[tSIN, tprm], W=[ttA])
                                    P.op("pool", "tensor_tensor", out=COS[:, ln:2 * ln], in0=COS[:, ln:2 * ln], in1=tA[:, 0:ln],
                                         op=ALU.subtract, R=[tCOS, ttA], W=[tCOS])
                                    P.op("act", "activation", out=SIN[:, ln:2 * ln], in_=SIN[:, 0:ln], func=AF.Identity, scale=ck,
                                         R=[tSIN, tprm], W=[tSIN])
                                    P.op("act", "activation", out=tB[:, 0:ln], in_=COS[:, 0:ln], func=AF.Identity, scale=sk,
                                         R=[tCOS, tprm], W=[ttB])
                                    P.op("pool", "tensor_tensor", out=SIN[:, ln:2 * ln], in0=SIN[:, ln:2 * ln], in1=tB[:, 0:ln],
                                         op=ALU.add, R=[tSIN, ttB], W=[tSIN])
                                prev = None
                                order = range(NT) if dr == 0 else range(NT - 1, -1, -1)
                                for tt in order:
                                    ts_ = slice(tt * TW, (tt + 1) * TW)
                                    i = it % NB
                                    it += 1
                                    b1_, b2_ = (0, 1) if (it % 2) else (2, 3)
                                    by = 4 + (it % 2)
                                    if dr == 0:
                                        Cs, Ss = COS[:, ts_], SIN[:, ts_]
                                    else:
                                        lo = S - (tt + 1) * TW
                                        Cs, Ss = COS[:, ::-1][:, tt * TW:(tt + 1) * TW], SIN[:, ::-1][:, tt * TW:(tt + 1) * TW]
                                    P.op("pe", "matmul", PS[b1_][:, :], lhsT=lb[:, gq, :], rhs=zs5[cs][:, ts_], start=True, stop=True,
                                         R=[tlb, tzs5[cs][tt]], W=[PSt[b1_]])
                                    P.op("pe", "matmul", PS[b2_][:, :], lhsT=lbw[:, gq, :], rhs=zs5[cs][:, ts_], start=True, stop=True,
                                         R=[tlb, tzs5[cs][tt]], W=[PSt[b2_]])
                                    P.op("dve", "tensor_tensor", out=tm_[i][:], in0=PS[b1_][:, :], in1=Cs, op=ALU.mult,
                                         R=[PSt[b1_], tCOS], W=[ttm[i]])
                                    P.op("dve", "tensor_tensor", out=vt[i][:], in0=PS[b2_][:, :], in1=Ss, op=ALU.mult,
                                         R=[PSt[b2_], tSIN], W=[tvt[i]])
                                    P.op("pool", "tensor_tensor", out=vt[i][:], in0=vt[i][:], in1=tm_[i][:], op=ALU.add,
                                         R=[ttm[i], tvt[i]], W=[tvt[i]])
                                    mcol = mag[:, gd:gd + 1].to_broadcast([128, TW])
                                    if dr == 0:
                                        init = 0.0 if prev is None else gt[prev][:, TW - 1:TW]
                                        P.op("dve", "tensor_tensor_scan", out=gt[i][:], data0=mcol, data1=vt[i][:], initial=init,
                                             op0=ALU.mult, op1=ALU.add, R=[tvt[i], tprm] + ([tgt[prev]] if prev is not None else []),
                                             W=[tgt[i]])
                                    else:
                                        init = 0.0 if prev is None else gt[prev][:, 0:1]
                                        P.op("dve", "tensor_tensor_scan", out=gt[i][:, ::-1], data0=mcol, data1=vt[i][:, ::-1],
                                             initial=init, op0=ALU.mult, op1=ALU.add,
                                             R=[tvt[i], tprm] + ([tgt[prev]] if prev is not None else []), W=[tgt[i]])
                                    prev = i
                                    P.op("dve", "tensor_tensor", out=a1[i][:], in0=gt[i][:], in1=Cs, op=ALU.mult,
                                         R=[tgt[i], tCOS], W=[ta1[i]])
                                    P.op("pool", "tensor_tensor", out=a2[i][:], in0=gt[i][:], in1=Ss, op=ALU.mult,
                                         R=[tgt[i], tSIN], W=[ta2[i]])
                                    P.op("pe", "matmul", PS[by][:, :], lhsT=w1p[:, gq * 128:(gq + 1) * 128], rhs=a1[i][:],
                                         start=True, stop=False, R=[twp, ta1[i]], W=[PSt[by]])
                                    P.op("pe", "matmul", PS[by][:, :], lhsT=w2p[:, gq * 128:(gq + 1) * 128], rhs=a2[i][:],
                                         start=False, stop=True, R=[twp, ta2[i]], W=[PSt[by]])
                                    P.op("dve", "tensor_tensor", out=yacc[:, ts_], in0=PS[by][:, :], in1=yacc[:, ts_], op=ALU.add,
                                         R=[PSt[by], tya[tt]], W=[tya[tt]])
                        for tt in range(NT):
                            ts_ = slice(tt * TW, (tt + 1) * TW)
                            P.op("act", "activation", out=gl[cs][:, ts_], in_=yacc[:, ts_], func=AF.Gelu_apprx_tanh,
                                 R=[tya[tt]], W=[tgl[cs][tt]])
                    sg = [sbuf(st, "s5_sg%d" % i, [128, TW]) for i in range(2)]
                    ys = [sbuf(st, "s5_ys%d" % i, [128, TW], BF16) for i in range(2)]
                    tsg, tys = toks(2), toks(2)
                    n_ = 0
                    for tt in range(NT):
                        ts_ = slice(tt * TW, (tt + 1) * TW)
                        for mo in range(4):
                            b = 6 + (n_ % 2)
                            i = n_ % 2
                            n_ += 1
                            for k in range(4):
                                P.op("pe", "matmul", PS[b][:, :], lhsT=gluw[:, k, mo * 128:(mo + 1) * 128], rhs=gl[k][:, ts_],
                                     start=(k == 0), stop=(k == 3), R=[tgw, tgl[k][tt]], W=[PSt[b]])
                            P.op("act", "activation", out=sg[i][:], in_=PS[b][:, :], func=AF.Sigmoid, bias=glub[:, mo:mo + 1],
                                 R=[PSt[b], tprm], W=[tsg[i]])
                            P.op("dve", "tensor_tensor", out=ys[i][:], in0=sg[i][:], in1=gl[mo][:, ts_], op=ALU.mult,
                                 R=[tsg[i], tgl[mo][tt]], W=[tys[i]])
                            P.dma("sp", mix_d[:, mo, ts_], ys[i][:], R=[tys[i]], W=[])
                    P.barrier()
                    P.emit()
            phase_tail(l, hout_d[e])

        def phase_odd(l):
            o = l // 2
            with ExitStack() as lst:
                XW = S + 4
                xr = [sbuf(lst, "xr%d" % i, [128, XW], BF16) for i in range(8)]
                txr = [Tok() for _ in range(8)]
                with ExitStack() as st:
                    win = sbuf(st, "rwin", [128, 8, 2048], BF16)
                    t_win = Tok()
                    wsrc = rin_d[o].rearrange("(k p) n -> p k n", p=128)
                    for q4 in range(4):
                        P.dma("pool", win[:, :, q4 * 512:(q4 + 1) * 512], wsrc[:, :, q4 * 512:(q4 + 1) * 512], W=[t_win])
                    for c in range(8):
                        P.op("pool", "memset", xr[c][:, 0:2], 0.0, W=[txr[c]])
                        P.op("pool", "memset", xr[c][:, S + 2:S + 4], 0.0, W=[txr[c]])
                    hh = sbuf(st, "hh", [128, 8, TW])
                    thh = Tok()
                    u = sbuf(st, "u", [128, 8, TW], BF16)
                    tu = Tok()
                    gs = [sbuf(st, "gs%d" % i, [128, TW], BF16) for i in range(3)]
                    tgs = toks(3)
                    nsc = norm_scratch(st, TW)
                    n_ = 0
                    for tt in range(NT):
                        ts_ = slice(tt * TW, (tt + 1) * TW)
                        P.dma("sp", hh[:], h_d[:, :, ts_], W=[thh])
                        norm_tile(nsc, hh, thh, TW, u, tu, lambda k: A1[:, l * 8 + k:l * 8 + k + 1],
                                  lambda k: modcol(l, 0, k), PS[0], PSt[0])
                        for c in range(16):
                            b = 1 + (c % 4)
                            for k in range(8):
                                P.op("pe", "matmul", PS[b][:, :], lhsT=win[:, k, c * 128:(c + 1) * 128], rhs=u[:, k, :],
                                     start=(k == 0), stop=(k == 7), R=[t_win, tu], W=[PSt[b]])
                            if c < 8:
                                i = n_ % 3
                                n_ += 1
                                P.op("act", "activation", out=gs[i][:], in_=PS[b][:, :], func=AF.Copy, R=[PSt[b]], W=[tgs[i]])
                                P.dma("sp", gate_d[:, c, ts_], gs[i][:], R=[tgs[i]], W=[])
                            else:
                                P.op("dve", "tensor_copy", out=xr[c - 8][:, 2 + tt * TW:2 + (tt + 1) * TW], in_=PS[b][:, :],
                                     R=[PSt[b]], W=[txr[c - 8]])
                    P.barrier()
                    P.emit()
                import os as _os
                with ExitStack() as st:
                    _nchunk = int(_os.environ.get("DBG_O2_CHUNKS", "8"))
                    cw = sbuf(st, "cw", [128, 32])
                    cbias = sbuf(st, "cbias", [128, 8])
                    rab = sbuf(st, "rab", [128, 16])
                    ixb = sbuf(st, "ixb", [128, 16])
                    lam = sbuf(st, "lam", [128, 16])
                    nsp = sbuf(st, "nsp", [128, 16])
                    nsp2 = sbuf(st, "nsp2", [128, 16])
                    tp_ = Tok()
                    for (dst, src) in ((cw, convw_d[o]), (cbias, convb_d[o]), (rab, rab_d[o]), (ixb, ixb_d[o]), (lam, rlam_d[o])):
                        P.dma("sp", dst[:], src, W=[tp_])
                    rabd = sbuf(st, "rabd", [128, 16, 128], BF16)
                    ixbd = sbuf(st, "ixbd", [128, 16, 128], BF16)
                    for hf in range(2):
                        P.dma("pool", rabd[:, hf * 8:(hf + 1) * 8, :],
                              rabd_d[o].rearrange("p (c m) -> p c m", m=128)[:, hf * 8:(hf + 1) * 8, :], W=[tp_])
                        P.dma("pool", ixbd[:, hf * 8:(hf + 1) * 8, :],
                              ixbd_d[o].rearrange("p (c m) -> p c m", m=128)[:, hf * 8:(hf + 1) * 8, :], W=[tp_])
                    P.op("act", "activation", out=nsp[:], in_=lam[:], func=AF.Exp, scale=-1.0, R=[tp_], W=[tp_])
                    P.op("act", "activation", out=nsp[:], in_=nsp[:], func=AF.Ln, bias=one_c[:], R=[tp_] + CT, W=[tp_])
                    P.op("dve", "tensor_scalar", out=nsp2[:], in0=nsp[:], scalar1=-16.0, scalar2=None, op0=ALU.mult, R=[tp_], W=[tp_])
                    P.op("dve", "tensor_scalar", out=nsp[:], in0=nsp[:], scalar1=-8.0, scalar2=None, op0=ALU.mult, R=[tp_], W=[tp_])
                    xc = sbuf(st, "xc", [128, S])
                    xcb = sbuf(st, "xcb", [128, S], BF16)
                    Rb = sbuf(st, "Rb", [128, S])
                    Ib = sbuf(st, "Ib", [128, S])
                    Ab = sbuf(st, "Ab", [128, S])
                    Yb = sbuf(st, "Yb", [128, S])
                    txc, txcb = Tok(), toks(NT)
                    tR, tI, tA_, tY = toks(NT), toks(NT), toks(NT), toks(NT)
                    gtile = [sbuf(st, "gt%d" % i, [128, S], BF16) for i in range(2)]
                    tgt_ = toks(2)
                    ot = [sbuf(st, "ot%d" % i, [128, TW], BF16) for i in range(2)]
                    tot = toks(2)
                    gg = [sbuf(st, "gg%d" % i, [128, TW]) for i in range(2)]
                    tgg = toks(2)
                    n_ = 0
                    for c in range(_nchunk):
                        P.dma("sp", gtile[c % 2][:], gate_d[:, c, :], W=[tgt_[c % 2]])
                        P.op("dve", "tensor_scalar", out=xc[:], in0=xr[c][:, 0:S], scalar1=cw[:, c * 4:c * 4 + 1],
                             scalar2=cbias[:, c:c + 1], op0=ALU.mult, op1=ALU.add, R=[txr[c], tp_], W=[txc])
                        for j in range(1, 4):
                            P.op("dve", "scalar_tensor_tensor", out=xc[:], in0=xr[c][:, j:j + S], scalar=cw[:, c * 4 + j:c * 4 + j + 1],
                                 in1=xc[:], op0=ALU.mult, op1=ALU.add, R=[txr[c], tp_, txc], W=[txc])
                        for tt in range(NT):
                            ts_ = slice(tt * TW, (tt + 1) * TW)
                            P.op("act", "activation", out=xcb[:, ts_], in_=xc[:, ts_], func=AF.Copy, R=[txc], W=[txcb[tt]])
                        for dr in range(2):
                            dc = dr * 8 + c
                            for tt in range(NT):
                                ts_ = slice(tt * TW, (tt + 1) * TW)
                                b = 1 + (n_ % 3)
                                b2_ = 4 + (n_ % 3)
                                n_ += 1
                                P.op("pe", "matmul", PS[b][:, :], lhsT=rabd[:, dc, :], rhs=xcb[:, ts_], start=True, stop=True,
                                     R=[tp_, txcb[tt]], W=[PSt[b]])
                                P.op("pe", "matmul", PS[b2_][:, :], lhsT=ixbd[:, dc, :], rhs=xcb[:, ts_], start=True, stop=True,
                                     R=[tp_, txcb[tt]], W=[PSt[b2_]])
                                P.op("act", "activation", out=Rb[:, ts_], in_=PS[b][:, :], func=AF.Sigmoid, bias=rab[:, dc:dc + 1],
                                     R=[PSt[b], tp_], W=[tR[tt]])
                                P.op("act", "activation", out=Ib[:, ts_], in_=PS[b2_][:, :], func=AF.Sigmoid, bias=ixb[:, dc:dc + 1],
                                     R=[PSt[b2_], tp_], W=[tI[tt]])
                            for tt in range(NT):
                                ts_ = slice(tt * TW, (tt + 1) * TW)
                                P.op("act", "activation", out=Ab[:, ts_], in_=Rb[:, ts_], func=AF.Exp, scale=nsp[:, dc:dc + 1],
                                     R=[tR[tt], tp_], W=[tA_[tt]])
                                P.op("act", "activation", out=Rb[:, ts_], in_=Rb[:, ts_], func=AF.Exp, scale=nsp2[:, dc:dc + 1],
                                     R=[tR[tt], tp_], W=[tR[tt]])
                            for tt in range(NT):
                                ts_ = slice(tt * TW, (tt + 1) * TW)
                                P.op("act", "activation", out=Rb[:, ts_], in_=Rb[:, ts_], func=AF.Sqrt, scale=-1.0, bias=one_c[:],
                                     R=[tR[tt]] + CT, W=[tR[tt]])
                                P.op("pool", "tensor_tensor", out=Ib[:, ts_], in0=Ib[:, ts_], in1=xc[:, ts_], op=ALU.mult,
                                     R=[tI[tt], txc], W=[tI[tt]])
                                P.op("dve", "tensor_tensor", out=Ib[:, ts_], in0=Ib[:, ts_], in1=Rb[:, ts_], op=ALU.mult,
                                     R=[tI[tt], tR[tt]], W=[tI[tt]])
                            order = range(NT) if dr == 0 else range(NT - 1, -1, -1)
                            prev = None
                            dst = Yb if dr == 0 else Rb
                            tdst = tY if dr == 0 else tR
                            for tt in order:
                                ts_ = slice(tt * TW, (tt + 1) * TW)
                                if dr == 0:
                                    init = 0.0 if prev is None else dst[:, prev * TW + TW - 1:prev * TW + TW]
                                    P.op("dve", "tensor_tensor_scan", out=dst[:, ts_], data0=Ab[:, ts_], data1=Ib[:, ts_], initial=init,
                                         op0=ALU.mult, op1=ALU.add,
                                         R=[tA_[tt], tI[tt]] + ([tdst[prev]] if prev is not None else []), W=[tdst[tt]])
                                else:
                                    init = 0.0 if prev is None else dst[:, prev * TW:prev * TW + 1]
                                    P.op("dve", "tensor_tensor_scan", out=dst[:, ts_][:, ::-1], data0=Ab[:, ts_][:, ::-1],
                                         data1=Ib[:, ts_][:, ::-1], initial=init, op0=ALU.mult, op1=ALU.add,
                                         R=[tA_[tt], tI[tt]] + ([tdst[prev]] if prev is not None else []), W=[tdst[tt]])
                                prev = tt
                        for tt in range(NT):
                            ts_ = slice(tt * TW, (tt + 1) * TW)
                            i = tt % 2
                            P.op("act", "activation", out=gg[i][:], in_=gtile[c % 2][:, ts_], func=AF.Gelu_apprx_tanh,
                                 R=[tgt_[c % 2]], W=[tgg[i]])
                            P.op("pool", "tensor_tensor", out=Yb[:, ts_], in0=Yb[:, ts_], in1=Rb[:, ts_], op=ALU.add,
                                 R=[tY[tt], tR[tt]], W=[tY[tt]])
                            P.op("dve", "tensor_tensor", out=ot[i][:], in0=Yb[:, ts_], in1=gg[i][:], op=ALU.mult,
                                 R=[tY[tt], tgg[i]], W=[tot[i]])
                            P.dma("sp", mix_d[:, c, ts_], ot[i][:], R=[tot[i]], W=[])
                    P.barrier()
                    P.emit()
            phase_tail(l, rout_d[o])

        for l in range(n_layers):
            if l % 2 == 0:
                phase_even(l)
            else:
                phase_odd(l)

        with ExitStack() as st:
            hh = [sbuf(st, "fh%d" % i, [128, 8, TW]) for i in range(2)]
            thh = toks(2)
            hn = sbuf(st, "fhn", [128, 8, TW])
            thn = Tok()
            ob = [sbuf(st, "fo%d" % i, [128, D]) for i in range(2)]
            tob = toks(2)
            nsc = norm_scratch(st, TW)
            tfin = []
            n_ = 0
            for tt in range(NT):
                ts_ = slice(tt * TW, (tt + 1) * TW)
                h, th = hh[tt % 2], thh[tt % 2]
                P.dma("sp", h[:], h_d[:, :, ts_], W=[th])
                norm_tile(nsc, h, th, TW, hn, thn, lambda k: fnw[:, k:k + 1], None, PS[0], PSt[0])
                for j in range(4):
                    o_, to_ = ob[n_ % 2], tob[n_ % 2]
                    for b in range(2):
                        bank = 1 + 2 * (n_ % 2) + b
                        for kk in range(4):
                            k = 4 * b + kk
                            P.op("pe", "transpose", out=PS[bank][:, kk * 128:(kk + 1) * 128], in_=hn[:, k, j * 128:(j + 1) * 128],
                                 identity=ident, R=[thn] + CT, W=[PSt[bank]])
                        if b == 0:
                            P.op("act", "activation", out=o_[:, 0:512], in_=PS[bank][:, :], func=AF.Copy, R=[PSt[bank]], W=[to_])
                        else:
                            P.op("dve", "tensor_copy", out=o_[:, 512:1024], in_=PS[bank][:, :], R=[PSt[bank]], W=[to_])
                    n_ += 1
                    tk = Tok()
                    r0 = tt * TW + j * 128
                    P.dma("sp", out_d[r0:r0 + 128, :], o_[:], R=[to_], W=[tk])
                    tfin.append(tk)
            for tk in tfin:
                P._wait("sp", tk.w[0], tk.w[1])
            P.barrier()
            P.emit()
        print("[kernel] program built: %d instructions" % P.ninst)
    return nc


def _fm(v, k):
    v = np.asarray(v, np.float32)
    lead = v.shape[:-1]
    v = v.reshape(lead + (k, 128))
    v = np.moveaxis(v, -1, 0)
    return np.ascontiguousarray(v)


def _rope_tables():
    rows = S // 64
    row_idx = np.repeat(np.arange(rows, dtype=np.float64), 64)
    col_idx = np.tile(np.arange(64, dtype=np.float64), rows)
    inv_freq = 10000.0 ** (-np.arange(16, dtype=np.float64) / 16)
    cos = np.zeros((128, S), np.float64)
    sin = np.zeros((128, S), np.float64)
    for p in range(128):
        d = p % 64
        a = d // 32
        b = (d // 16) % 2
        f = d % 16
        ang = (row_idx if a == 0 else col_idx) * inv_freq[f]
        cos[p] = np.cos(ang)
        sin[p] = (-np.sin(ang)) if b == 0 else np.sin(ang)
    return cos.astype(np.float32), sin.astype(np.float32)


def _consts():
    c = np.zeros((128, 400), np.float32)
    c[:, 0:128] = np.eye(128, dtype=np.float32)
    c[0:64, 128:192] = 1.0
    c[64:128, 192:256] = 1.0
    c[0:64, 384] = -1.0
    c[64:128, 384] = 1.0
    c[0:64, 385] = 1.0
    c[64:128, 385] = -1.0
    for g in range(8):
        c[16 * g:16 * g + 16, 386 + g] = 1.0
    return c


def prepare_shared(inp):
    f = lambda a: np.ascontiguousarray(np.asarray(a, np.float32))
    sh = {}
    sh["ada_w"] = f(inp["ada_w"])
    sh["ada_b_fm"] = np.ascontiguousarray(f(inp["ada_b"]).reshape(4, 48, 128).transpose(2, 0, 1).reshape(128, 192))
    sh["norm_w_fm"] = np.ascontiguousarray(f(inp["norm_w"]).reshape(4, 2, 8, 128).transpose(3, 0, 1, 2).reshape(128, 64))
    sh["fnw_fm"] = np.ascontiguousarray(f(inp["final_norm_w"]).reshape(8, 128).T)
    sh["mlp_w1"] = f(inp["mlp_w1"])
    sh["mlp_w2"] = f(inp["mlp_w2"])
    idx = np.arange(128)
    perm = idx ^ 16
    cols = list(range(512))
    for j in range(4):
        cols += list(512 + 128 * j + idx)
    for j in range(4):
        cols += list(512 + 128 * j + perm)
    for kv in range(2):
        cols += list(1024 + 64 * kv + (idx % 64))
    for kv in range(2):
        cols += list(1024 + 64 * kv + (perm % 64))
    cols += list(range(1152, 1280))
    cols = np.asarray(cols)
    assert cols.shape[0] == 2176
    sh["hyb_w_in_ext"] = np.ascontiguousarray(f(inp["hyb_w_in"])[:, :, cols])
    sh["hyb_w_out"] = f(inp["hyb_w_out"])

    def stk(a):
        a = f(a).transpose(0, 3, 1, 2).reshape(2, 64, 64)
        return np.ascontiguousarray(np.concatenate([a, a], axis=1))
    sh["s5_lamre"] = stk(inp["s5_lam_re"])
    sh["s5_lamim"] = stk(inp["s5_lam_im"])
    sh["s5_logdt"] = np.ascontiguousarray(np.broadcast_to(f(inp["s5_log_dt"]).reshape(2, 1, 64), (2, 128, 64)))
    bre = f(inp["s5_b_re"]).transpose(0, 3, 1, 2, 4).reshape(2, 64, 1024)
    bim = f(inp["s5_b_im"]).transpose(0, 3, 1, 2, 4).reshape(2, 64, 1024)
    sh["s5_b1"] = np.ascontiguousarray(np.concatenate([bre, bim], axis=1))
    sh["s5_b2"] = np.ascontiguousarray(np.concatenate([bim, bre], axis=1))
    cre = f(inp["s5_c_re"]).transpose(0, 4, 1, 2, 3).reshape(2, 64, 1024)
    cim = f(inp["s5_c_im"]).transpose(0, 4, 1, 2, 3).reshape(2, 64, 1024)
    sh["s5_c1"] = np.ascontiguousarray(np.concatenate([cre, cim], axis=1))
    sh["s5_c2"] = np.ascontiguousarray(np.concatenate([cim, cre], axis=1))
    sh["s5_d_fm"] = np.ascontiguousarray(f(inp["s5_d"]).reshape(2, 4, 128).transpose(0, 2, 1))
    sh["s5_glu_w"] = f(inp["s5_glu_w"])
    sh["s5_glu_b_fm"] = np.ascontiguousarray(f(inp["s5_glu_b"]).reshape(2, 4, 128).transpose(0, 2, 1))
    qn = f(inp["attn_q_norm"])
    kn = f(inp["attn_k_norm"])
    d = idx % 64
    dp = (idx ^ 16) % 64
    sh["qkn_fm"] = np.ascontiguousarray(np.stack([qn[:, d], qn[:, dp], kn[:, d], kn[:, dp]], axis=-1))
    sh["rec_w_in"] = f(inp["rec_w_in"])
    sh["rec_w_out"] = f(inp["rec_w_out"])
    sh["rec_conv_fm"] = np.ascontiguousarray(f(inp["rec_conv_w"]).reshape(2, 4, 8, 128).transpose(0, 3, 2, 1).reshape(2, 128, 32))
    sh["rec_convb_fm"] = np.ascontiguousarray(f(inp["rec_conv_b"]).reshape(2, 8, 128).transpose(0, 2, 1))

    def bd(w):
        w = f(w)
        o = np.zeros((2, 128, 2, 8, 128), np.float32)
        for c in range(8):
            for hl in range(2):
                o[:, 64 * hl:64 * hl + 64, :, c, 64 * hl:64 * hl + 64] = w[:, :, 2 * c + hl].transpose(0, 2, 1, 3)
        return np.ascontiguousarray(o.reshape(2, 128, 2048))
    sh["rec_ra_bd"] = bd(inp["rec_ra_w"])
    sh["rec_ix_bd"] = bd(inp["rec_ix_w"])

    def fm2(a):
        return np.ascontiguousarray(f(a).reshape(2, 2, 8, 128).transpose(0, 3, 1, 2).reshape(2, 128, 16))
    sh["rec_rab_fm"] = fm2(inp["rec_ra_b"])
    sh["rec_ixb_fm"] = fm2(inp["rec_ix_b"])
    sh["rec_lam_fm"] = fm2(inp["rec_lam"])
    rc, rs = _rope_tables()
    sh["rope_cos"] = rc
    sh["rope_sin"] = rs
    sh["consts"] = _consts()
    return sh


_CACHE = {}


def run(inputs, core_batches, n_layers=4, trace=False):
    key = n_layers
    if key not in _CACHE:
        _CACHE[key] = build_program(n_layers)
    nc = _CACHE[key]
    sh = prepare_shared(inputs)
    x = np.asarray(inputs["x"], np.float32)
    c = np.asarray(inputs["c"], np.float32)
    in_maps = []
    for b in core_batches:
        m = dict(sh)
        m["x"] = np.ascontiguousarray(x[b])
        m["c_fm"] = np.ascontiguousarray(c[b].reshape(8, 128).T)
        in_maps.append(m)
    res = run_bass_kernel_spmd(nc, in_maps, core_ids=list(range(len(core_batches))), **({"trace": True} if trace else {}))
    outs = np.stack([np.asarray(r["out"], np.float32) for r in res.results], axis=0)
    return outs, res


def kernel(**inputs):
    outs, _ = run(inputs, list(range(8)), 4)
    return outs.astype(np.float32)
```

```python
import math
import numpy as np
from contextlib import ExitStack
import concourse.bass as bass
import concourse.mybir as mybir
from concourse.bass_utils import run_bass_kernel_spmd

F32 = mybir.dt.float32
BF16 = mybir.dt.bfloat16
I32 = mybir.dt.int32
AF = mybir.ActivationFunctionType
ALU = mybir.AluOpType

S = 4096
D = 1024
EPS = 1e-6
NT = 8
TW = 512
PI = math.pi


class Tok:
    __slots__ = ("w", "r", "x")

    def __init__(self, x=False):
        self.w = None
        self.r = {}
        self.x = x


def toks(n):
    return [Tok() for _ in range(n)]


class Prog:
    ENGS = ("pe", "act", "dve", "pool", "sp")

    def __init__(self, nc, n_dma_sems=12):
        self.nc = nc
        self.ops = {e: [] for e in self.ENGS}
        self.cnt = {}
        self.sems = {}
        self.waited = {e: {} for e in self.ENGS}
        self.n_dma_sems = n_dma_sems
        self.dma_rr = {}
        self.ninst = 0

    def alloc_sems(self, stack):
        for e in ("pe", "act", "dve", "pool"):
            self.sems[e] = stack.enter_context(self.nc.semaphore("s_" + e))
            self.cnt[e] = 0
        for q in ("sp", "pool"):
            self.dma_rr[q] = 0
            for i in range(self.n_dma_sems):
                k = "d_%s_%d" % (q, i)
                self.sems[k] = stack.enter_context(self.nc.semaphore(k))
                self.cnt[k] = 0

    def _wait(self, E, key, val):
        if val <= 0 or self.waited[E].get(key, 0) >= val:
            return
        self.waited[E][key] = val
        sem = self.sems[key]
        self.ops[E].append(lambda eng, sem=sem, val=val: eng.wait_ge(sem, val))
        self.ninst += 1

    @staticmethod
    def _deps(reads, writes):
        need = {}
        for t in reads:
            if t.w is not None:
                k, v = t.w
                if need.get(k, 0) < v:
                    need[k] = v
        for t in writes:
            if t.w is not None:
                k, v = t.w
                if need.get(k, 0) < v:
                    need[k] = v
            for k, v in t.r.items():
                if need.get(k, 0) < v:
                    need[k] = v
        return need

    def op(self, E, meth, *args, R=(), W=(), **kw):
        if any(t.x for t in R):
            W = list(W) + [t for t in R if t.x]
            R = [t for t in R if not t.x]
        need = self._deps(R, W)
        for k, v in need.items():
            if k == E and E == "pe":
                continue
            self._wait(E, k, v)
        self.cnt[E] += 1
        idx = self.cnt[E]
        sem = self.sems[E]
        self.ops[E].append(lambda eng, meth=meth, args=args, kw=kw, sem=sem:
                           getattr(eng, meth)(*args, **kw).then_inc(sem, 1))
        self.ninst += 1
        for t in R:
            t.r[E] = idx
        for t in W:
            t.w = (E, idx)
            t.r = {}

    def dma(self, Q, out, in_, R=(), W=()):
        need = self._deps(R, W)
        for k, v in need.items():
            self._wait(Q, k, v)
        i = self.dma_rr[Q]
        self.dma_rr[Q] = (i + 1) % self.n_dma_sems
        key = "d_%s_%d" % (Q, i)
        self._wait(Q, key, self.cnt[key])
        self.cnt[key] += 16
        val = self.cnt[key]
        sem = self.sems[key]
        self.ops[Q].append(lambda eng, out=out, in_=in_, sem=sem: eng.dma_start(out=out, in_=in_).then_inc(sem, 16))
        self.ninst += 1
        for t in R:
            t.r[key] = val
        for t in W:
            t.w = (key, val)
            t.r = {}

    def barrier(self):
        for E in self.ENGS:
            for key, v in self.cnt.items():
                if key != E:
                    self._wait(E, key, v)

    def emit(self, name=None):
        nc = self.nc
        ops = self.ops
        self.ops = {e: [] for e in self.ENGS}
        self.nphase = getattr(self, "nphase", 0) + 1
        with nc.named_scope("ph%02d" % self.nphase), nc.Block() as block:
            @block.tensor
            def _(eng):
                for f in ops["pe"]:
                    f(eng)

            @block.scalar
            def _(eng):
                for f in ops["act"]:
                    f(eng)

            @block.vector
            def _(eng):
                for f in ops["dve"]:
                    f(eng)

            @block.gpsimd
            def _(eng):
                for f in ops["pool"]:
                    f(eng)

            @block.sync
            def _(eng):
                for f in ops["sp"]:
                    f(eng)


def build_program(n_layers=4):
    nc = bass.Bass("TRN2", target_bir_lowering=False)

    def din(name, shape):
        return nc.dram_tensor(name, list(shape), F32, kind="ExternalInput").ap()

    x_d = din("x", [S, D])
    c_d = din("c_fm", [128, 8])
    adaw_d = din("ada_w", [4, 1024, 6144])
    adab_d = din("ada_b_fm", [128, 192])
    normw_d = din("norm_w_fm", [128, 64])
    fnw_d = din("fnw_fm", [128, 8])
    w1_d = din("mlp_w1", [4, 1024, 4096])
    w2_d = din("mlp_w2", [4, 4096, 1024])
    hin_d = din("hyb_w_in_ext", [2, 1024, 2176])
    hout_d = din("hyb_w_out", [2, 1024, 1024])
    lamre_d = din("s5_lamre", [2, 128, 64])
    lamim_d = din("s5_lamim", [2, 128, 64])
    logdt_d = din("s5_logdt", [2, 128, 64])
    sb1_d = din("s5_b1", [2, 128, 1024])
    sb2_d = din("s5_b2", [2, 128, 1024])
    sc1_d = din("s5_c1", [2, 128, 1024])
    sc2_d = din("s5_c2", [2, 128, 1024])
    s5d_d = din("s5_d_fm", [2, 128, 4])
    gluw_d = din("s5_glu_w", [2, 512, 512])
    glub_d = din("s5_glu_b_fm", [2, 128, 4])
    qkn_d = din("qkn_fm", [2, 128, 4])
    rin_d = din("rec_w_in", [2, 1024, 2048])
    rout_d = din("rec_w_out", [2, 1024, 1024])
    convw_d = din("rec_conv_fm", [2, 128, 32])
    convb_d = din("rec_convb_fm", [2, 128, 8])
    rabd_d = din("rec_ra_bd", [2, 128, 2048])
    ixbd_d = din("rec_ix_bd", [2, 128, 2048])
    rab_d = din("rec_rab_fm", [2, 128, 16])
    ixb_d = din("rec_ixb_fm", [2, 128, 16])
    rlam_d = din("rec_lam_fm", [2, 128, 16])
    rope_c_d = din("rope_cos", [128, S])
    rope_s_d = din("rope_sin", [128, S])
    const_d = din("consts", [128, 400])
    out_d = nc.dram_tensor("out", [S, D], F32, kind="ExternalOutput").ap()

    h_d = nc.dram_tensor("h_scr", [128, 8, S], F32, kind="Internal").ap()
    mix_d = nc.dram_tensor("mix_scr", [128, 8, S], BF16, kind="Internal").ap()
    gate_d = nc.dram_tensor("gate_scr", [128, 8, S], BF16, kind="Internal").ap()

    with ExitStack() as g:
        P = Prog(nc)
        P.alloc_sems(g)

        def gsb(name, shape, dt=F32):
            return g.enter_context(nc.sbuf_tensor(name, list(shape), dt))

        PS = [g.enter_context(nc.psum_tensor("ps%d" % i, [128, 512], F32)) for i in range(8)]
        PSt = [Tok(x=True) for _ in range(8)]

        cst = gsb("cst", [128, 400])
        t_cst = Tok()
        P.dma("sp", cst[:], const_d, W=[t_cst])
        ident = cst[:, 0:128]
        swapm = cst[:, 256:384]
        sgn = cst[:, 384:385]
        nsgn = cst[:, 385:386]
        gmask = cst[:, 386:394]
        ones_f = gsb("ones_f", [128, 128])
        ones_b = gsb("ones_b", [128, 128], BF16)
        bones_b = gsb("bones_b", [128, 128], BF16)
        eps_c = gsb("eps_c", [128, 1])
        one_c = gsb("one_c", [128, 1])
        t_c2 = Tok()
        P.op("pool", "memset", ones_f[:], 1.0, W=[t_c2])
        P.op("pool", "memset", ones_b[:], 1.0, W=[t_c2])
        P.op("pool", "memset", eps_c[:], EPS, W=[t_c2])
        P.op("pool", "memset", one_c[:], 1.0, W=[t_c2])
        P.op("pool", "tensor_copy", out=bones_b[:], in_=cst[:, 128:256], R=[t_cst], W=[t_c2])
        swapm_b = gsb("swapm_b", [128, 128], BF16)
        P.op("pool", "tensor_copy", out=swapm_b[:], in_=cst[:, 256:384], R=[t_cst], W=[t_c2])
        CT = [t_cst, t_c2]

        modall = gsb("modall", [128, 192])
        normw = gsb("normw", [128, 64])
        fnw = gsb("fnw", [128, 8])
        A1 = gsb("A1", [128, 32])
        A2 = gsb("A2", [128, 32])
        t_mod = Tok()

        uid = [0]

        def sbuf(st, name, shape, dt=F32):
            uid[0] += 1
            return st.enter_context(nc.sbuf_tensor("%s_u%d" % (name, uid[0]), list(shape), dt))

        with ExitStack() as st:
            cf = sbuf(st, "cf", [128, 8])
            cb = sbuf(st, "cb", [128, 8], BF16)
            adab = sbuf(st, "adab", [128, 192])
            t_cf, t_cb, t_ab, t_nw = toks(4)
            P.dma("sp", cf[:], c_d, W=[t_cf])
            P.dma("sp", adab[:], adab_d, W=[t_ab])
            P.dma("sp", normw[:], normw_d, W=[t_nw])
            P.dma("sp", fnw[:], fnw_d, W=[t_nw])
            P.op("act", "activation", out=cb[:], in_=cf[:], func=AF.Silu, R=[t_cf], W=[t_cb])
            NB = 3
            wt = [sbuf(st, "adaw%d" % i, [128, 8, 512], BF16) for i in range(NB)]
            twt = toks(NB)
            n = 0
            for l in range(4):
                src_l = adaw_d[l].rearrange("(k p) n -> p k n", p=128)
                for blk in range(12):
                    i = n % NB
                    n += 1
                    P.dma("pool", wt[i][:], src_l[:, :, blk * 512:(blk + 1) * 512], W=[twt[i]])
                    for j in range(4):
                        col = l * 48 + blk * 4 + j
                        for k in range(8):
                            P.op("pe", "matmul", PS[0][:, col:col + 1], lhsT=wt[i][:, k, j * 128:(j + 1) * 128],
                                 rhs=cb[:, k:k + 1], start=(k == 0), stop=(k == 7), R=[twt[i], t_cb], W=[PSt[0]])
            P.op("dve", "tensor_tensor", out=modall[:], in0=PS[0][:, 0:192], in1=adab[:], op=ALU.add,
                 R=[PSt[0], t_ab], W=[t_mod])
            for l in range(4):
                for (A, sc0, nw0) in ((A1, 8, 0), (A2, 32, 8)):
                    P.op("dve", "tensor_scalar", out=A[:, l * 8:(l + 1) * 8], in0=modall[:, l * 48 + sc0:l * 48 + sc0 + 8],
                         scalar1=1.0, scalar2=None, op0=ALU.add, R=[t_mod], W=[t_mod])
                    P.op("dve", "tensor_tensor", out=A[:, l * 8:(l + 1) * 8], in0=A[:, l * 8:(l + 1) * 8],
                         in1=normw[:, l * 16 + nw0:l * 16 + nw0 + 8], op=ALU.mult, R=[t_mod, t_nw], W=[t_mod])
            P.barrier()
            P.emit()

        def modcol(l, which, k):
            c = l * 48 + which * 8 + k
            return modall[:, c:c + 1]

        def norm_tile(st_tiles, h, th, N, u, tu, Acol, Bcol, pss, tpss):
            sq, tsq, srt, tsrt, rstd, trstd, tmp, ttmp = st_tiles
            for k in range(8):
                P.op("act", "activation", out=sq[k % 2][:, :N], in_=h[:, k, :N], func=AF.Square, R=[th], W=[tsq[k % 2]])
                P.op("pe", "matmul", pss[:, :N], lhsT=ones_b[:], rhs=sq[k % 2][:, :N], start=(k == 0), stop=(k == 7),
                     R=[tsq[k % 2]] + CT, W=[tpss])
            P.op("act", "activation", out=srt[:, :N], in_=pss[:, :N], func=AF.Sqrt, scale=1.0 / D, bias=eps_c[:],
                 R=[tpss] + CT, W=[tsrt])
            P.op("dve", "reciprocal", out=rstd[:, :N], in_=srt[:, :N], R=[tsrt], W=[trstd])
            for k in range(8):
                P.op("dve", "tensor_tensor", out=tmp[k % 2][:, :N], in0=h[:, k, :N], in1=rstd[:, :N], op=ALU.mult,
                     R=[th, trstd], W=[ttmp[k % 2]])
                if Bcol is None:
                    P.op("act", "activation", out=u[:, k, :N], in_=tmp[k % 2][:, :N], func=AF.Identity, scale=Acol(k),
                         R=[ttmp[k % 2], t_mod], W=[tu])
                else:
                    P.op("act", "activation", out=u[:, k, :N], in_=tmp[k % 2][:, :N], func=AF.Identity, scale=Acol(k),
                         bias=Bcol(k), R=[ttmp[k % 2], t_mod], W=[tu])

        def norm_scratch(st, N):
            sq = [sbuf(st, "n_sq%d" % i, [128, N], BF16) for i in range(2)]
            srt = sbuf(st, "n_srt", [128, N])
            rstd = sbuf(st, "n_rstd", [128, N])
            tmp = [sbuf(st, "n_tmp%d" % i, [128, N]) for i in range(2)]
            return (sq, toks(2), srt, Tok(), rstd, Tok(), tmp, toks(2))

        with ExitStack() as st:
            xt = [sbuf(st, "xt%d" % i, [128, D]) for i in range(2)]
            txt = toks(2)
            stg = [sbuf(st, "stg%d" % i, [128, 8, TW]) for i in range(2)]
            tstg = toks(2)
            for tt in range(NT):
                sg, tsg = stg[tt % 2], tstg[tt % 2]
                for j in range(4):
                    i = tt * 4 + j
                    xx, txx = xt[i % 2], txt[i % 2]
                    P.dma("sp", xx[:], x_d[i * 128:(i + 1) * 128, :], W=[txx])
                    for b in range(2):
                        bank = 2 * (i % 2) + b
                        for kk in range(4):
                            k = 4 * b + kk
                            P.op("pe", "transpose", out=PS[bank][:, kk * 128:(kk + 1) * 128], in_=xx[:, k * 128:(k + 1) * 128],
                                 identity=ident, R=[txx] + CT, W=[PSt[bank]])
                        P.op("act" if b == 0 else "dve", *(("activation",) if b == 0 else ("tensor_copy",)),
                             out=sg[:, 4 * b:4 * b + 4, j * 128:(j + 1) * 128],
                             in_=PS[bank][:, :].rearrange("p (k t) -> p k t", t=128),
                             **({"func": AF.Copy} if b == 0 else {}), R=[PSt[bank]], W=[tsg])
                P.dma("sp", h_d[:, :, tt * TW:(tt + 1) * TW], sg[:], R=[tsg], W=[])
            P.barrier()
            P.emit()

        def phase_tail(l, wout_src):
            TT = 256
            with ExitStack() as st:
                wo = sbuf(st, "wo", [128, 8, 1024], BF16)
                w1 = sbuf(st, "w1", [128, 8, 4096], BF16)
                w2 = sbuf(st, "w2", [128, 32, 1024], BF16)
                t_wo = Tok()
                t_w1, t_w2 = toks(4), toks(4)
                P.dma("pool", wo[:], wout_src.rearrange("(k p) n -> p k n", p=128), W=[t_wo])
                w1s = w1_d[l].rearrange("(k p) n -> p k n", p=128)
                for q4 in range(4):
                    P.dma("pool", w1[:, :, q4 * 1024:(q4 + 1) * 1024], w1s[:, :, q4 * 1024:(q4 + 1) * 1024], W=[t_w1[q4]])
                w2s = w2_d[l].rearrange("(k p) n -> p k n", p=128)
                for q4 in range(4):
                    P.dma("pool", w2[:, q4 * 8:(q4 + 1) * 8, :], w2s[:, q4 * 8:(q4 + 1) * 8, :], W=[t_w2[q4]])
                mx = [sbuf(st, "mx%d" % i, [128, 8, TT], BF16) for i in range(2)]
                tmx = toks(2)
                hh = [sbuf(st, "hh%d" % i, [128, 8, TT]) for i in range(2)]
                thh = toks(2)
                u2s = [sbuf(st, "u2_%d" % i, [128, 8, TT], BF16) for i in range(2)]
                tu2s = toks(2)
                ff = sbuf(st, "ff", [128, 32, TT], BF16)
                tff = toks(32)
                rl = [sbuf(st, "rl%d" % i, [128, TT], BF16) for i in range(3)]
                trl = toks(3)
                nsc = norm_scratch(st, TT)
                NIT = S // TT

                def stage_a1(it):
                    t0 = it * TT
                    m, tm = mx[it % 2], tmx[it % 2]
                    h, th = hh[it % 2], thh[it % 2]
                    P.dma("sp", m[:], mix_d[:, :, t0:t0 + TT], W=[tm])
                    P.dma("sp", h[:], h_d[:, :, t0:t0 + TT], W=[th])
                    for mo in range(8):
                        b = 1 + (mo % 2)
                        for k in range(8):
                            P.op("pe", "matmul", PS[b][:, :TT], lhsT=wo[:, k, mo * 128:(mo + 1) * 128], rhs=m[:, k, :],
                                 start=(k == 0), stop=(k == 7), R=[t_wo, tm], W=[PSt[b]])
                        P.op("dve", "scalar_tensor_tensor", out=h[:, mo, :], in0=PS[b][:, :TT], scalar=modcol(l, 2, mo),
                             in1=h[:, mo, :], op0=ALU.mult, op1=ALU.add, R=[PSt[b], t_mod, th], W=[th])

                def stage_a2(it):
                    norm_tile(nsc, hh[it % 2], thh[it % 2], TT, u2s[it % 2], tu2s[it % 2],
                              lambda k: A2[:, l * 8 + k:l * 8 + k + 1], lambda k: modcol(l, 3, k), PS[0], PSt[0])

                stage_a1(0)
                stage_a2(0)
                for it in range(NIT):
                    t0 = it * TT
                    h, th = hh[it % 2], thh[it % 2]
                    u2, tu2 = u2s[it % 2], tu2s[it % 2]
                    for f in range(32):
                        b = 3 + (f % 3)
                        for k in range(8):
                            P.op("pe", "matmul", PS[b][:, :TT], lhsT=w1[:, k, f * 128:(f + 1) * 128], rhs=u2[:, k, :],
                                 start=(k == 0), stop=(k == 7), R=[t_w1[f // 8], tu2], W=[PSt[b]])
                        r, tr = rl[f % 3], trl[f % 3]
                        P.op("act", "activation", out=r[:], in_=PS[b][:, :TT], func=AF.Relu, R=[PSt[b]], W=[tr])
                        P.op("pool", "tensor_tensor", out=ff[:, f, :], in0=r[:], in1=r[:], op=ALU.mult, R=[tr], W=[tff[f]])
                    if it + 1 < NIT:
                        stage_a1(it + 1)
                    for mo in range(8):
                        if mo == 4 and it + 1 < NIT:
                            stage_a2(it + 1)
                        b = 6 + (mo % 2)
                        for f in range(32):
                            P.op("pe", "matmul", PS[b][:, :TT], lhsT=w2[:, f, mo * 128:(mo + 1) * 128], rhs=ff[:, f, :],
                                 start=(f == 0), stop=(f == 31), R=[t_w2[f // 8], tff[f]], W=[PSt[b]])
                        P.op("dve", "scalar_tensor_tensor", out=h[:, mo, :], in0=PS[b][:, :TT], scalar=modcol(l, 5, mo),
                             in1=h[:, mo, :], op0=ALU.mult, op1=ALU.add, R=[PSt[b], t_mod, th], W=[th])
                    P.dma("sp", h_d[:, :, t0:t0 + TT], h[:], R=[th], W=[])
                P.barrier()
                P.emit()

        def phase_even(l):
            e = l // 2
            with ExitStack() as lst:
                zs5 = [sbuf(lst, "zs5_%d" % i, [128, S], BF16) for i in range(4)]
                tzs5 = [toks(NT) for _ in range(4)]
                with ExitStack() as ast:
                    qb = [sbuf(ast, "q%d" % i, [128, S], BF16) for i in range(4)]
                    tq = [toks(NT) for _ in range(4)]
                    kd = [sbuf(ast, "kd%d" % i, [128, S], BF16) for i in range(2)]
                    tkd = [Tok() for _ in range(2)]
                    Vx = sbuf(ast, "Vx", [128, 32, 2, 192], BF16)
                    tVx = Tok()
                    with ExitStack() as st:
                        win = sbuf(st, "win", [128, 8, 2176], BF16)
                        t_win = Tok()
                        wsrc = hin_d[e].rearrange("(k p) n -> p k n", p=128)
                        for q4 in range(4):
                            P.dma("pool", win[:, :, q4 * 544:(q4 + 1) * 544], wsrc[:, :, q4 * 544:(q4 + 1) * 544], W=[t_win])
                        qkn = sbuf(st, "qkn", [128, 4])
                        t_qkn = Tok()
                        P.dma("sp", qkn[:], qkn_d[e], W=[t_qkn])
                        P.op("pool", "memset", Vx[:], 0.0, W=[tVx])
                        P.op("pool", "memset", Vx[:, :, :, 64:65], 1.0, W=[tVx])
                        hh = sbuf(st, "hh", [128, 8, TW])
                        thh = Tok()
                        u = sbuf(st, "u", [128, 8, TW], BF16)
                        tu = Tok()
                        rc = [sbuf(st, "rc%d" % i, [128, TW]) for i in range(2)]
                        rs = [sbuf(st, "rs%d" % i, [128, TW]) for i in range(2)]
                        trc = toks(2)
                        sqh = sbuf(st, "sqh", [128, TW], BF16)
                        srt = sbuf(st, "srt", [128, TW])
                        rstd = sbuf(st, "rstd", [128, TW])
                        t1 = sbuf(st, "t1", [128, TW])
                        t2 = sbuf(st, "t2", [128, TW])
                        tsqh, tsrt, trstd, tt1, tt2 = toks(5)
                        nsc = norm_scratch(st, TW)
                        for tt in range(NT):
                            ts_ = slice(tt * TW, (tt + 1) * TW)
                            P.dma("sp", hh[:], h_d[:, :, ts_], W=[thh])
                            P.dma("sp", rc[tt % 2][:], rope_c_d[:, ts_], W=[trc[tt % 2]])
                            P.dma("sp", rs[tt % 2][:], rope_s_d[:, ts_], W=[trc[tt % 2]])
                            norm_tile(nsc, hh, thh, TW, u, tu, lambda k: A1[:, l * 8 + k:l * 8 + k + 1],
                                      lambda k: modcol(l, 0, k), PS[0], PSt[0])

                            def proj(bank, c0):
                                for k in range(8):
                                    P.op("pe", "matmul", PS[bank][:, :], lhsT=win[:, k, c0:c0 + 128], rhs=u[:, k, :],
                                         start=(k == 0), stop=(k == 7), R=[t_win, tu], W=[PSt[bank]])
                            for cs in range(4):
                                b = 1 + (cs % 2)
                                proj(b, cs * 128)
                                P.op("act", "activation", out=zs5[cs][:, ts_], in_=PS[b][:, :], func=AF.Copy,
                                     R=[PSt[b]], W=[tzs5[cs][tt]])
                            items = [(512 + 128 * j, 1024 + 128 * j, qb[j], tq[j][tt], 0, 1) for j in range(4)]
                            items += [(1536 + 128 * j, 1792 + 128 * j, kd[j], tkd[j], 2, 3) for j in range(2)]
                            for n_, (cz, cp, dst, tdst, wc, wpc) in enumerate(items):
                                bz = 3 + (n_ % 2)
                                bp = 5 + (n_ % 2)
                                proj(bz, cz)
                                proj(bp, cp)
                                P.op("act", "activation", out=sqh[:], in_=PS[bz][:, :], func=AF.Square, R=[PSt[bz]], W=[tsqh])
                                P.op("pe", "matmul", PS[7][:, :], lhsT=bones_b[:], rhs=sqh[:], start=True, stop=True,
                                     R=[tsqh] + CT, W=[PSt[7]])
                                P.op("act", "activation", out=srt[:], in_=PS[7][:, :], func=AF.Sqrt, scale=1.0 / 64, bias=eps_c[:],
                                     R=[PSt[7]] + CT, W=[tsrt])
                                P.op("dve", "reciprocal", out=rstd[:], in_=srt[:], R=[tsrt], W=[trstd])
                                P.op("dve", "scalar_tensor_tensor", out=t1[:], in0=PS[bz][:, :], scalar=qkn[:, wc:wc + 1],
                                     in1=rc[tt % 2][:], op0=ALU.mult, op1=ALU.mult, R=[PSt[bz], t_qkn, trc[tt % 2]], W=[tt1])
                                P.op("dve", "scalar_tensor_tensor", out=t2[:], in0=PS[bp][:, :], scalar=qkn[:, wpc:wpc + 1],
                                     in1=rs[tt % 2][:], op0=ALU.mult, op1=ALU.mult, R=[PSt[bp], t_qkn, trc[tt % 2]], W=[tt2])
                                P.op("pool", "tensor_tensor", out=t1[:], in0=t1[:], in1=t2[:], op=ALU.add, R=[tt1, tt2], W=[tt1])
                                P.op("dve", "tensor_tensor", out=dst[:, ts_], in0=t1[:], in1=rstd[:], op=ALU.mult,
                                     R=[tt1, trstd], W=[tdst])
                            b = 1 + (tt % 2)
                            for j in range(4):
                                for k in range(8):
                                    P.op("pe", "matmul", PS[b][:, j * 128:(j + 1) * 128], lhsT=u[:, k, j * 128:(j + 1) * 128],
                                         rhs=win[:, k, 2048:2176], start=(k == 0), stop=(k == 7), R=[t_win, tu], W=[PSt[b]])
                            src = PS[b][:, :].rearrange("p (j v d) -> p j v d", j=4, v=2)
                            P.op("act", "activation", out=Vx[:, tt * 4:(tt + 1) * 4, :, 0:64], in_=src, func=AF.Copy,
                                 R=[PSt[b]], W=[tVx])
                            P.op("dve", "tensor_copy", out=Vx[:, tt * 4:(tt + 1) * 4, :, 128:192], in_=src, R=[PSt[b]], W=[tVx])
                        P.barrier()
                        P.emit()
                    with ExitStack() as st:
                        NSB = 3
                        pt = [sbuf(st, "pt%d" % i, [128, TW], BF16) for i in range(NSB)]
                        tpt = toks(NSB)
                        rec = [sbuf(st, "rec%d" % i, [128, TW]) for i in range(2)]
                        bcs = [sbuf(st, "bcs%d" % i, [128, TW]) for i in range(2)]
                        trec, tbcs = toks(2), toks(2)
                        oat = [sbuf(st, "oat%d" % i, [128, TW], BF16) for i in range(2)]
                        toat = toks(2)
                        steps = [(j, qt, hhf, kt) for j in range(4) for qt in range(NT) for hhf in range(2) for kt in range(32)]
                        NS = len(steps)
                        qz = [sbuf(st, "qz%d" % i, [128, S], BF16) for i in range(4)]
                        tqz = toks(4)
                        for j in range(4):
                            P.op("pool", "memset", qz[j][0:64, :], 0.0, W=[tqz[j]])
                            P.op("act" if j % 2 == 0 else "dve", *(("activation",) if j % 2 == 0 else ("tensor_copy",)),
                                 out=qz[j][64:128, :], in_=qb[j][64:128, :], **({"func": AF.Copy} if j % 2 == 0 else {}),
                                 R=tq[j], W=[tqz[j]])
                            P.op("pool", "memset", qb[j][64:128, :], 0.0, R=[tqz[j]], W=tq[j])

                        def emit_qk(s_):
                            j, qt, hhf, kt = steps[s_]
                            kv, bs = j // 2, s_ % NSB
                            qsrc = qb[j] if hhf == 0 else qz[j]
                            P.op("pe", "matmul", PS[bs][:, :], lhsT=kd[kv][:, kt * 128:(kt + 1) * 128],
                                 rhs=qsrc[:, qt * TW:(qt + 1) * TW], start=True, stop=True,
                                 R=[tkd[kv], tq[j][qt], tqz[j]], W=[PSt[bs]])

                        pending = []
                        emit_qk(0)
                        emit_qk(1)
                        for s_ in range(NS):
                            j, qt, hhf, kt = steps[s_]
                            kv, bs = j // 2, s_ % NSB
                            un = s_ // 32
                            bo = 3 + (un % 2)
                            P.op("act", "activation", out=pt[bs][:], in_=PS[bs][:, :], func=AF.Exp, scale=0.125,
                                 R=[PSt[bs]], W=[tpt[bs]])
                            if s_ + 2 < NS:
                                emit_qk(s_ + 2)
                            if hhf == 0:
                                P.op("pe", "matmul", PS[bo][:, :], lhsT=Vx[:, kt, kv, 0:128], rhs=pt[bs][:],
                                     start=(kt == 0), stop=(kt == 31), R=[tVx, tpt[bs]], W=[PSt[bo]])
                            else:
                                P.op("pe", "matmul", PS[bo][:, :], lhsT=Vx[:, kt, kv, 64:192], rhs=pt[bs][:],
                                     start=(kt == 0), stop=(kt == 31), R=[tVx, tpt[bs]], W=[PSt[bo]])
                            if kt == 31:
                                ob_ = (j * NT + qt) % 2
                                o, to = oat[ob_], toat[ob_]
                                rr, trr, bb, tbb = rec[un % 2], trec[un % 2], bcs[un % 2], tbcs[un % 2]
                                qs = slice(qt * TW, (qt + 1) * TW)
                                if hhf == 0:
                                    P.op("dve", "reciprocal", out=rr[64:65, :], in_=PS[bo][64:65, :], R=[PSt[bo]], W=[trr])

                                    def f2(rr=rr, trr=trr):
                                        P.op("pe", "matmul", PS[5][0:64, :], lhsT=ones_f[64:65, 0:64], rhs=rr[64:65, :],
                                             start=True, stop=True, R=[trr] + CT, W=[PSt[5]])

                                    def f3(bb=bb, tbb=tbb, o=o, to=to, bo=bo):
                                        P.op("act", "activation", out=bb[0:64, :], in_=PS[5][0:64, :], func=AF.Copy,
                                             R=[PSt[5]], W=[tbb])
                                        P.op("dve", "tensor_tensor", out=o[0:64, :], in0=PS[bo][0:64, :], in1=bb[0:64, :],
                                             op=ALU.mult, R=[PSt[bo], tbb], W=[to])
                                else:
                                    P.op("dve", "reciprocal", out=rr[0:1, :], in_=PS[bo][0:1, :], R=[PSt[bo]], W=[trr])

                                    def f2(rr=rr, trr=trr):
                                        P.op("pe", "matmul", PS[5][:, :], lhsT=ones_f[0:1, :], rhs=rr[0:1, :],
                                             start=True, stop=True, R=[trr] + CT, W=[PSt[5]])

                                    def f3(bb=bb, tbb=tbb, o=o, to=to, bo=bo, j=j, qs=qs):
                                        P.op("act", "activation", out=bb[64:128, :], in_=PS[5][64:128, :], func=AF.Copy,
                                             R=[PSt[5]], W=[tbb])
                                        P.op("dve", "tensor_tensor", out=o[64:128, :], in0=PS[bo][64:128, :], in1=bb[64:128, :],
                                             op=ALU.mult, R=[PSt[bo], tbb], W=[to])
                                        P.dma("sp", mix_d[:, 4 + j, qs], o[:], R=[to], W=[])
                                pending.append((s_ + 3, f2))
                                pending.append((s_ + 6, f3))
                            while pending and pending[0][0] <= s_:
                                pending.pop(0)[1]()
                        for _, fn in pending:
                            fn()
                        P.barrier()
                        P.emit()
                with ExitStack() as st:
                    NG = 64
                    prm = {}
                    tprm = Tok()

                    def ptile(name, w=NG):
                        prm[name] = sbuf(st, "s5p_" + name, [128, w])
                        return prm[name]
                    lre, lim, ldt = ptile("lre"), ptile("lim"), ptile("ldt")
                    P.dma("sp", lre[:], lamre_d[e], W=[tprm])
                    P.dma("sp", lim[:], lamim_d[e], W=[tprm])
                    P.dma("sp", ldt[:], logdt_d[e], W=[tprm])

                    def dv(meth, **kw):
                        P.op("dve", meth, R=[tprm] + CT, W=[tprm], **kw)

                    def ac(**kw):
                        P.op("act", "activation", R=[tprm] + CT, W=[tprm], **kw)
                    lr, dt, x1, mag, th = ptile("lr"), ptile("dt"), ptile("x1"), ptile("mag"), ptile("th")
                    dv("tensor_scalar", out=lr[:], in0=lre[:], scalar1=-1e-4, scalar2=None, op0=ALU.min)
                    ac(out=dt[:], in_=ldt[:], func=AF.Exp)
                    dv("tensor_tensor", out=x1[:], in0=lr[:], in1=dt[:], op=ALU.mult)
                    ac(out=mag[:], in_=x1[:], func=AF.Exp)
                    dv("tensor_tensor", out=th[:], in0=lim[:], in1=dt[:], op=ALU.mult)
                    ti = sbuf(st, "s5p_ti", [128, NG], I32)
                    tf, red = ptile("tf"), ptile("red")

                    def sin_of(dst, src, shift):
                        dv("tensor_scalar", out=tf[:], in0=src[:], scalar1=shift, scalar2=1.0 / (2 * PI), op0=ALU.add, op1=ALU.mult)
                        dv("tensor_copy", out=ti[:], in_=tf[:])
                        dv("tensor_copy", out=tf[:], in_=ti[:])
                        dv("tensor_scalar", out=red[:], in0=src[:], scalar1=shift, scalar2=None, op0=ALU.add)
                        dv("scalar_tensor_tensor", out=red[:], in0=tf[:], scalar=-2 * PI, in1=red[:], op0=ALU.mult, op1=ALU.add)
                        dv("tensor_scalar", out=red[:], in0=red[:], scalar1=PI, scalar2=-PI, op0=ALU.min, op1=ALU.max)
                        ac(out=dst[:], in_=red[:], func=AF.Sin)
                    sn, cs_ = ptile("sn"), ptile("cs")
                    sin_of(sn, th, 0.0)
                    sin_of(cs_, th, PI / 2)
                    are, aim = ptile("are"), ptile("aim")
                    dv("tensor_tensor", out=are[:], in0=mag[:], in1=cs_[:], op=ALU.mult)
                    dv("tensor_tensor", out=aim[:], in0=mag[:], in1=sn[:], op=ALU.mult)
                    den, t_a, t_b = ptile("den"), ptile("ta"), ptile("tb")
                    dv("tensor_tensor", out=den[:], in0=lr[:], in1=lr[:], op=ALU.mult)
                    dv("tensor_tensor", out=t_a[:], in0=lim[:], in1=lim[:], op=ALU.mult)
                    dv("tensor_tensor", out=den[:], in0=den[:], in1=t_a[:], op=ALU.add)
                    dv("reciprocal", out=den[:], in_=den[:])
                    nr = ptile("nr")
                    dv("tensor_scalar", out=nr[:], in0=are[:], scalar1=-1.0, scalar2=None, op0=ALU.add)
                    fre, fim = ptile("fre"), ptile("fim")
                    dv("tensor_tensor", out=t_a[:], in0=nr[:], in1=lr[:], op=ALU.mult)
                    dv("tensor_tensor", out=t_b[:], in0=aim[:], in1=lim[:], op=ALU.mult)
                    dv("tensor_tensor", out=t_a[:], in0=t_a[:], in1=t_b[:], op=ALU.add)
                    dv("tensor_tensor", out=fre[:], in0=t_a[:], in1=den[:], op=ALU.mult)
                    dv("tensor_tensor", out=t_a[:], in0=aim[:], in1=lr[:], op=ALU.mult)
                    dv("tensor_tensor", out=t_b[:], in0=nr[:], in1=lim[:], op=ALU.mult)
                    dv("tensor_tensor", out=t_a[:], in0=t_a[:], in1=t_b[:], op=ALU.subtract)
                    dv("tensor_tensor", out=fim[:], in0=t_a[:], in1=den[:], op=ALU.mult)
                    F2, F3 = ptile("F2"), ptile("F3")
                    dv("tensor_scalar", out=F2[:], in0=fim[:], scalar1=sgn, scalar2=None, op0=ALU.mult)
                    dv("tensor_scalar", out=F3[:], in0=fre[:], scalar1=nsgn, scalar2=None, op0=ALU.mult)
                    CK = sbuf(st, "s5p_CK", [128, 12, NG])
                    SK = sbuf(st, "s5p_SK", [128, 12, NG])
                    dv("tensor_copy", out=CK[:, 0, :], in_=cs_[:])
                    dv("tensor_copy", out=SK[:, 0, :], in_=sn[:])
                    for k in range(11):
                        dv("tensor_tensor", out=t_a[:], in0=CK[:, k, :], in1=CK[:, k, :], op=ALU.mult)
                        dv("tensor_tensor", out=t_b[:], in0=SK[:, k, :], in1=SK[:, k, :], op=ALU.mult)
                        dv("tensor_tensor", out=CK[:, k + 1, :], in0=t_a[:], in1=t_b[:], op=ALU.subtract)
                        dv("tensor_tensor", out=t_a[:], in0=CK[:, k, :], in1=SK[:, k, :], op=ALU.mult)
                        dv("tensor_scalar", out=SK[:, k + 1, :], in0=t_a[:], scalar1=2.0, scalar2=None, op0=ALU.mult)
                    SKs9 = ptile("SKs9")
                    dv("tensor_scalar", out=SKs9[:], in0=SK[:, 9, :], scalar1=sgn, scalar2=None, op0=ALU.mult)
                    bbs = sbuf(st, "s5_bbs", [128, NG, 16])
                    bbw = sbuf(st, "s5_bbw", [128, NG, 16])

                    def bc16(a):
                        return a[:].unsqueeze(2).to_broadcast([128, NG, 16])
                    with ExitStack() as sub:
                        b1 = sbuf(sub, "s5_b1", [128, NG, 16])
                        b2 = sbuf(sub, "s5_b2", [128, NG, 16])
                        tmpb = sbuf(sub, "s5_tmpb", [128, NG, 16])
                        P.dma("sp", b1[:], sb1_d[e].rearrange("p (g c) -> p g c", c=16), W=[tprm])
                        P.dma("sp", b2[:], sb2_d[e].rearrange("p (g c) -> p g c", c=16), W=[tprm])
                        dv("tensor_tensor", out=bbs[:], in0=b1[:], in1=bc16(fre), op=ALU.mult)
                        dv("tensor_tensor", out=tmpb[:], in0=b2[:], in1=bc16(F2), op=ALU.mult)
                        dv("tensor_tensor", out=bbs[:], in0=bbs[:], in1=tmpb[:], op=ALU.add)
                        dv("tensor_tensor", out=bbw[:], in0=b2[:], in1=bc16(F3), op=ALU.mult)
                        dv("tensor_tensor", out=tmpb[:], in0=b1[:], in1=bc16(fim), op=ALU.mult)
                        dv("tensor_tensor", out=bbw[:], in0=bbw[:], in1=tmpb[:], op=ALU.add)
                        P.barrier()
                        P.emit()
                    c1 = sbuf(st, "s5_c1", [128, NG, 16])
                    c2 = sbuf(st, "s5_c2", [128, NG, 16])
                    P.dma("sp", c1[:], sc1_d[e].rearrange("p (g c) -> p g c", c=16), W=[tprm])
                    P.dma("sp", c2[:], sc2_d[e].rearrange("p (g c) -> p g c", c=16), W=[tprm])
                    s5d = sbuf(st, "s5_d", [128, 4])
                    glub = sbuf(st, "s5_glub", [128, 4])
                    P.dma("sp", s5d[:], s5d_d[e], W=[tprm])
                    P.dma("sp", glub[:], glub_d[e], W=[tprm])
                    gluw = sbuf(st, "s5_gluw", [128, 4, 512], BF16)
                    tgw = Tok()
                    P.dma("pool", gluw[:], gluw_d[e].rearrange("(k p) n -> p k n", p=128), W=[tgw])

                    bT = sbuf(st, "s5_bT", [128, 128])
                    bTw = sbuf(st, "s5_bTw", [128, 128])
                    tbT = Tok()
                    lb = [sbuf(st, "s5_lb%d" % i, [128, 8, 128], BF16) for i in range(2)]
                    lbw = [sbuf(st, "s5_lbw%d" % i, [128, 8, 128], BF16) for i in range(2)]
                    tlb = toks(2)
                    w1p = [sbuf(st, "s5_w1p%d" % i, [128, 1152], BF16) for i in range(2)]
                    w2p = [sbuf(st, "s5_w2p%d" % i, [128, 1152], BF16) for i in range(2)]
                    twp = toks(2)
                    for i in range(2):
                        P.op("pool", "memset", w1p[i][:], 0.0, W=[twp[i]])
                        P.op("pool", "memset", w2p[i][:], 0.0, W=[twp[i]])
                    COS = [sbuf(st, "s5_COS%d" % i, [128, TW]) for i in range(2)]
                    SIN = [sbuf(st, "s5_SIN%d" % i, [128, TW]) for i in range(2)]
                    tCOS, tSIN = toks(2), toks(2)
                    tA = sbuf(st, "s5_tA", [128, TW // 2])
                    tB = sbuf(st, "s5_tB", [128, TW // 2])
                    NCR = 4
                    crt = [sbuf(st, "s5_crt%d" % i, [128, 1]) for i in range(NCR)]
                    crc = [sbuf(st, "s5_crc%d" % i, [128, 1]) for i in range(NCR)]
                    tcrt, tcrc = toks(NCR), toks(NCR)
                    ttA, ttB = toks(2)
                    yacc = sbuf(st, "s5_yacc", [128, S])
                    tya = toks(NT)
                    gl, tgl = zs5, tzs5
                    NB = 3
                    vt = [sbuf(st, "s5_v%d" % i, [128, TW], BF16) for i in range(NB)]
                    tm_ = [sbuf(st, "s5_tm%d" % i, [128, TW], BF16) for i in range(NB)]
                    gt = [sbuf(st, "s5_g%d" % i, [128, TW], BF16) for i in range(NB)]
                    C16 = [sbuf(st, "s5_C16_%d" % i, [128, TW], BF16) for i in range(2)]
                    S16 = [sbuf(st, "s5_S16_%d" % i, [128, TW], BF16) for i in range(2)]
                    tC16, tS16 = toks(2), toks(2)
                    a1 = [sbuf(st, "s5_a1%d" % i, [128, TW], BF16) for i in range(NB)]
                    a2 = [sbuf(st, "s5_a2%d" % i, [128, TW], BF16) for i in range(NB)]
                    tvt, ttm, tgt, ta1, ta2 = toks(NB), toks(NB), toks(NB), toks(NB), toks(NB)
                    BUP = [(0, 1), (2, 3)]

                    gds = [(cs, dr, gq) for cs in range(4) for dr in range(2) for gq in range(8)]
                    steps = [(gi, tp) for gi in range(len(gds)) for tp in range(NT)]
                    NS = len(steps)

                    def gd_of(gi):
                        cs, dr, gq = gds[gi]
                        return cs, dr, gq, dr * 32 + cs * 8 + gq, (cs * 2 + dr) % 2

                    def gen_piece(gi, piece):
                        if gi >= len(gds):
                            return
                        gd = gd_of(gi)[3]
                        C_, S_, tC, tS = COS[gi % 2], SIN[gi % 2], tCOS[gi % 2], tSIN[gi % 2]
                        work = {0: [(k, 0, 1 << k) for k in range(4)], 1: [(4, 0, 16), (5, 0, 32)], 2: [(6, 0, 64)],
                                3: [(7, 0, 128)], 4: [(8, 0, 256)], 5: [], 6: [], 7: []}[piece]
                        if piece == 0:
                            P.op("pool", "memset", C_[:, 0:1], 1.0, W=[tC])
                            P.op("pool", "memset", S_[:, 0:1], 0.0, W=[tS])
                        if piece == 5:
                            dr_ = gds[gi][1]
                            oc = C16[gi % 2][:, :] if dr_ == 0 else C16[gi % 2][:, ::-1]
                            os_ = S16[gi % 2][:, :] if dr_ == 0 else S16[gi % 2][:, ::-1]
                            P.op("act", "activation", out=oc, in_=C_[:, :], func=AF.Copy, R=[tC], W=[tC16[gi % 2]])
                            P.op("act", "activation", out=os_, in_=S_[:, :], func=AF.Copy, R=[tS], W=[tS16[gi % 2]])
                        for (k, c0, c1_) in work:
                            ln = 1 << k
                            n = c1_ - c0
                            ck = CK[:, k, gd:gd + 1]
                            sk = SK[:, k, gd:gd + 1]
                            P.op("act", "activation", out=C_[:, ln + c0:ln + c1_], in_=C_[:, c0:c1_], func=AF.Identity, scale=ck,
                                 R=[tC, tprm], W=[tC])
                            P.op("act", "activation", out=tA[:, 0:n], in_=S_[:, c0:c1_], func=AF.Identity, scale=sk,
                                 R=[tS, tprm], W=[ttA])
                            P.op("pool", "tensor_tensor", out=C_[:, ln + c0:ln + c1_], in0=C_[:, ln + c0:ln + c1_], in1=tA[:, 0:n],
                                 op=ALU.subtract, R=[tC, ttA], W=[tC])
                            P.op("act", "activation", out=S_[:, ln + c0:ln + c1_], in_=S_[:, c0:c1_], func=AF.Identity, scale=ck,
                                 R=[tS, tprm], W=[tS])
                            P.op("act", "activation", out=tB[:, 0:n], in_=C_[:, c0:c1_], func=AF.Identity, scale=sk,
                                 R=[tC, tprm], W=[ttB])
                            P.op("pool", "tensor_tensor", out=S_[:, ln + c0:ln + c1_], in0=S_[:, ln + c0:ln + c1_], in1=tB[:, 0:n],
                                 op=ALU.add, R=[tS, ttB], W=[tS])

                    def prep_group(cs, dr):
                        ws = (cs * 2 + dr) % 2
                        g0 = dr * 32 + cs * 8
                        P.op("pe", "transpose", out=PS[4][:, 0:128], in_=bbs[:, g0:g0 + 8, :].rearrange("p g c -> p (g c)"),
                             identity=ident, R=[tprm] + CT, W=[PSt[4]])
                        P.op("pe", "transpose", out=PS[4][:, 128:256], in_=bbw[:, g0:g0 + 8, :].rearrange("p g c -> p (g c)"),
                             identity=ident, R=[tprm] + CT, W=[PSt[4]])
                        P.op("act", "activation", out=bT[:], in_=PS[4][:, 0:128], func=AF.Copy, R=[PSt[4]], W=[tbT])
                        P.op("act", "activation", out=bTw[:], in_=PS[4][:, 128:256], func=AF.Copy, R=[PSt[4]], W=[tbT])
                        for gq in range(8):
                            P.op("dve", "tensor_scalar", out=lb[ws][:, gq, :], in0=bT[:], scalar1=gmask[:, gq:gq + 1], scalar2=None,
                                 op0=ALU.mult, R=[tbT] + CT, W=[tlb[ws]])
                            P.op("dve", "tensor_scalar", out=lbw[ws][:, gq, :], in0=bTw[:], scalar1=gmask[:, gq:gq + 1], scalar2=None,
                                 op0=ALU.mult, R=[tbT] + CT, W=[tlb[ws]])
                        P.op("dve", "tensor_scalar", out=w1p[ws][:].rearrange("p (g w) -> p g w", w=144)[:, :, 0:16],
                             in0=c1[:, g0:g0 + 8, :], scalar1=nsgn, scalar2=None, op0=ALU.mult, R=[tprm] + CT, W=[twp[ws]])
                        P.op("dve", "tensor_scalar", out=w2p[ws][:].rearrange("p (g w) -> p g w", w=144)[:, :, 0:16],
                             in0=c2[:, g0:g0 + 8, :], scalar1=-1.0, scalar2=None, op0=ALU.mult, R=[tprm] + CT, W=[twp[ws]])

                    def tile_of(s_):
                        gi, tp = steps[s_]
                        cs, dr, gq, gd, ws = gd_of(gi)
                        tt = tp if dr == 0 else NT - 1 - tp
                        return gi, tp, cs, dr, gq, gd, ws, tt

                    def tabs(s_):
                        gi, tp, cs, dr, gq, gd, ws, tt = tile_of(s_)
                        if dr == 0:
                            return COS[gi % 2][:, :], SIN[gi % 2][:, :]
                        return COS[gi % 2][:, ::-1], SIN[gi % 2][:, ::-1]

                    def st_bu(s_):
                        if s_ >= NS:
                            return
                        gi, tp, cs, dr, gq, gd, ws, tt = tile_of(s_)
                        if tp == 0 and gq == 0:
                            prep_group(cs, dr)
                        b1_, b2_ = BUP[s_ % 2]
                        ts_ = slice(tt * TW, (tt + 1) * TW)
                        P.op("pe", "matmul", PS[b1_][:, :], lhsT=lb[ws][:, gq, :], rhs=zs5[cs][:, ts_], start=True, stop=True,
                             R=[tlb[ws], tzs5[cs][tt]], W=[PSt[b1_]])
                        P.op("pe", "matmul", PS[b2_][:, :], lhsT=lbw[ws][:, gq, :], rhs=zs5[cs][:, ts_], start=True, stop=True,
                             R=[tlb[ws], tzs5[cs][tt]], W=[PSt[b2_]])

                    def st_a(s_):
                        if s_ >= NS:
                            return
                        gi = steps[s_][0]
                        b1_, b2_ = BUP[s_ % 2]
                        i = s_ % NB
                        Cs, Ss = tabs(s_)
                        P.op("dve", "tensor_tensor", out=tm_[i][:], in0=PS[b1_][:, :], in1=Cs, op=ALU.mult,
                             R=[PSt[b1_], tCOS[gi % 2]], W=[ttm[i]])
                        P.op("dve", "tensor_tensor", out=vt[i][:], in0=PS[b2_][:, :], in1=Ss, op=ALU.mult,
                             R=[PSt[b2_], tSIN[gi % 2]], W=[tvt[i]])
                        P.op("dve", "tensor_tensor", out=vt[i][:], in0=vt[i][:], in1=tm_[i][:], op=ALU.add,
                             R=[ttm[i], tvt[i]], W=[tvt[i]])

                    def st_swap(s_):
                        gi, tp, cs, dr, gq, gd, ws, tt = tile_of(s_)
                        if tp == NT - 1:
                            return
                        i = s_ % NB
                        col = 2 * (s_ % 128)
                        src = gt[i][:, TW - 2:TW] if dr == 0 else gt[i][:, 0:2]
                        P.op("pe", "matmul", PS[6][:, col:col + 2], lhsT=swapm_b[:], rhs=src, start=True, stop=True,
                             R=[tgt[i]] + CT, W=[PSt[6]])

                    def st_carry(s_):
                        gi, tp, cs, dr, gq, gd, ws, tt = tile_of(s_)
                        if tp == 0:
                            return
                        pv = (s_ - 1) % NB
                        cc = s_ % NCR
                        col = 2 * ((s_ - 1) % 128) + (1 if dr == 0 else 0)
                        glast = gt[pv][:, TW - 1:TW] if dr == 0 else gt[pv][:, 0:1]
                        P.op("dve", "tensor_scalar", out=crt[cc][:], in0=PS[6][:, col:col + 1], scalar1=SKs9[:, gd:gd + 1], scalar2=None,
                             op0=ALU.mult, R=[PSt[6], tprm], W=[tcrt[cc]])
                        P.op("dve", "scalar_tensor_tensor", out=crc[cc][:], in0=glast, scalar=CK[:, 9, gd:gd + 1], in1=crt[cc][:],
                             op0=ALU.mult, op1=ALU.add, R=[tgt[pv], tcrt[cc], tprm], W=[tcrc[cc]])

                    def st_b(s_):
                        gi, tp, cs, dr, gq, gd, ws, tt = tile_of(s_)
                        i = s_ % NB
                        cc = s_ % NCR
                        mcol = mag[:, gd:gd + 1].to_broadcast([128, TW])
                        extra = [tcrc[cc]] if tp > 0 else []
                        init = 0.0 if tp == 0 else crc[cc][:, 0:1]
                        if dr == 0:
                            P.op("dve", "tensor_tensor_scan", out=gt[i][:], data0=mcol, data1=vt[i][:], initial=init,
                                 op0=ALU.mult, op1=ALU.add, R=[tvt[i], tprm] + extra, W=[tgt[i]])
                        else:
                            P.op("dve", "tensor_tensor_scan", out=gt[i][:, ::-1], data0=mcol, data1=vt[i][:, ::-1], initial=init,
                                 op0=ALU.mult, op1=ALU.add, R=[tvt[i], tprm] + extra, W=[tgt[i]])

                    def st_c(s_):
                        gi, tp, cs, dr, gq, gd, ws, tt = tile_of(s_)
                        i = s_ % NB
                        by = 4 + (s_ % 2)
                        P.op("dve", "tensor_tensor", out=a1[i][:], in0=gt[i][:], in1=C16[gi % 2][:], op=ALU.mult,
                             R=[tgt[i], tC16[gi % 2]], W=[ta1[i]])
                        P.op("dve", "tensor_tensor", out=a2[i][:], in0=gt[i][:], in1=S16[gi % 2][:], op=ALU.mult,
                             R=[tgt[i], tS16[gi % 2]], W=[ta2[i]])
                        P.op("pe", "matmul", PS[by][:, :], lhsT=w1p[ws][:, gq * 128:(gq + 1) * 128], rhs=a1[i][:],
                             start=True, stop=False, R=[twp[ws], ta1[i]], W=[PSt[by]])
                        P.op("pe", "matmul", PS[by][:, :], lhsT=w2p[ws][:, gq * 128:(gq + 1) * 128], rhs=a2[i][:],
                             start=False, stop=True, R=[twp[ws], ta2[i]], W=[PSt[by]])

                    def st_acc(s_):
                        if s_ < 0:
                            return
                        gi, tp, cs, dr, gq, gd, ws, tt = tile_of(s_)
                        by = 4 + (s_ % 2)
                        ts_ = slice(tt * TW, (tt + 1) * TW)
                        P.op("dve", "tensor_tensor", out=yacc[:, ts_], in0=PS[by][:, :], in1=yacc[:, ts_], op=ALU.add,
                             R=[PSt[by], tya[tt]], W=[tya[tt]])
                        if s_ % 128 == 127:
                            for t2 in range(NT):
                                t2s = slice(t2 * TW, (t2 + 1) * TW)
                                P.op("act", "activation", out=gl[cs][:, t2s], in_=yacc[:, t2s], func=AF.Gelu_apprx_tanh,
                                     R=[tya[t2]], W=[tgl[cs][t2]])
                            if cs + 1 < 4:
                                y_init(cs + 1)

                    def y_init(cs):
                        for t2 in range(NT):
                            t2s = slice(t2 * TW, (t2 + 1) * TW)
                            P.op("act", "activation", out=yacc[:, t2s], in_=zs5[cs][:, t2s], func=AF.Identity, scale=s5d[:, cs:cs + 1],
                                 R=[tzs5[cs][t2], tprm], W=[tya[t2]])

                    for pc in range(8):
                        gen_piece(0, pc)
                    gen_piece(1, 0)
                    y_init(0)
                    st_bu(0)
                    st_bu(1)
                    st_a(0)
                    for s_ in range(NS):
                        gi_, tp_ = steps[s_]
                        st_bu(s_ + 2)
                        st_a(s_ + 1)
                        st_carry(s_)
                        st_b(s_)
                        st_swap(s_)
                        st_c(s_)
                        st_acc(s_ - 1)
                        if tp_ < NT - 1:
                            gen_piece(gi_ + 1, tp_ + 1)
                        else:
                            gen_piece(gi_ + 2, 0)
                    st_acc(NS - 1)
                    sg = [sbuf(st, "s5_sg%d" % i, [128, TW]) for i in range(2)]
                    ys = [sbuf(st, "s5_ys%d" % i, [128, TW], BF16) for i in range(2)]
                    tsg, tys = toks(2), toks(2)
                    n_ = 0
                    for tt in range(NT):
                        ts_ = slice(tt * TW, (tt + 1) * TW)
                        for mo in range(4):
                            b = 6 + (n_ % 2)
                            i = n_ % 2
                            n_ += 1
                            for k in range(4):
                                P.op("pe", "matmul", PS[b][:, :], lhsT=gluw[:, k, mo * 128:(mo + 1) * 128], rhs=gl[k][:, ts_],
                                     start=(k == 0), stop=(k == 3), R=[tgw, tgl[k][tt]], W=[PSt[b]])
                            P.op("act", "activation", out=sg[i][:], in_=PS[b][:, :], func=AF.Sigmoid, bias=glub[:, mo:mo + 1],
                                 R=[PSt[b], tprm], W=[tsg[i]])
                            P.op("dve", "tensor_tensor", out=ys[i][:], in0=sg[i][:], in1=gl[mo][:, ts_], op=ALU.mult,
                                 R=[tsg[i], tgl[mo][tt]], W=[tys[i]])
                            P.dma("sp", mix_d[:, mo, ts_], ys[i][:], R=[tys[i]], W=[])
                    P.barrier()
                    P.emit()
            phase_tail(l, hout_d[e])

        def phase_odd(l):
            o = l // 2
            with ExitStack() as lst:
                XW = S + 4
                xr = [sbuf(lst, "xr%d" % i, [128, XW], BF16) for i in range(8)]
                txr = [Tok() for _ in range(8)]
                with ExitStack() as st:
                    win = sbuf(st, "rwin", [128, 8, 2048], BF16)
                    t_win = Tok()
                    wsrc = rin_d[o].rearrange("(k p) n -> p k n", p=128)
                    for q4 in range(4):
                        P.dma("pool", win[:, :, q4 * 512:(q4 + 1) * 512], wsrc[:, :, q4 * 512:(q4 + 1) * 512], W=[t_win])
                    for c in range(8):
                        P.op("pool", "memset", xr[c][:, 0:2], 0.0, W=[txr[c]])
                        P.op("pool", "memset", xr[c][:, S + 2:S + 4], 0.0, W=[txr[c]])
                    hh = sbuf(st, "hh", [128, 8, TW])
                    thh = Tok()
                    u = sbuf(st, "u", [128, 8, TW], BF16)
                    tu = Tok()
                    gs = [sbuf(st, "gs%d" % i, [128, TW], BF16) for i in range(3)]
                    tgs = toks(3)
                    nsc = norm_scratch(st, TW)
                    n_ = 0
                    for tt in range(NT):
                        ts_ = slice(tt * TW, (tt + 1) * TW)
                        P.dma("sp", hh[:], h_d[:, :, ts_], W=[thh])
                        norm_tile(nsc, hh, thh, TW, u, tu, lambda k: A1[:, l * 8 + k:l * 8 + k + 1],
                                  lambda k: modcol(l, 0, k), PS[0], PSt[0])
                        for c in range(16):
                            b = 1 + (c % 4)
                            for k in range(8):
                                P.op("pe", "matmul", PS[b][:, :], lhsT=win[:, k, c * 128:(c + 1) * 128], rhs=u[:, k, :],
                                     start=(k == 0), stop=(k == 7), R=[t_win, tu], W=[PSt[b]])
                            if c < 8:
                                i = n_ % 3
                                n_ += 1
                                P.op("act", "activation", out=gs[i][:], in_=PS[b][:, :], func=AF.Copy, R=[PSt[b]], W=[tgs[i]])
                                P.dma("sp", gate_d[:, c, ts_], gs[i][:], R=[tgs[i]], W=[])
                            else:
                                P.op("dve", "tensor_copy", out=xr[c - 8][:, 2 + tt * TW:2 + (tt + 1) * TW], in_=PS[b][:, :],
                                     R=[PSt[b]], W=[txr[c - 8]])
                    P.barrier()
                    P.emit()
                import os as _os
                with ExitStack() as st:
                    _nchunk = int(_os.environ.get("DBG_O2_CHUNKS", "8"))
                    cw = sbuf(st, "cw", [128, 32])
                    cbias = sbuf(st, "cbias", [128, 8])
                    rab = sbuf(st, "rab", [128, 16])
                    ixb = sbuf(st, "ixb", [128, 16])
                    lam = sbuf(st, "lam", [128, 16])
                    nsp = sbuf(st, "nsp", [128, 16])
                    nsp2 = sbuf(st, "nsp2", [128, 16])
                    tp_ = Tok()
                    for (dst, src) in ((cw, convw_d[o]), (cbias, convb_d[o]), (rab, rab_d[o]), (ixb, ixb_d[o]), (lam, rlam_d[o])):
                        P.dma("sp", dst[:], src, W=[tp_])
                    rabd = sbuf(st, "rabd", [128, 16, 128], BF16)
                    ixbd = sbuf(st, "ixbd", [128, 16, 128], BF16)
                    for hf in range(2):
                        P.dma("pool", rabd[:, hf * 8:(hf + 1) * 8, :],
                              rabd_d[o].rearrange("p (c m) -> p c m", m=128)[:, hf * 8:(hf + 1) * 8, :], W=[tp_])
                        P.dma("pool", ixbd[:, hf * 8:(hf + 1) * 8, :],
                              ixbd_d[o].rearrange("p (c m) -> p c m", m=128)[:, hf * 8:(hf + 1) * 8, :], W=[tp_])
                    P.op("act", "activation", out=nsp[:], in_=lam[:], func=AF.Exp, scale=-1.0, R=[tp_], W=[tp_])
                    P.op("act", "activation", out=nsp[:], in_=nsp[:], func=AF.Ln, bias=one_c[:], R=[tp_] + CT, W=[tp_])
                    P.op("dve", "tensor_scalar", out=nsp2[:], in0=nsp[:], scalar1=-16.0, scalar2=None, op0=ALU.mult, R=[tp_], W=[tp_])
                    P.op("dve", "tensor_scalar", out=nsp[:], in0=nsp[:], scalar1=-8.0, scalar2=None, op0=ALU.mult, R=[tp_], W=[tp_])
                    xc = sbuf(st, "xc", [128, S])
                    xcb = sbuf(st, "xcb", [128, S], BF16)
                    Rb = sbuf(st, "Rb", [128, S])
                    Ib = sbuf(st, "Ib", [128, S])
                    Ab = sbuf(st, "Ab", [128, S])
                    Yb = sbuf(st, "Yb", [128, S])
                    txc, txcb = Tok(), toks(NT)
                    tR, tI, tA_, tY = toks(NT), toks(NT), toks(NT), toks(NT)
                    gtile = [sbuf(st, "gt%d" % i, [128, S], BF16) for i in range(2)]
                    tgt_ = toks(2)
                    ot = [sbuf(st, "ot%d" % i, [128, TW], BF16) for i in range(2)]
                    tot = toks(2)
                    gg = [sbuf(st, "gg%d" % i, [128, TW]) for i in range(2)]
                    tgg = toks(2)
                    n_ = 0
                    for c in range(_nchunk):
                        P.dma("sp", gtile[c % 2][:], gate_d[:, c, :], W=[tgt_[c % 2]])
                        P.op("dve", "tensor_scalar", out=xc[:], in0=xr[c][:, 0:S], scalar1=cw[:, c * 4:c * 4 + 1],
                             scalar2=cbias[:, c:c + 1], op0=ALU.mult, op1=ALU.add, R=[txr[c], tp_], W=[txc])
                        for j in range(1, 4):
                            P.op("dve", "scalar_tensor_tensor", out=xc[:], in0=xr[c][:, j:j + S], scalar=cw[:, c * 4 + j:c * 4 + j + 1],
                                 in1=xc[:], op0=ALU.mult, op1=ALU.add, R=[txr[c], tp_, txc], W=[txc])
                        for tt in range(NT):
                            ts_ = slice(tt * TW, (tt + 1) * TW)
                            P.op("act", "activation", out=xcb[:, ts_], in_=xc[:, ts_], func=AF.Copy, R=[txc], W=[txcb[tt]])
                        for dr in range(2):
                            dc = dr * 8 + c
                            for tt in range(NT):
                                ts_ = slice(tt * TW, (tt + 1) * TW)
                                b = 1 + (n_ % 3)
                                b2_ = 4 + (n_ % 3)
                                n_ += 1
                                P.op("pe", "matmul", PS[b][:, :], lhsT=rabd[:, dc, :], rhs=xcb[:, ts_], start=True, stop=True,
                                     R=[tp_, txcb[tt]], W=[PSt[b]])
                                P.op("pe", "matmul", PS[b2_][:, :], lhsT=ixbd[:, dc, :], rhs=xcb[:, ts_], start=True, stop=True,
                                     R=[tp_, txcb[tt]], W=[PSt[b2_]])
                                P.op("act", "activation", out=Rb[:, ts_], in_=PS[b][:, :], func=AF.Sigmoid, bias=rab[:, dc:dc + 1],
                                     R=[PSt[b], tp_], W=[tR[tt]])
                                P.op("act", "activation", out=Ib[:, ts_], in_=PS[b2_][:, :], func=AF.Sigmoid, bias=ixb[:, dc:dc + 1],
                                     R=[PSt[b2_], tp_], W=[tI[tt]])
                            for tt in range(NT):
                                ts_ = slice(tt * TW, (tt + 1) * TW)
                                P.op("act", "activation", out=Ab[:, ts_], in_=Rb[:, ts_], func=AF.Exp, scale=nsp[:, dc:dc + 1],
                                     R=[tR[tt], tp_], W=[tA_[tt]])
                                P.op("act", "activation", out=Rb[:, ts_], in_=Rb[:, ts_], func=AF.Exp, scale=nsp2[:, dc:dc + 1],
                                     R=[tR[tt], tp_], W=[tR[tt]])
                            for tt in range(NT):
                                ts_ = slice(tt * TW, (tt + 1) * TW)
                                P.op("act", "activation", out=Rb[:, ts_], in_=Rb[:, ts_], func=AF.Sqrt, scale=-1.0, bias=one_c[:],
                                     R=[tR[tt]] + CT, W=[tR[tt]])
                                P.op("pool", "tensor_tensor", out=Ib[:, ts_], in0=Ib[:, ts_], in1=xc[:, ts_], op=ALU.mult,
                                     R=[tI[tt], txc], W=[tI[tt]])
                                P.op("dve", "tensor_tensor", out=Ib[:, ts_], in0=Ib[:, ts_], in1=Rb[:, ts_], op=ALU.mult,
                                     R=[tI[tt], tR[tt]], W=[tI[tt]])
                            order = range(NT) if dr == 0 else range(NT - 1, -1, -1)
                            prev = None
                            dst = Yb if dr == 0 else Rb
                            tdst = tY if dr == 0 else tR
                            for tt in order:
                                ts_ = slice(tt * TW, (tt + 1) * TW)
                                if dr == 0:
                                    init = 0.0 if prev is None else dst[:, prev * TW + TW - 1:prev * TW + TW]
                                    P.op("dve", "tensor_tensor_scan", out=dst[:, ts_], data0=Ab[:, ts_], data1=Ib[:, ts_], initial=init,
                                         op0=ALU.mult, op1=ALU.add,
                                         R=[tA_[tt], tI[tt]] + ([tdst[prev]] if prev is not None else []), W=[tdst[tt]])
                                else:
                                    init = 0.0 if prev is None else dst[:, prev * TW:prev * TW + 1]
                                    P.op("dve", "tensor_tensor_scan", out=dst[:, ts_][:, ::-1], data0=Ab[:, ts_][:, ::-1],
                                         data1=Ib[:, ts_][:, ::-1], initial=init, op0=ALU.mult, op1=ALU.add,
                                         R=[tA_[tt], tI[tt]] + ([tdst[prev]] if prev is not None else []), W=[tdst[tt]])
                                prev = tt
                        for tt in range(NT):
                            ts_ = slice(tt * TW, (tt + 1) * TW)
                            i = tt % 2
                            P.op("act", "activation", out=gg[i][:], in_=gtile[c % 2][:, ts_], func=AF.Gelu_apprx_tanh,
                                 R=[tgt_[c % 2]], W=[tgg[i]])
                            P.op("pool", "tensor_tensor", out=Yb[:, ts_], in0=Yb[:, ts_], in1=Rb[:, ts_], op=ALU.add,
                                 R=[tY[tt], tR[tt]], W=[tY[tt]])
                            P.op("dve", "tensor_tensor", out=ot[i][:], in0=Yb[:, ts_], in1=gg[i][:], op=ALU.mult,
                                 R=[tY[tt], tgg[i]], W=[tot[i]])
                            P.dma("sp", mix_d[:, c, ts_], ot[i][:], R=[tot[i]], W=[])
                    P.barrier()
                    P.emit()
            phase_tail(l, rout_d[o])

        for l in range(n_layers):
            if l % 2 == 0:
                phase_even(l)
            else:
                phase_odd(l)

        with ExitStack() as st:
            hh = [sbuf(st, "fh%d" % i, [128, 8, TW]) for i in range(2)]
            thh = toks(2)
            hn = sbuf(st, "fhn", [128, 8, TW])
            thn = Tok()
            ob = [sbuf(st, "fo%d" % i, [128, D]) for i in range(2)]
            tob = toks(2)
            nsc = norm_scratch(st, TW)
            tfin = []
            n_ = 0
            for tt in range(NT):
                ts_ = slice(tt * TW, (tt + 1) * TW)
                h, th = hh[tt % 2], thh[tt % 2]
                P.dma("sp", h[:], h_d[:, :, ts_], W=[th])
                norm_tile(nsc, h, th, TW, hn, thn, lambda k: fnw[:, k:k + 1], None, PS[0], PSt[0])
                for j in range(4):
                    o_, to_ = ob[n_ % 2], tob[n_ % 2]
                    for b in range(2):
                        bank = 1 + 2 * (n_ % 2) + b
                        for kk in range(4):
                            k = 4 * b + kk
                            P.op("pe", "transpose", out=PS[bank][:, kk * 128:(kk + 1) * 128], in_=hn[:, k, j * 128:(j + 1) * 128],
                                 identity=ident, R=[thn] + CT, W=[PSt[bank]])
                        if b == 0:
                            P.op("act", "activation", out=o_[:, 0:512], in_=PS[bank][:, :], func=AF.Copy, R=[PSt[bank]], W=[to_])
                        else:
                            P.op("dve", "tensor_copy", out=o_[:, 512:1024], in_=PS[bank][:, :], R=[PSt[bank]], W=[to_])
                    n_ += 1
                    tk = Tok()
                    r0 = tt * TW + j * 128
                    P.dma("sp", out_d[r0:r0 + 128, :], o_[:], R=[to_], W=[tk])
                    tfin.append(tk)
            for tk in tfin:
                P._wait("sp", tk.w[0], tk.w[1])
            P.barrier()
            P.emit()
        print("[kernel] program built: %d instructions" % P.ninst)
    return nc


def _fm(v, k):
    v = np.asarray(v, np.float32)
    lead = v.shape[:-1]
    v = v.reshape(lead + (k, 128))
    v = np.moveaxis(v, -1, 0)
    return np.ascontiguousarray(v)


def _rope_tables():
    rows = S // 64
    row_idx = np.repeat(np.arange(rows, dtype=np.float64), 64)
    col_idx = np.tile(np.arange(64, dtype=np.float64), rows)
    inv_freq = 10000.0 ** (-np.arange(16, dtype=np.float64) / 16)
    cos = np.zeros((128, S), np.float64)
    sin = np.zeros((128, S), np.float64)
    for p in range(128):
        d = p % 64
        a = d // 32
        b = (d // 16) % 2
        f = d % 16
        ang = (row_idx if a == 0 else col_idx) * inv_freq[f]
        cos[p] = np.cos(ang)
        sin[p] = (-np.sin(ang)) if b == 0 else np.sin(ang)
    return cos.astype(np.float32), sin.astype(np.float32)


def _consts():
    c = np.zeros((128, 400), np.float32)
    c[:, 0:128] = np.eye(128, dtype=np.float32)
    c[0:64, 128:192] = 1.0
    c[64:128, 192:256] = 1.0
    for k in range(128):
        c[k, 256 + (k + 64) % 128] = 1.0
    c[0:64, 384] = -1.0
    c[64:128, 384] = 1.0
    c[0:64, 385] = 1.0
    c[64:128, 385] = -1.0
    for g in range(8):
        c[16 * g:16 * g + 16, 386 + g] = 1.0
    return c


def prepare_shared(inp):
    f = lambda a: np.ascontiguousarray(np.asarray(a, np.float32))
    sh = {}
    sh["ada_w"] = f(inp["ada_w"])
    sh["ada_b_fm"] = np.ascontiguousarray(f(inp["ada_b"]).reshape(4, 48, 128).transpose(2, 0, 1).reshape(128, 192))
    sh["norm_w_fm"] = np.ascontiguousarray(f(inp["norm_w"]).reshape(4, 2, 8, 128).transpose(3, 0, 1, 2).reshape(128, 64))
    sh["fnw_fm"] = np.ascontiguousarray(f(inp["final_norm_w"]).reshape(8, 128).T)
    sh["mlp_w1"] = f(inp["mlp_w1"])
    sh["mlp_w2"] = f(inp["mlp_w2"])
    idx = np.arange(128)
    perm = idx ^ 16
    cols = list(range(512))
    for j in range(4):
        cols += list(512 + 128 * j + idx)
    for j in range(4):
        cols += list(512 + 128 * j + perm)
    for kv in range(2):
        cols += list(1024 + 64 * kv + (idx % 64))
    for kv in range(2):
        cols += list(1024 + 64 * kv + (perm % 64))
    cols += list(range(1152, 1280))
    cols = np.asarray(cols)
    assert cols.shape[0] == 2176
    sh["hyb_w_in_ext"] = np.ascontiguousarray(f(inp["hyb_w_in"])[:, :, cols])
    sh["hyb_w_out"] = f(inp["hyb_w_out"])

    def stk(a):
        a = f(a).transpose(0, 3, 1, 2).reshape(2, 64, 64)
        return np.ascontiguousarray(np.concatenate([a, a], axis=1))
    sh["s5_lamre"] = stk(inp["s5_lam_re"])
    sh["s5_lamim"] = stk(inp["s5_lam_im"])
    sh["s5_logdt"] = np.ascontiguousarray(np.broadcast_to(f(inp["s5_log_dt"]).reshape(2, 1, 64), (2, 128, 64)))
    bre = f(inp["s5_b_re"]).transpose(0, 3, 1, 2, 4).reshape(2, 64, 1024)
    bim = f(inp["s5_b_im"]).transpose(0, 3, 1, 2, 4).reshape(2, 64, 1024)
    sh["s5_b1"] = np.ascontiguousarray(np.concatenate([bre, bim], axis=1))
    sh["s5_b2"] = np.ascontiguousarray(np.concatenate([bim, bre], axis=1))
    cre = f(inp["s5_c_re"]).transpose(0, 4, 1, 2, 3).reshape(2, 64, 1024)
    cim = f(inp["s5_c_im"]).transpose(0, 4, 1, 2, 3).reshape(2, 64, 1024)
    sh["s5_c1"] = np.ascontiguousarray(np.concatenate([cre, cim], axis=1))
    sh["s5_c2"] = np.ascontiguousarray(np.concatenate([cim, cre], axis=1))
    sh["s5_d_fm"] = np.ascontiguousarray(f(inp["s5_d"]).reshape(2, 4, 128).transpose(0, 2, 1))
    sh["s5_glu_w"] = f(inp["s5_glu_w"])
    sh["s5_glu_b_fm"] = np.ascontiguousarray(f(inp["s5_glu_b"]).reshape(2, 4, 128).transpose(0, 2, 1))
    qn = f(inp["attn_q_norm"])
    kn = f(inp["attn_k_norm"])
    d = idx % 64
    dp = (idx ^ 16) % 64
    sh["qkn_fm"] = np.ascontiguousarray(np.stack([qn[:, d], qn[:, dp], kn[:, d], kn[:, dp]], axis=-1))
    sh["rec_w_in"] = f(inp["rec_w_in"])
    sh["rec_w_out"] = f(inp["rec_w_out"])
    sh["rec_conv_fm"] = np.ascontiguousarray(f(inp["rec_conv_w"]).reshape(2, 4, 8, 128).transpose(0, 3, 2, 1).reshape(2, 128, 32))
    sh["rec_convb_fm"] = np.ascontiguousarray(f(inp["rec_conv_b"]).reshape(2, 8, 128).transpose(0, 2, 1))

    def bd(w):
        w = f(w)
        o = np.zeros((2, 128, 2, 8, 128), np.float32)
        for c in range(8):
            for hl in range(2):
                o[:, 64 * hl:64 * hl + 64, :, c, 64 * hl:64 * hl + 64] = w[:, :, 2 * c + hl].transpose(0, 2, 1, 3)
        return np.ascontiguousarray(o.reshape(2, 128, 2048))
    sh["rec_ra_bd"] = bd(inp["rec_ra_w"])
    sh["rec_ix_bd"] = bd(inp["rec_ix_w"])

    def fm2(a):
        return np.ascontiguousarray(f(a).reshape(2, 2, 8, 128).transpose(0, 3, 1, 2).reshape(2, 128, 16))
    sh["rec_rab_fm"] = fm2(inp["rec_ra_b"])
    sh["rec_ixb_fm"] = fm2(inp["rec_ix_b"])
    sh["rec_lam_fm"] = fm2(inp["rec_lam"])
    rc, rs = _rope_tables()
    sh["rope_cos"] = rc
    sh["rope_sin"] = rs
    sh["consts"] = _consts()
    return sh


_CACHE = {}


def run(inputs, core_batches, n_layers=4, trace=False):
    key = n_layers
    if key not in _CACHE:
        _CACHE[key] = build_program(n_layers)
    nc = _CACHE[key]
    sh = prepare_shared(inputs)
    x = np.asarray(inputs["x"], np.float32)
    c = np.asarray(inputs["c"], np.float32)
    in_maps = []
    for b in core_batches:
        m = dict(sh)
        m["x"] = np.ascontiguousarray(x[b])
        m["c_fm"] = np.ascontiguousarray(c[b].reshape(8, 128).T)
        in_maps.append(m)
    res = run_bass_kernel_spmd(nc, in_maps, core_ids=list(range(len(core_batches))), **({"trace": True} if trace else {}))
    outs = np.stack([np.asarray(r["out"], np.float32) for r in res.results], axis=0)
    return outs, res


def kernel(**inputs):
    outs, _ = run(inputs, list(range(8)), 4)
    return outs.astype(np.float32)
```

```python
import math
import numpy as np
from contextlib import ExitStack
import concourse.bass as bass
import concourse.mybir as mybir
from concourse.bass_utils import run_bass_kernel_spmd

F32 = mybir.dt.float32
BF16 = mybir.dt.bfloat16
I32 = mybir.dt.int32
AF = mybir.ActivationFunctionType
ALU = mybir.AluOpType

S = 4096
D = 1024
EPS = 1e-6
NT = 8
TW = 512
PI = math.pi


class Tok:
    __slots__ = ("w", "r", "x")

    def __init__(self, x=False):
        self.w = None
        self.r = {}
        self.x = x


def toks(n):
    return [Tok() for _ in range(n)]


class Prog:
    ENGS = ("pe", "act", "dve", "pool", "sp")

    def __init__(self, nc, n_dma_sems=12):
        self.nc = nc
        self.ops = {e: [] for e in self.ENGS}
        self.cnt = {}
        self.sems = {}
        self.waited = {e: {} for e in self.ENGS}
        self.n_dma_sems = n_dma_sems
        self.dma_rr = {}
        self.ninst = 0

    def alloc_sems(self, stack):
        for e in ("pe", "act", "dve", "pool"):
            self.sems[e] = stack.enter_context(self.nc.semaphore("s_" + e))
            self.cnt[e] = 0
        for q in ("sp", "pool"):
            self.dma_rr[q] = 0
            for i in range(self.n_dma_sems):
                k = "d_%s_%d" % (q, i)
                self.sems[k] = stack.enter_context(self.nc.semaphore(k))
                self.cnt[k] = 0

    def _wait(self, E, key, val):
        if val <= 0 or self.waited[E].get(key, 0) >= val:
            return
        self.waited[E][key] = val
        sem = self.sems[key]
        self.ops[E].append(lambda eng, sem=sem, val=val: eng.wait_ge(sem, val))
        self.ninst += 1

    @staticmethod
    def _deps(reads, writes):
        need = {}
        for t in reads:
            if t.w is not None:
                k, v = t.w
                if need.get(k, 0) < v:
                    need[k] = v
        for t in writes:
            if t.w is not None:
                k, v = t.w
                if need.get(k, 0) < v:
                    need[k] = v
            for k, v in t.r.items():
                if need.get(k, 0) < v:
                    need[k] = v
        return need

    def op(self, E, meth, *args, R=(), W=(), **kw):
        if any(t.x for t in R):
            W = list(W) + [t for t in R if t.x]
            R = [t for t in R if not t.x]
        need = self._deps(R, W)
        for k, v in need.items():
            if k == E and E == "pe":
                continue
            self._wait(E, k, v)
        self.cnt[E] += 1
        idx = self.cnt[E]
        sem = self.sems[E]
        self.ops[E].append(lambda eng, meth=meth, args=args, kw=kw, sem=sem:
                           getattr(eng, meth)(*args, **kw).then_inc(sem, 1))
        self.ninst += 1
        for t in R:
            t.r[E] = idx
        for t in W:
            t.w = (E, idx)
            t.r = {}

    def dma(self, Q, out, in_, R=(), W=()):
        need = self._deps(R, W)
        for k, v in need.items():
            self._wait(Q, k, v)
        i = self.dma_rr[Q]
        self.dma_rr[Q] = (i + 1) % self.n_dma_sems
        key = "d_%s_%d" % (Q, i)
        self._wait(Q, key, self.cnt[key])
        self.cnt[key] += 16
        val = self.cnt[key]
        sem = self.sems[key]
        self.ops[Q].append(lambda eng, out=out, in_=in_, sem=sem: eng.dma_start(out=out, in_=in_).then_inc(sem, 16))
        self.ninst += 1
        for t in R:
            t.r[key] = val
        for t in W:
            t.w = (key, val)
            t.r = {}

    def barrier(self):
        for E in self.ENGS:
            for key, v in self.cnt.items():
                if key != E:
                    self._wait(E, key, v)

    def emit(self, name=None):
        nc = self.nc
        ops = self.ops
        self.ops = {e: [] for e in self.ENGS}
        self.nphase = getattr(self, "nphase", 0) + 1
        with nc.named_scope("ph%02d" % self.nphase), nc.Block() as block:
            @block.tensor
            def _(eng):
                for f in ops["pe"]:
                    f(eng)

            @block.scalar
            def _(eng):
                for f in ops["act"]:
                    f(eng)

            @block.vector
            def _(eng):
                for f in ops["dve"]:
                    f(eng)

            @block.gpsimd
            def _(eng):
                for f in ops["pool"]:
                    f(eng)

            @block.sync
            def _(eng):
                for f in ops["sp"]:
                    f(eng)


def build_program(n_layers=4):
    nc = bass.Bass("TRN2", target_bir_lowering=False)

    def din(name, shape):
        return nc.dram_tensor(name, list(shape), F32, kind="ExternalInput").ap()

    x_d = din("x", [S, D])
    c_d = din("c_fm", [128, 8])
    adaw_d = din("ada_w", [4, 1024, 6144])
    adab_d = din("ada_b_fm", [128, 192])
    normw_d = din("norm_w_fm", [128, 64])
    fnw_d = din("fnw_fm", [128, 8])
    w1_d = din("mlp_w1", [4, 1024, 4096])
    w2_d = din("mlp_w2", [4, 4096, 1024])
    hin_d = din("hyb_w_in_ext", [2, 1024, 2176])
    hout_d = din("hyb_w_out", [2, 1024, 1024])
    lamre_d = din("s5_lamre", [2, 128, 64])
    lamim_d = din("s5_lamim", [2, 128, 64])
    logdt_d = din("s5_logdt", [2, 128, 64])
    sb1_d = din("s5_b1", [2, 128, 1024])
    sb2_d = din("s5_b2", [2, 128, 1024])
    sc1_d = din("s5_c1", [2, 128, 1024])
    sc2_d = din("s5_c2", [2, 128, 1024])
    s5d_d = din("s5_d_fm", [2, 128, 4])
    gluw_d = din("s5_glu_w", [2, 512, 512])
    glub_d = din("s5_glu_b_fm", [2, 128, 4])
    qkn_d = din("qkn_fm", [2, 128, 4])
    rin_d = din("rec_w_in", [2, 1024, 2048])
    rout_d = din("rec_w_out", [2, 1024, 1024])
    convw_d = din("rec_conv_fm", [2, 128, 32])
    convb_d = din("rec_convb_fm", [2, 128, 8])
    rabd_d = din("rec_ra_bd", [2, 128, 2048])
    ixbd_d = din("rec_ix_bd", [2, 128, 2048])
    rab_d = din("rec_rab_fm", [2, 128, 16])
    ixb_d = din("rec_ixb_fm", [2, 128, 16])
    rlam_d = din("rec_lam_fm", [2, 128, 16])
    rope_c_d = din("rope_cos", [128, S])
    rope_s_d = din("rope_sin", [128, S])
    const_d = din("consts", [128, 400])
    out_d = nc.dram_tensor("out", [S, D], F32, kind="ExternalOutput").ap()

    h_d = nc.dram_tensor("h_scr", [128, 8, S], F32, kind="Internal").ap()
    mix_d = nc.dram_tensor("mix_scr", [128, 8, S], BF16, kind="Internal").ap()
    gate_d = nc.dram_tensor("gate_scr", [128, 8, S], BF16, kind="Internal").ap()

    with ExitStack() as g:
        P = Prog(nc)
        P.alloc_sems(g)

        def gsb(name, shape, dt=F32):
            return g.enter_context(nc.sbuf_tensor(name, list(shape), dt))

        PS = [g.enter_context(nc.psum_tensor("ps%d" % i, [128, 512], F32)) for i in range(8)]
        PSt = [Tok(x=True) for _ in range(8)]

        cst = gsb("cst", [128, 400])
        t_cst = Tok()
        P.dma("sp", cst[:], const_d, W=[t_cst])
        ident = cst[:, 0:128]
        swapm = cst[:, 256:384]
        sgn = cst[:, 384:385]
        nsgn = cst[:, 385:386]
        gmask = cst[:, 386:394]
        ones_f = gsb("ones_f", [128, 128])
        ones_b = gsb("ones_b", [128, 128], BF16)
        bones_b = gsb("bones_b", [128, 128], BF16)
        eps_c = gsb("eps_c", [128, 1])
        one_c = gsb("one_c", [128, 1])
        t_c2 = Tok()
        P.op("pool", "memset", ones_f[:], 1.0, W=[t_c2])
        P.op("pool", "memset", ones_b[:], 1.0, W=[t_c2])
        P.op("pool", "memset", eps_c[:], EPS, W=[t_c2])
        P.op("pool", "memset", one_c[:], 1.0, W=[t_c2])
        P.op("pool", "tensor_copy", out=bones_b[:], in_=cst[:, 128:256], R=[t_cst], W=[t_c2])
        swapm_b = gsb("swapm_b", [128, 128], BF16)
        P.op("pool", "tensor_copy", out=swapm_b[:], in_=cst[:, 256:384], R=[t_cst], W=[t_c2])
        CT = [t_cst, t_c2]

        modall = gsb("modall", [128, 192])
        normw = gsb("normw", [128, 64])
        fnw = gsb("fnw", [128, 8])
        A1 = gsb("A1", [128, 32])
        A2 = gsb("A2", [128, 32])
        t_mod = Tok()

        uid = [0]

        def sbuf(st, name, shape, dt=F32):
            uid[0] += 1
            return st.enter_context(nc.sbuf_tensor("%s_u%d" % (name, uid[0]), list(shape), dt))

        with ExitStack() as st:
            cf = sbuf(st, "cf", [128, 8])
            cb = sbuf(st, "cb", [128, 8], BF16)
            adab = sbuf(st, "adab", [128, 192])
            t_cf, t_cb, t_ab, t_nw = toks(4)
            P.dma("sp", cf[:], c_d, W=[t_cf])
            P.dma("sp", adab[:], adab_d, W=[t_ab])
            P.dma("sp", normw[:], normw_d, W=[t_nw])
            P.dma("sp", fnw[:], fnw_d, W=[t_nw])
            P.op("act", "activation", out=cb[:], in_=cf[:], func=AF.Silu, R=[t_cf], W=[t_cb])
            NB = 3
            wt = [sbuf(st, "adaw%d" % i, [128, 8, 512], BF16) for i in range(NB)]
            twt = toks(NB)
            n = 0
            for l in range(4):
                src_l = adaw_d[l].rearrange("(k p) n -> p k n", p=128)
                for blk in range(12):
                    i = n % NB
                    n += 1
                    P.dma("pool", wt[i][:], src_l[:, :, blk * 512:(blk + 1) * 512], W=[twt[i]])
                    for j in range(4):
                        col = l * 48 + blk * 4 + j
                        for k in range(8):
                            P.op("pe", "matmul", PS[0][:, col:col + 1], lhsT=wt[i][:, k, j * 128:(j + 1) * 128],
                                 rhs=cb[:, k:k + 1], start=(k == 0), stop=(k == 7), R=[twt[i], t_cb], W=[PSt[0]])
            P.op("dve", "tensor_tensor", out=modall[:], in0=PS[0][:, 0:192], in1=adab[:], op=ALU.add,
                 R=[PSt[0], t_ab], W=[t_mod])
            for l in range(4):
                for (A, sc0, nw0) in ((A1, 8, 0), (A2, 32, 8)):
                    P.op("dve", "tensor_scalar", out=A[:, l * 8:(l + 1) * 8], in0=modall[:, l * 48 + sc0:l * 48 + sc0 + 8],
                         scalar1=1.0, scalar2=None, op0=ALU.add, R=[t_mod], W=[t_mod])
                    P.op("dve", "tensor_tensor", out=A[:, l * 8:(l + 1) * 8], in0=A[:, l * 8:(l + 1) * 8],
                         in1=normw[:, l * 16 + nw0:l * 16 + nw0 + 8], op=ALU.mult, R=[t_mod, t_nw], W=[t_mod])
            P.barrier()
            P.emit()

        def modcol(l, which, k):
            c = l * 48 + which * 8 + k
            return modall[:, c:c + 1]

        def norm_tile(st_tiles, h, th, N, u, tu, Acol, Bcol, pss, tpss):
            sq, tsq, srt, tsrt, rstd, trstd, tmp, ttmp = st_tiles
            for k in range(8):
                P.op("act", "activation", out=sq[k % 2][:, :N], in_=h[:, k, :N], func=AF.Square, R=[th], W=[tsq[k % 2]])
                P.op("pe", "matmul", pss[:, :N], lhsT=ones_b[:], rhs=sq[k % 2][:, :N], start=(k == 0), stop=(k == 7),
                     R=[tsq[k % 2]] + CT, W=[tpss])
            P.op("act", "activation", out=srt[:, :N], in_=pss[:, :N], func=AF.Sqrt, scale=1.0 / D, bias=eps_c[:],
                 R=[tpss] + CT, W=[tsrt])
            P.op("dve", "reciprocal", out=rstd[:, :N], in_=srt[:, :N], R=[tsrt], W=[trstd])
            for k in range(8):
                P.op("dve", "tensor_tensor", out=tmp[k % 2][:, :N], in0=h[:, k, :N], in1=rstd[:, :N], op=ALU.mult,
                     R=[th, trstd], W=[ttmp[k % 2]])
                if Bcol is None:
                    P.op("act", "activation", out=u[:, k, :N], in_=tmp[k % 2][:, :N], func=AF.Identity, scale=Acol(k),
                         R=[ttmp[k % 2], t_mod], W=[tu])
                else:
                    P.op("act", "activation", out=u[:, k, :N], in_=tmp[k % 2][:, :N], func=AF.Identity, scale=Acol(k),
                         bias=Bcol(k), R=[ttmp[k % 2], t_mod], W=[tu])

        def norm_scratch(st, N):
            sq = [sbuf(st, "n_sq%d" % i, [128, N], BF16) for i in range(2)]
            srt = sbuf(st, "n_srt", [128, N])
            rstd = sbuf(st, "n_rstd", [128, N])
            tmp = [sbuf(st, "n_tmp%d" % i, [128, N]) for i in range(2)]
            return (sq, toks(2), srt, Tok(), rstd, Tok(), tmp, toks(2))

        with ExitStack() as st:
            xt = [sbuf(st, "xt%d" % i, [128, D]) for i in range(2)]
            txt = toks(2)
            stg = [sbuf(st, "stg%d" % i, [128, 8, TW]) for i in range(2)]
            tstg = toks(2)
            for tt in range(NT):
                sg, tsg = stg[tt % 2], tstg[tt % 2]
                for j in range(4):
                    i = tt * 4 + j
                    xx, txx = xt[i % 2], txt[i % 2]
                    P.dma("sp", xx[:], x_d[i * 128:(i + 1) * 128, :], W=[txx])
                    for b in range(2):
                        bank = 2 * (i % 2) + b
                        for kk in range(4):
                            k = 4 * b + kk
                            P.op("pe", "transpose", out=PS[bank][:, kk * 128:(kk + 1) * 128], in_=xx[:, k * 128:(k + 1) * 128],
                                 identity=ident, R=[txx] + CT, W=[PSt[bank]])
                        P.op("act" if b == 0 else "dve", *(("activation",) if b == 0 else ("tensor_copy",)),
                             out=sg[:, 4 * b:4 * b + 4, j * 128:(j + 1) * 128],
                             in_=PS[bank][:, :].rearrange("p (k t) -> p k t", t=128),
                             **({"func": AF.Copy} if b == 0 else {}), R=[PSt[bank]], W=[tsg])
                P.dma("sp", h_d[:, :, tt * TW:(tt + 1) * TW], sg[:], R=[tsg], W=[])
            P.barrier()
            P.emit()

        def phase_tail(l, wout_src):
            TT = 256
            with ExitStack() as st:
                wo = sbuf(st, "wo", [128, 8, 1024], BF16)
                w1 = sbuf(st, "w1", [128, 8, 4096], BF16)
                w2 = sbuf(st, "w2", [128, 32, 1024], BF16)
                t_wo = Tok()
                t_w1, t_w2 = toks(4), toks(4)
                P.dma("pool", wo[:], wout_src.rearrange("(k p) n -> p k n", p=128), W=[t_wo])
                w1s = w1_d[l].rearrange("(k p) n -> p k n", p=128)
                for q4 in range(4):
                    P.dma("pool", w1[:, :, q4 * 1024:(q4 + 1) * 1024], w1s[:, :, q4 * 1024:(q4 + 1) * 1024], W=[t_w1[q4]])
                w2s = w2_d[l].rearrange("(k p) n -> p k n", p=128)
                for q4 in range(4):
                    P.dma("pool", w2[:, q4 * 8:(q4 + 1) * 8, :], w2s[:, q4 * 8:(q4 + 1) * 8, :], W=[t_w2[q4]])
                mx = [sbuf(st, "mx%d" % i, [128, 8, TT], BF16) for i in range(2)]
                tmx = toks(2)
                hh = [sbuf(st, "hh%d" % i, [128, 8, TT]) for i in range(2)]
                thh = toks(2)
                u2s = [sbuf(st, "u2_%d" % i, [128, 8, TT], BF16) for i in range(2)]
                tu2s = toks(2)
                ff = sbuf(st, "ff", [128, 32, TT], BF16)
                tff = toks(32)
                rl = [sbuf(st, "rl%d" % i, [128, TT], BF16) for i in range(3)]
                trl = toks(3)
                nsc = norm_scratch(st, TT)
                NIT = S // TT

                def stage_a1(it):
                    t0 = it * TT
                    m, tm = mx[it % 2], tmx[it % 2]
                    h, th = hh[it % 2], thh[it % 2]
                    P.dma("sp", m[:], mix_d[:, :, t0:t0 + TT], W=[tm])
                    P.dma("sp", h[:], h_d[:, :, t0:t0 + TT], W=[th])
                    for mo in range(8):
                        b = 1 + (mo % 2)
                        for k in range(8):
                            P.op("pe", "matmul", PS[b][:, :TT], lhsT=wo[:, k, mo * 128:(mo + 1) * 128], rhs=m[:, k, :],
                                 start=(k == 0), stop=(k == 7), R=[t_wo, tm], W=[PSt[b]])
                        P.op("dve", "scalar_tensor_tensor", out=h[:, mo, :], in0=PS[b][:, :TT], scalar=modcol(l, 2, mo),
                             in1=h[:, mo, :], op0=ALU.mult, op1=ALU.add, R=[PSt[b], t_mod, th], W=[th])

                def stage_a2(it):
                    norm_tile(nsc, hh[it % 2], thh[it % 2], TT, u2s[it % 2], tu2s[it % 2],
                              lambda k: A2[:, l * 8 + k:l * 8 + k + 1], lambda k: modcol(l, 3, k), PS[0], PSt[0])

                stage_a1(0)
                stage_a2(0)
                for it in range(NIT):
                    t0 = it * TT
                    h, th = hh[it % 2], thh[it % 2]
                    u2, tu2 = u2s[it % 2], tu2s[it % 2]
                    for f in range(32):
                        b = 3 + (f % 3)
                        for k in range(8):
                            P.op("pe", "matmul", PS[b][:, :TT], lhsT=w1[:, k, f * 128:(f + 1) * 128], rhs=u2[:, k, :],
                                 start=(k == 0), stop=(k == 7), R=[t_w1[f // 8], tu2], W=[PSt[b]])
                        r, tr = rl[f % 3], trl[f % 3]
                        P.op("act", "activation", out=r[:], in_=PS[b][:, :TT], func=AF.Relu, R=[PSt[b]], W=[tr])
                        P.op("pool", "tensor_tensor", out=ff[:, f, :], in0=r[:], in1=r[:], op=ALU.mult, R=[tr], W=[tff[f]])
                    if it + 1 < NIT:
                        stage_a1(it + 1)
                    for mo in range(8):
                        if mo == 4 and it + 1 < NIT:
                            stage_a2(it + 1)
                        b = 6 + (mo % 2)
                        for f in range(32):
                            P.op("pe", "matmul", PS[b][:, :TT], lhsT=w2[:, f, mo * 128:(mo + 1) * 128], rhs=ff[:, f, :],
                                 start=(f == 0), stop=(f == 31), R=[t_w2[f // 8], tff[f]], W=[PSt[b]])
                        P.op("dve", "scalar_tensor_tensor", out=h[:, mo, :], in0=PS[b][:, :TT], scalar=modcol(l, 5, mo),
                             in1=h[:, mo, :], op0=ALU.mult, op1=ALU.add, R=[PSt[b], t_mod, th], W=[th])
                    P.dma("sp", h_d[:, :, t0:t0 + TT], h[:], R=[th], W=[])
                P.barrier()
                P.emit()

        def phase_even(l):
            e = l // 2
            with ExitStack() as lst:
                zs5 = [sbuf(lst, "zs5_%d" % i, [128, S], BF16) for i in range(4)]
                tzs5 = [toks(NT) for _ in range(4)]
                with ExitStack() as ast:
                    qb = [sbuf(ast, "q%d" % i, [128, S], BF16) for i in range(4)]
                    tq = [toks(NT) for _ in range(4)]
                    kd = [sbuf(ast, "kd%d" % i, [128, S], BF16) for i in range(2)]
                    tkd = [Tok() for _ in range(2)]
                    Vx = sbuf(ast, "Vx", [128, 32, 2, 192], BF16)
                    tVx = Tok()
                    with ExitStack() as st:
                        win = sbuf(st, "win", [128, 8, 2176], BF16)
                        t_winp = toks(4)
                        wsrc = hin_d[e].rearrange("(k p) n -> p k n", p=128)
                        for q4 in range(4):
                            P.dma("pool", win[:, :, q4 * 544:(q4 + 1) * 544], wsrc[:, :, q4 * 544:(q4 + 1) * 544], W=[t_winp[q4]])

                        def wtok(c0, n=128):
                            return [t_winp[i] for i in range(c0 // 544, (c0 + n - 1) // 544 + 1)]
                        qkn = sbuf(st, "qkn", [128, 4])
                        t_qkn = Tok()
                        P.dma("sp", qkn[:], qkn_d[e], W=[t_qkn])
                        P.op("pool", "memset", Vx[:], 0.0, W=[tVx])
                        P.op("pool", "memset", Vx[:, :, :, 64:65], 1.0, W=[tVx])
                        hh = sbuf(st, "hh", [128, 8, TW])
                        thh = Tok()
                        u = sbuf(st, "u", [128, 8, TW], BF16)
                        tu = Tok()
                        rc = [sbuf(st, "rc%d" % i, [128, TW]) for i in range(2)]
                        rs = [sbuf(st, "rs%d" % i, [128, TW]) for i in range(2)]
                        trc = toks(2)
                        sqh_ = [sbuf(st, "sqh%d" % i, [128, TW], BF16) for i in range(2)]
                        srt_ = [sbuf(st, "srt%d" % i, [128, TW]) for i in range(2)]
                        rstd_ = [sbuf(st, "rstd%d" % i, [128, TW]) for i in range(2)]
                        t1_ = [sbuf(st, "t1_%d" % i, [128, TW]) for i in range(2)]
                        t2_ = [sbuf(st, "t2_%d" % i, [128, TW]) for i in range(2)]
                        tsqh_, tsrt_, trstd_, tt1_, tt2_ = toks(2), toks(2), toks(2), toks(2), toks(2)
                        nsc = norm_scratch(st, TW)
                        for tt in range(NT):
                            ts_ = slice(tt * TW, (tt + 1) * TW)
                            P.dma("sp", hh[:], h_d[:, :, ts_], W=[thh])
                            P.dma("sp", rc[tt % 2][:], rope_c_d[:, ts_], W=[trc[tt % 2]])
                            P.dma("sp", rs[tt % 2][:], rope_s_d[:, ts_], W=[trc[tt % 2]])
                            norm_tile(nsc, hh, thh, TW, u, tu, lambda k: A1[:, l * 8 + k:l * 8 + k + 1],
                                      lambda k: modcol(l, 0, k), PS[0], PSt[0])

                            def proj(bank, c0):
                                for k in range(8):
                                    P.op("pe", "matmul", PS[bank][:, :], lhsT=win[:, k, c0:c0 + 128], rhs=u[:, k, :],
                                         start=(k == 0), stop=(k == 7), R=wtok(c0) + [tu], W=[PSt[bank]])
                            for cs in range(4):
                                b = 1 + (cs % 2)
                                proj(b, cs * 128)
                                P.op("act", "activation", out=zs5[cs][:, ts_], in_=PS[b][:, :], func=AF.Copy,
                                     R=[PSt[b]], W=[tzs5[cs][tt]])
                            items = [(512 + 128 * j, 1024 + 128 * j, qb[j], tq[j][tt], 0, 1) for j in range(4)]
                            items += [(1536 + 128 * j, 1792 + 128 * j, kd[j], tkd[j], 2, 3) for j in range(2)]
                            for n_, (cz, cp, dst, tdst, wc, wpc) in enumerate(items):
                                bz = 3 + (n_ % 2)
                                bp = 5 + (n_ % 2)
                                sqh, srt, rstd, t1, t2 = sqh_[n_ % 2], srt_[n_ % 2], rstd_[n_ % 2], t1_[n_ % 2], t2_[n_ % 2]
                                tsqh, tsrt, trstd, tt1, tt2 = tsqh_[n_ % 2], tsrt_[n_ % 2], trstd_[n_ % 2], tt1_[n_ % 2], tt2_[n_ % 2]
                                proj(bz, cz)
                                proj(bp, cp)
                                P.op("act", "activation", out=sqh[:], in_=PS[bz][:, :], func=AF.Square, R=[PSt[bz]], W=[tsqh])
                                P.op("pe", "matmul", PS[7][:, :], lhsT=bones_b[:], rhs=sqh[:], start=True, stop=True,
                                     R=[tsqh] + CT, W=[PSt[7]])
                                P.op("act", "activation", out=srt[:], in_=PS[7][:, :], func=AF.Sqrt, scale=1.0 / 64, bias=eps_c[:],
                                     R=[PSt[7]] + CT, W=[tsrt])
                                P.op("dve", "reciprocal", out=rstd[:], in_=srt[:], R=[tsrt], W=[trstd])
                                P.op("dve", "scalar_tensor_tensor", out=t1[:], in0=PS[bz][:, :], scalar=qkn[:, wc:wc + 1],
                                     in1=rc[tt % 2][:], op0=ALU.mult, op1=ALU.mult, R=[PSt[bz], t_qkn, trc[tt % 2]], W=[tt1])
                                P.op("dve", "scalar_tensor_tensor", out=t2[:], in0=PS[bp][:, :], scalar=qkn[:, wpc:wpc + 1],
                                     in1=rs[tt % 2][:], op0=ALU.mult, op1=ALU.mult, R=[PSt[bp], t_qkn, trc[tt % 2]], W=[tt2])
                                P.op("pool", "tensor_tensor", out=t1[:], in0=t1[:], in1=t2[:], op=ALU.add, R=[tt1, tt2], W=[tt1])
                                P.op("dve", "tensor_tensor", out=dst[:, ts_], in0=t1[:], in1=rstd[:], op=ALU.mult,
                                     R=[tt1, trstd], W=[tdst])
                            b = 1 + (tt % 2)
                            for j in range(4):
                                for k in range(8):
                                    P.op("pe", "matmul", PS[b][:, j * 128:(j + 1) * 128], lhsT=u[:, k, j * 128:(j + 1) * 128],
                                         rhs=win[:, k, 2048:2176], start=(k == 0), stop=(k == 7), R=wtok(2048) + [tu], W=[PSt[b]])
                            src = PS[b][:, :].rearrange("p (j v d) -> p j v d", j=4, v=2)
                            P.op("act", "activation", out=Vx[:, tt * 4:(tt + 1) * 4, :, 0:64], in_=src, func=AF.Copy,
                                 R=[PSt[b]], W=[tVx])
                            P.op("dve", "tensor_copy", out=Vx[:, tt * 4:(tt + 1) * 4, :, 128:192], in_=src, R=[PSt[b]], W=[tVx])
                        P.barrier()
                        P.emit()
                    with ExitStack() as st:
                        NSB = 3
                        pt = [sbuf(st, "pt%d" % i, [128, TW], BF16) for i in range(NSB)]
                        tpt = toks(NSB)
                        rec = [sbuf(st, "rec%d" % i, [128, TW]) for i in range(2)]
                        bcs = [sbuf(st, "bcs%d" % i, [128, TW]) for i in range(2)]
                        trec, tbcs = toks(2), toks(2)
                        oat = [sbuf(st, "oat%d" % i, [128, TW], BF16) for i in range(2)]
                        toat = toks(2)
                        steps = [(j, qt, hhf, kt) for j in range(4) for qt in range(NT) for hhf in range(2) for kt in range(32)]
                        NS = len(steps)
                        qz = [sbuf(st, "qz%d" % i, [128, S], BF16) for i in range(4)]
                        tqz = toks(4)
                        for j in range(4):
                            P.op("pool", "memset", qz[j][0:64, :], 0.0, W=[tqz[j]])
                            P.op("act" if j % 2 == 0 else "dve", *(("activation",) if j % 2 == 0 else ("tensor_copy",)),
                                 out=qz[j][64:128, :], in_=qb[j][64:128, :], **({"func": AF.Copy} if j % 2 == 0 else {}),
                                 R=tq[j], W=[tqz[j]])
                            P.op("pool", "memset", qb[j][64:128, :], 0.0, R=[tqz[j]], W=tq[j])

                        def emit_qk(s_):
                            j, qt, hhf, kt = steps[s_]
                            kv, bs = j // 2, s_ % NSB
                            qsrc = qb[j] if hhf == 0 else qz[j]
                            P.op("pe", "matmul", PS[bs][:, :], lhsT=kd[kv][:, kt * 128:(kt + 1) * 128],
                                 rhs=qsrc[:, qt * TW:(qt + 1) * TW], start=True, stop=True,
                                 R=[tkd[kv], tq[j][qt], tqz[j]], W=[PSt[bs]])

                        pending = []
                        emit_qk(0)
                        emit_qk(1)
                        for s_ in range(NS):
                            j, qt, hhf, kt = steps[s_]
                            kv, bs = j // 2, s_ % NSB
                            un = s_ // 32
                            bo = 3 + (un % 2)
                            P.op("act", "activation", out=pt[bs][:], in_=PS[bs][:, :], func=AF.Exp, scale=0.125,
                                 R=[PSt[bs]], W=[tpt[bs]])
                            if s_ + 2 < NS:
                                emit_qk(s_ + 2)
                            if hhf == 0:
                                P.op("pe", "matmul", PS[bo][:, :], lhsT=Vx[:, kt, kv, 0:128], rhs=pt[bs][:],
                                     start=(kt == 0), stop=(kt == 31), R=[tVx, tpt[bs]], W=[PSt[bo]])
                            else:
                                P.op("pe", "matmul", PS[bo][:, :], lhsT=Vx[:, kt, kv, 64:192], rhs=pt[bs][:],
                                     start=(kt == 0), stop=(kt == 31), R=[tVx, tpt[bs]], W=[PSt[bo]])
                            if kt == 31:
                                ob_ = (j * NT + qt) % 2
                                o, to = oat[ob_], toat[ob_]
                                rr, trr, bb, tbb = rec[un % 2], trec[un % 2], bcs[un % 2], tbcs[un % 2]
                                qs = slice(qt * TW, (qt + 1) * TW)
                                if hhf == 0:
                                    P.op("dve", "reciprocal", out=rr[64:65, :], in_=PS[bo][64:65, :], R=[PSt[bo]], W=[trr])

                                    def f2(rr=rr, trr=trr):
                                        P.op("pe", "matmul", PS[5][0:64, :], lhsT=ones_f[64:65, 0:64], rhs=rr[64:65, :],
                                             start=True, stop=True, R=[trr] + CT, W=[PSt[5]])

                                    def f3(bb=bb, tbb=tbb, o=o, to=to, bo=bo):
                                        P.op("act", "activation", out=bb[0:64, :], in_=PS[5][0:64, :], func=AF.Copy,
                                             R=[PSt[5]], W=[tbb])
                                        P.op("dve", "tensor_tensor", out=o[0:64, :], in0=PS[bo][0:64, :], in1=bb[0:64, :],
                                             op=ALU.mult, R=[PSt[bo], tbb], W=[to])
                                else:
                                    P.op("dve", "reciprocal", out=rr[0:1, :], in_=PS[bo][0:1, :], R=[PSt[bo]], W=[trr])

                                    def f2(rr=rr, trr=trr):
                                        P.op("pe", "matmul", PS[5][:, :], lhsT=ones_f[0:1, :], rhs=rr[0:1, :],
                                             start=True, stop=True, R=[trr] + CT, W=[PSt[5]])

                                    def f3(bb=bb, tbb=tbb, o=o, to=to, bo=bo, j=j, qs=qs):
                                        P.op("act", "activation", out=bb[64:128, :], in_=PS[5][64:128, :], func=AF.Copy,
                                             R=[PSt[5]], W=[tbb])
                                        P.op("dve", "tensor_tensor", out=o[64:128, :], in0=PS[bo][64:128, :], in1=bb[64:128, :],
                                             op=ALU.mult, R=[PSt[bo], tbb], W=[to])
                                        P.dma("sp", mix_d[:, 4 + j, qs], o[:], R=[to], W=[])
                                pending.append((s_ + 3, f2))
                                pending.append((s_ + 6, f3))
                            while pending and pending[0][0] <= s_:
                                pending.pop(0)[1]()
                        for _, fn in pending:
                            fn()
                        P.barrier()
                        P.emit()
                with ExitStack() as st:
                    NG = 64
                    prm = {}
                    tprm = Tok()

                    def ptile(name, w=NG):
                        prm[name] = sbuf(st, "s5p_" + name, [128, w])
                        return prm[name]
                    lre, lim, ldt = ptile("lre"), ptile("lim"), ptile("ldt")
                    P.dma("sp", lre[:], lamre_d[e], W=[tprm])
                    P.dma("sp", lim[:], lamim_d[e], W=[tprm])
                    P.dma("sp", ldt[:], logdt_d[e], W=[tprm])

                    def dv(meth, **kw):
                        P.op("dve", meth, R=[tprm] + CT, W=[tprm], **kw)

                    def ac(**kw):
                        P.op("act", "activation", R=[tprm] + CT, W=[tprm], **kw)
                    lr, dt, x1, mag, th = ptile("lr"), ptile("dt"), ptile("x1"), ptile("mag"), ptile("th")
                    dv("tensor_scalar", out=lr[:], in0=lre[:], scalar1=-1e-4, scalar2=None, op0=ALU.min)
                    ac(out=dt[:], in_=ldt[:], func=AF.Exp)
                    dv("tensor_tensor", out=x1[:], in0=lr[:], in1=dt[:], op=ALU.mult)
                    ac(out=mag[:], in_=x1[:], func=AF.Exp)
                    dv("tensor_tensor", out=th[:], in0=lim[:], in1=dt[:], op=ALU.mult)
                    ti = sbuf(st, "s5p_ti", [128, NG], I32)
                    tf, red = ptile("tf"), ptile("red")

                    def sin_of(dst, src, shift):
                        dv("tensor_scalar", out=tf[:], in0=src[:], scalar1=shift, scalar2=1.0 / (2 * PI), op0=ALU.add, op1=ALU.mult)
                        dv("tensor_copy", out=ti[:], in_=tf[:])
                        dv("tensor_copy", out=tf[:], in_=ti[:])
                        dv("tensor_scalar", out=red[:], in0=src[:], scalar1=shift, scalar2=None, op0=ALU.add)
                        dv("scalar_tensor_tensor", out=red[:], in0=tf[:], scalar=-2 * PI, in1=red[:], op0=ALU.mult, op1=ALU.add)
                        dv("tensor_scalar", out=red[:], in0=red[:], scalar1=PI, scalar2=-PI, op0=ALU.min, op1=ALU.max)
                        ac(out=dst[:], in_=red[:], func=AF.Sin)
                    sn, cs_ = ptile("sn"), ptile("cs")
                    sin_of(sn, th, 0.0)
                    sin_of(cs_, th, PI / 2)
                    are, aim = ptile("are"), ptile("aim")
                    dv("tensor_tensor", out=are[:], in0=mag[:], in1=cs_[:], op=ALU.mult)
                    dv("tensor_tensor", out=aim[:], in0=mag[:], in1=sn[:], op=ALU.mult)
                    den, t_a, t_b = ptile("den"), ptile("ta"), ptile("tb")
                    dv("tensor_tensor", out=den[:], in0=lr[:], in1=lr[:], op=ALU.mult)
                    dv("tensor_tensor", out=t_a[:], in0=lim[:], in1=lim[:], op=ALU.mult)
                    dv("tensor_tensor", out=den[:], in0=den[:], in1=t_a[:], op=ALU.add)
                    dv("reciprocal", out=den[:], in_=den[:])
                    nr = ptile("nr")
                    dv("tensor_scalar", out=nr[:], in0=are[:], scalar1=-1.0, scalar2=None, op0=ALU.add)
                    fre, fim = ptile("fre"), ptile("fim")
                    dv("tensor_tensor", out=t_a[:], in0=nr[:], in1=lr[:], op=ALU.mult)
                    dv("tensor_tensor", out=t_b[:], in0=aim[:], in1=lim[:], op=ALU.mult)
                    dv("tensor_tensor", out=t_a[:], in0=t_a[:], in1=t_b[:], op=ALU.add)
                    dv("tensor_tensor", out=fre[:], in0=t_a[:], in1=den[:], op=ALU.mult)
                    dv("tensor_tensor", out=t_a[:], in0=aim[:], in1=lr[:], op=ALU.mult)
                    dv("tensor_tensor", out=t_b[:], in0=nr[:], in1=lim[:], op=ALU.mult)
                    dv("tensor_tensor", out=t_a[:], in0=t_a[:], in1=t_b[:], op=ALU.subtract)
                    dv("tensor_tensor", out=fim[:], in0=t_a[:], in1=den[:], op=ALU.mult)
                    F2, F3 = ptile("F2"), ptile("F3")
                    dv("tensor_scalar", out=F2[:], in0=fim[:], scalar1=sgn, scalar2=None, op0=ALU.mult)
                    dv("tensor_scalar", out=F3[:], in0=fre[:], scalar1=nsgn, scalar2=None, op0=ALU.mult)
                    CK = sbuf(st, "s5p_CK", [128, 12, NG])
                    SK = sbuf(st, "s5p_SK", [128, 12, NG])
                    dv("tensor_copy", out=CK[:, 0, :], in_=cs_[:])
                    dv("tensor_copy", out=SK[:, 0, :], in_=sn[:])
                    for k in range(11):
                        dv("tensor_tensor", out=t_a[:], in0=CK[:, k, :], in1=CK[:, k, :], op=ALU.mult)
                        dv("tensor_tensor", out=t_b[:], in0=SK[:, k, :], in1=SK[:, k, :], op=ALU.mult)
                        dv("tensor_tensor", out=CK[:, k + 1, :], in0=t_a[:], in1=t_b[:], op=ALU.subtract)
                        dv("tensor_tensor", out=t_a[:], in0=CK[:, k, :], in1=SK[:, k, :], op=ALU.mult)
                        dv("tensor_scalar", out=SK[:, k + 1, :], in0=t_a[:], scalar1=2.0, scalar2=None, op0=ALU.mult)
                    SKs9 = ptile("SKs9")
                    dv("tensor_scalar", out=SKs9[:], in0=SK[:, 9, :], scalar1=sgn, scalar2=None, op0=ALU.mult)
                    bbs = sbuf(st, "s5_bbs", [128, NG, 16])
                    bbw = sbuf(st, "s5_bbw", [128, NG, 16])

                    def bc16(a):
                        return a[:].unsqueeze(2).to_broadcast([128, NG, 16])
                    with ExitStack() as sub:
                        b1 = sbuf(sub, "s5_b1", [128, NG, 16])
                        b2 = sbuf(sub, "s5_b2", [128, NG, 16])
                        tmpb = sbuf(sub, "s5_tmpb", [128, NG, 16])
                        P.dma("sp", b1[:], sb1_d[e].rearrange("p (g c) -> p g c", c=16), W=[tprm])
                        P.dma("sp", b2[:], sb2_d[e].rearrange("p (g c) -> p g c", c=16), W=[tprm])
                        dv("tensor_tensor", out=bbs[:], in0=b1[:], in1=bc16(fre), op=ALU.mult)
                        dv("tensor_tensor", out=tmpb[:], in0=b2[:], in1=bc16(F2), op=ALU.mult)
                        dv("tensor_tensor", out=bbs[:], in0=bbs[:], in1=tmpb[:], op=ALU.add)
                        dv("tensor_tensor", out=bbw[:], in0=b2[:], in1=bc16(F3), op=ALU.mult)
                        dv("tensor_tensor", out=tmpb[:], in0=b1[:], in1=bc16(fim), op=ALU.mult)
                        dv("tensor_tensor", out=bbw[:], in0=bbw[:], in1=tmpb[:], op=ALU.add)
                        P.barrier()
                        P.emit()
                    c1 = sbuf(st, "s5_c1", [128, NG, 16])
                    c2 = sbuf(st, "s5_c2", [128, NG, 16])
                    P.dma("sp", c1[:], sc1_d[e].rearrange("p (g c) -> p g c", c=16), W=[tprm])
                    P.dma("sp", c2[:], sc2_d[e].rearrange("p (g c) -> p g c", c=16), W=[tprm])
                    s5d = sbuf(st, "s5_d", [128, 4])
                    glub = sbuf(st, "s5_glub", [128, 4])
                    P.dma("sp", s5d[:], s5d_d[e], W=[tprm])
                    P.dma("sp", glub[:], glub_d[e], W=[tprm])
                    gluw = sbuf(st, "s5_gluw", [128, 4, 512], BF16)
                    tgw = Tok()
                    P.dma("pool", gluw[:], gluw_d[e].rearrange("(k p) n -> p k n", p=128), W=[tgw])

                    bT = sbuf(st, "s5_bT", [128, 128])
                    bTw = sbuf(st, "s5_bTw", [128, 128])
                    tbT = Tok()
                    lb = [sbuf(st, "s5_lb%d" % i, [128, 8, 128], BF16) for i in range(2)]
                    lbw = [sbuf(st, "s5_lbw%d" % i, [128, 8, 128], BF16) for i in range(2)]
                    tlb = toks(2)
                    w1p = [sbuf(st, "s5_w1p%d" % i, [128, 1152], BF16) for i in range(2)]
                    w2p = [sbuf(st, "s5_w2p%d" % i, [128, 1152], BF16) for i in range(2)]
                    twp = toks(2)
                    for i in range(2):
                        P.op("pool", "memset", w1p[i][:], 0.0, W=[twp[i]])
                        P.op("pool", "memset", w2p[i][:], 0.0, W=[twp[i]])
                    COS = [sbuf(st, "s5_COS%d" % i, [128, TW]) for i in range(2)]
                    SIN = [sbuf(st, "s5_SIN%d" % i, [128, TW]) for i in range(2)]
                    tCOS, tSIN = toks(2), toks(2)
                    tA = sbuf(st, "s5_tA", [128, TW // 2])
                    tB = sbuf(st, "s5_tB", [128, TW // 2])
                    NCR = 4
                    crt = [sbuf(st, "s5_crt%d" % i, [128, 1]) for i in range(NCR)]
                    crc = [sbuf(st, "s5_crc%d" % i, [128, 1]) for i in range(NCR)]
                    tcrt, tcrc = toks(NCR), toks(NCR)
                    ttA, ttB = toks(2)
                    yacc = sbuf(st, "s5_yacc", [128, S])
                    tya = toks(NT)
                    gl, tgl = zs5, tzs5
                    NB = 3
                    vt = [sbuf(st, "s5_v%d" % i, [128, TW], BF16) for i in range(NB)]
                    tm_ = [sbuf(st, "s5_tm%d" % i, [128, TW], BF16) for i in range(NB)]
                    gt = [sbuf(st, "s5_g%d" % i, [128, TW], BF16) for i in range(NB)]
                    e1 = [sbuf(st, "s5_e1_%d" % i, [128, TW], BF16) for i in range(NB)]
                    e2 = [sbuf(st, "s5_e2_%d" % i, [128, TW], BF16) for i in range(NB)]
                    te1, te2 = toks(NB), toks(NB)
                    C16 = [sbuf(st, "s5_C16_%d" % i, [128, TW], BF16) for i in range(2)]
                    S16 = [sbuf(st, "s5_S16_%d" % i, [128, TW], BF16) for i in range(2)]
                    tC16, tS16 = toks(2), toks(2)
                    a1 = [sbuf(st, "s5_a1%d" % i, [128, TW], BF16) for i in range(NB)]
                    a2 = [sbuf(st, "s5_a2%d" % i, [128, TW], BF16) for i in range(NB)]
                    tvt, ttm, tgt, ta1, ta2 = toks(NB), toks(NB), toks(NB), toks(NB), toks(NB)
                    BUP = [(0, 1), (2, 3)]

                    gds = [(cs, dr, gq) for cs in range(4) for dr in range(2) for gq in range(8)]
                    steps = [(gi, tp) for gi in range(len(gds)) for tp in range(NT)]
                    NS = len(steps)

                    def gd_of(gi):
                        cs, dr, gq = gds[gi]
                        return cs, dr, gq, dr * 32 + cs * 8 + gq, (cs * 2 + dr) % 2

                    def gen_piece(gi, piece):
                        if gi >= len(gds):
                            return
                        gd = gd_of(gi)[3]
                        C_, S_, tC, tS = COS[gi % 2], SIN[gi % 2], tCOS[gi % 2], tSIN[gi % 2]
                        work = {0: [(0, 0, 1), (1, 0, 2)], 1: [(2, 0, 4), (3, 0, 8)], 2: [(4, 0, 16)], 3: [(5, 0, 32)],
                                4: [(6, 0, 64)], 5: [(7, 0, 128)], 6: [(8, 0, 256)], 7: []}[piece]
                        if piece == 0:
                            P.op("pool", "memset", C_[:, 0:1], 1.0, W=[tC])
                            P.op("pool", "memset", S_[:, 0:1], 0.0, W=[tS])
                        if piece == 7:
                            dr_ = gds[gi][1]
                            oc = C16[gi % 2][:, :] if dr_ == 0 else C16[gi % 2][:, ::-1]
                            os_ = S16[gi % 2][:, :] if dr_ == 0 else S16[gi % 2][:, ::-1]
                            P.op("act", "activation", out=oc, in_=C_[:, :], func=AF.Copy, R=[tC], W=[tC16[gi % 2]])
                            P.op("act", "activation", out=os_, in_=S_[:, :], func=AF.Copy, R=[tS], W=[tS16[gi % 2]])
                        for (k, c0, c1_) in work:
                            ln = 1 << k
                            n = c1_ - c0
                            ck = CK[:, k, gd:gd + 1]
                            sk = SK[:, k, gd:gd + 1]
                            P.op("act", "activation", out=C_[:, ln + c0:ln + c1_], in_=C_[:, c0:c1_], func=AF.Identity, scale=ck,
                                 R=[tC, tprm], W=[tC])
                            P.op("act", "activation", out=tA[:, 0:n], in_=S_[:, c0:c1_], func=AF.Identity, scale=sk,
                                 R=[tS, tprm], W=[ttA])
                            P.op("pool", "tensor_tensor", out=C_[:, ln + c0:ln + c1_], in0=C_[:, ln + c0:ln + c1_], in1=tA[:, 0:n],
                                 op=ALU.subtract, R=[tC, ttA], W=[tC])
                            P.op("act", "activation", out=S_[:, ln + c0:ln + c1_], in_=S_[:, c0:c1_], func=AF.Identity, scale=ck,
                                 R=[tS, tprm], W=[tS])
                            P.op("act", "activation", out=tB[:, 0:n], in_=C_[:, c0:c1_], func=AF.Identity, scale=sk,
                                 R=[tC, tprm], W=[ttB])
                            P.op("pool", "tensor_tensor", out=S_[:, ln + c0:ln + c1_], in0=S_[:, ln + c0:ln + c1_], in1=tB[:, 0:n],
                                 op=ALU.add, R=[tS, ttB], W=[tS])

                    def prep_group(cs, dr):
                        ws = (cs * 2 + dr) % 2
                        g0 = dr * 32 + cs * 8
                        P.op("pe", "transpose", out=PS[4][:, 0:128], in_=bbs[:, g0:g0 + 8, :].rearrange("p g c -> p (g c)"),
                             identity=ident, R=[tprm] + CT, W=[PSt[4]])
                        P.op("pe", "transpose", out=PS[4][:, 128:256], in_=bbw[:, g0:g0 + 8, :].rearrange("p g c -> p (g c)"),
                             identity=ident, R=[tprm] + CT, W=[PSt[4]])
                        P.op("act", "activation", out=bT[:], in_=PS[4][:, 0:128], func=AF.Copy, R=[PSt[4]], W=[tbT])
                        P.op("act", "activation", out=bTw[:], in_=PS[4][:, 128:256], func=AF.Copy, R=[PSt[4]], W=[tbT])
                        for gq in range(8):
                            P.op("dve", "tensor_scalar", out=lb[ws][:, gq, :], in0=bT[:], scalar1=gmask[:, gq:gq + 1], scalar2=None,
                                 op0=ALU.mult, R=[tbT] + CT, W=[tlb[ws]])
                            P.op("dve", "tensor_scalar", out=lbw[ws][:, gq, :], in0=bTw[:], scalar1=gmask[:, gq:gq + 1], scalar2=None,
                                 op0=ALU.mult, R=[tbT] + CT, W=[tlb[ws]])
                        P.op("dve", "tensor_scalar", out=w1p[ws][:].rearrange("p (g w) -> p g w", w=144)[:, :, 0:16],
                             in0=c1[:, g0:g0 + 8, :], scalar1=nsgn, scalar2=None, op0=ALU.mult, R=[tprm] + CT, W=[twp[ws]])
                        P.op("dve", "tensor_scalar", out=w2p[ws][:].rearrange("p (g w) -> p g w", w=144)[:, :, 0:16],
                             in0=c2[:, g0:g0 + 8, :], scalar1=-1.0, scalar2=None, op0=ALU.mult, R=[tprm] + CT, W=[twp[ws]])

                    def tile_of(s_):
                        gi, tp = steps[s_]
                        cs, dr, gq, gd, ws = gd_of(gi)
                        tt = tp if dr == 0 else NT - 1 - tp
                        return gi, tp, cs, dr, gq, gd, ws, tt

                    def tabs(s_):
                        gi, tp, cs, dr, gq, gd, ws, tt = tile_of(s_)
                        if dr == 0:
                            return COS[gi % 2][:, :], SIN[gi % 2][:, :]
                        return COS[gi % 2][:, ::-1], SIN[gi % 2][:, ::-1]

                    def st_bu(s_):
                        if s_ >= NS:
                            return
                        gi, tp, cs, dr, gq, gd, ws, tt = tile_of(s_)
                        if tp == 0 and gq == 0:
                            prep_group(cs, dr)
                        b1_, b2_ = BUP[s_ % 2]
                        ts_ = slice(tt * TW, (tt + 1) * TW)
                        P.op("pe", "matmul", PS[b1_][:, :], lhsT=lb[ws][:, gq, :], rhs=zs5[cs][:, ts_], start=True, stop=True,
                             R=[tlb[ws], tzs5[cs][tt]], W=[PSt[b1_]])
                        P.op("pe", "matmul", PS[b2_][:, :], lhsT=lbw[ws][:, gq, :], rhs=zs5[cs][:, ts_], start=True, stop=True,
                             R=[tlb[ws], tzs5[cs][tt]], W=[PSt[b2_]])

                    def st_a(s_):
                        if s_ >= NS:
                            return
                        gi = steps[s_][0]
                        b1_, b2_ = BUP[s_ % 2]
                        i = s_ % NB
                        P.op("act", "activation", out=e1[i][:], in_=PS[b1_][:, :], func=AF.Copy, R=[PSt[b1_]], W=[te1[i]])
                        P.op("act", "activation", out=e2[i][:], in_=PS[b2_][:, :], func=AF.Copy, R=[PSt[b2_]], W=[te2[i]])
                        P.op("dve", "tensor_tensor", out=tm_[i][:], in0=e1[i][:], in1=C16[gi % 2][:], op=ALU.mult,
                             R=[te1[i], tC16[gi % 2]], W=[ttm[i]])
                        P.op("dve", "tensor_tensor", out=vt[i][:], in0=e2[i][:], in1=S16[gi % 2][:], op=ALU.mult,
                             R=[te2[i], tS16[gi % 2]], W=[tvt[i]])
                        P.op("dve", "tensor_tensor", out=vt[i][:], in0=vt[i][:], in1=tm_[i][:], op=ALU.add,
                             R=[ttm[i], tvt[i]], W=[tvt[i]])

                    def st_swap(s_):
                        gi, tp, cs, dr, gq, gd, ws, tt = tile_of(s_)
                        if tp == NT - 1:
                            return
                        i = s_ % NB
                        col = 2 * (s_ % 128)
                        src = gt[i][:, TW - 2:TW] if dr == 0 else gt[i][:, 0:2]
                        P.op("pe", "matmul", PS[6][:, col:col + 2], lhsT=swapm_b[:], rhs=src, start=True, stop=True,
                             R=[tgt[i]] + CT, W=[PSt[6]])

                    def st_carry(s_):
                        gi, tp, cs, dr, gq, gd, ws, tt = tile_of(s_)
                        if tp == 0:
                            return
                        pv = (s_ - 1) % NB
                        cc = s_ % NCR
                        col = 2 * ((s_ - 1) % 128) + (1 if dr == 0 else 0)
                        glast = gt[pv][:, TW - 1:TW] if dr == 0 else gt[pv][:, 0:1]
                        P.op("dve", "tensor_scalar", out=crt[cc][:], in0=PS[6][:, col:col + 1], scalar1=SKs9[:, gd:gd + 1], scalar2=None,
                             op0=ALU.mult, R=[PSt[6], tprm], W=[tcrt[cc]])
                        P.op("dve", "scalar_tensor_tensor", out=crc[cc][:], in0=glast, scalar=CK[:, 9, gd:gd + 1], in1=crt[cc][:],
                             op0=ALU.mult, op1=ALU.add, R=[tgt[pv], tcrt[cc], tprm], W=[tcrc[cc]])

                    def st_b(s_):
                        gi, tp, cs, dr, gq, gd, ws, tt = tile_of(s_)
                        i = s_ % NB
                        cc = s_ % NCR
                        mcol = mag[:, gd:gd + 1].to_broadcast([128, TW])
                        extra = [tcrc[cc]] if tp > 0 else []
                        init = 0.0 if tp == 0 else crc[cc][:, 0:1]
                        if dr == 0:
                            P.op("dve", "tensor_tensor_scan", out=gt[i][:], data0=mcol, data1=vt[i][:], initial=init,
                                 op0=ALU.mult, op1=ALU.add, R=[tvt[i], tprm] + extra, W=[tgt[i]])
                        else:
                            P.op("dve", "tensor_tensor_scan", out=gt[i][:, ::-1], data0=mcol, data1=vt[i][:, ::-1], initial=init,
                                 op0=ALU.mult, op1=ALU.add, R=[tvt[i], tprm] + extra, W=[tgt[i]])

                    def st_c(s_):
                        gi, tp, cs, dr, gq, gd, ws, tt = tile_of(s_)
                        i = s_ % NB
                        by = 4 + (s_ % 2)
                        P.op("dve", "tensor_tensor", out=a1[i][:], in0=gt[i][:], in1=C16[gi % 2][:], op=ALU.mult,
                             R=[tgt[i], tC16[gi % 2]], W=[ta1[i]])
                        P.op("dve", "tensor_tensor", out=a2[i][:], in0=gt[i][:], in1=S16[gi % 2][:], op=ALU.mult,
                             R=[tgt[i], tS16[gi % 2]], W=[ta2[i]])
                        P.op("pe", "matmul", PS[by][:, :], lhsT=w1p[ws][:, gq * 128:(gq + 1) * 128], rhs=a1[i][:],
                             start=True, stop=False, R=[twp[ws], ta1[i]], W=[PSt[by]])
                        P.op("pe", "matmul", PS[by][:, :], lhsT=w2p[ws][:, gq * 128:(gq + 1) * 128], rhs=a2[i][:],
                             start=False, stop=True, R=[twp[ws], ta2[i]], W=[PSt[by]])

                    def st_acc(s_):
                        if s_ < 0:
                            return
                        gi, tp, cs, dr, gq, gd, ws, tt = tile_of(s_)
                        by = 4 + (s_ % 2)
                        ts_ = slice(tt * TW, (tt + 1) * TW)
                        P.op("dve", "tensor_tensor", out=yacc[:, ts_], in0=PS[by][:, :], in1=yacc[:, ts_], op=ALU.add,
                             R=[PSt[by], tya[tt]], W=[tya[tt]])
                        if s_ % 128 == 127:
                            for t2 in range(NT):
                                t2s = slice(t2 * TW, (t2 + 1) * TW)
                                P.op("act", "activation", out=gl[cs][:, t2s], in_=yacc[:, t2s], func=AF.Gelu_apprx_tanh,
                                     R=[tya[t2]], W=[tgl[cs][t2]])
                            if cs + 1 < 4:
                                y_init(cs + 1)

                    def y_init(cs):
                        for t2 in range(NT):
                            t2s = slice(t2 * TW, (t2 + 1) * TW)
                            P.op("act", "activation", out=yacc[:, t2s], in_=zs5[cs][:, t2s], func=AF.Identity, scale=s5d[:, cs:cs + 1],
                                 R=[tzs5[cs][t2], tprm], W=[tya[t2]])

                    for pc in range(8):
                        gen_piece(0, pc)
                    gen_piece(1, 0)
                    y_init(0)
                    st_bu(0)
                    st_bu(1)
                    st_a(0)
                    for s_ in range(NS):
                        gi_, tp_ = steps[s_]
                        st_bu(s_ + 2)
                        st_a(s_ + 1)
                        st_carry(s_)
                        st_b(s_)
                        st_swap(s_)
                        st_c(s_)
                        st_acc(s_ - 1)
                        if tp_ < NT - 1:
                            gen_piece(gi_ + 1, tp_ + 1)
                        else:
                            gen_piece(gi_ + 2, 0)
                    st_acc(NS - 1)
                    sg = [sbuf(st, "s5_sg%d" % i, [128, TW]) for i in range(2)]
                    ys = [sbuf(st, "s5_ys%d" % i, [128, TW], BF16) for i in range(2)]
                    tsg, tys = toks(2), toks(2)
                    n_ = 0
                    for tt in range(NT):
                        ts_ = slice(tt * TW, (tt + 1) * TW)
                        for mo in range(4):
                            b = 6 + (n_ % 2)
                            i = n_ % 2
                            n_ += 1
                            for k in range(4):
                                P.op("pe", "matmul", PS[b][:, :], lhsT=gluw[:, k, mo * 128:(mo + 1) * 128], rhs=gl[k][:, ts_],
                                     start=(k == 0), stop=(k == 3), R=[tgw, tgl[k][tt]], W=[PSt[b]])
                            P.op("act", "activation", out=sg[i][:], in_=PS[b][:, :], func=AF.Sigmoid, bias=glub[:, mo:mo + 1],
                                 R=[PSt[b], tprm], W=[tsg[i]])
                            P.op("dve", "tensor_tensor", out=ys[i][:], in0=sg[i][:], in1=gl[mo][:, ts_], op=ALU.mult,
                                 R=[tsg[i], tgl[mo][tt]], W=[tys[i]])
                            P.dma("sp", mix_d[:, mo, ts_], ys[i][:], R=[tys[i]], W=[])
                    P.barrier()
                    P.emit()
            phase_tail(l, hout_d[e])

        def phase_odd(l):
            o = l // 2
            with ExitStack() as lst:
                XW = S + 4
                xr = [sbuf(lst, "xr%d" % i, [128, XW], BF16) for i in range(8)]
                txr = [Tok() for _ in range(8)]
                with ExitStack() as st:
                    win = sbuf(st, "rwin", [128, 8, 2048], BF16)
                    t_winp = toks(4)
                    wsrc = rin_d[o].rearrange("(k p) n -> p k n", p=128)
                    for q4 in range(4):
                        P.dma("pool", win[:, :, q4 * 512:(q4 + 1) * 512], wsrc[:, :, q4 * 512:(q4 + 1) * 512], W=[t_winp[q4]])
                    for c in range(8):
                        P.op("pool", "memset", xr[c][:, 0:2], 0.0, W=[txr[c]])
                        P.op("pool", "memset", xr[c][:, S + 2:S + 4], 0.0, W=[txr[c]])
                    hh = sbuf(st, "hh", [128, 8, TW])
                    thh = Tok()
                    u = sbuf(st, "u", [128, 8, TW], BF16)
                    tu = Tok()
                    gs = [sbuf(st, "gs%d" % i, [128, TW], BF16) for i in range(3)]
                    tgs = toks(3)
                    nsc = norm_scratch(st, TW)
                    n_ = 0
                    for tt in range(NT):
                        ts_ = slice(tt * TW, (tt + 1) * TW)
                        P.dma("sp", hh[:], h_d[:, :, ts_], W=[thh])
                        norm_tile(nsc, hh, thh, TW, u, tu, lambda k: A1[:, l * 8 + k:l * 8 + k + 1],
                                  lambda k: modcol(l, 0, k), PS[0], PSt[0])
                        for c in range(16):
                            b = 1 + (c % 4)
                            for k in range(8):
                                P.op("pe", "matmul", PS[b][:, :], lhsT=win[:, k, c * 128:(c + 1) * 128], rhs=u[:, k, :],
                                     start=(k == 0), stop=(k == 7), R=[t_winp[c // 4], tu], W=[PSt[b]])
                            if c < 8:
                                i = n_ % 3
                                n_ += 1
                                P.op("act", "activation", out=gs[i][:], in_=PS[b][:, :], func=AF.Copy, R=[PSt[b]], W=[tgs[i]])
                                P.dma("sp", gate_d[:, c, ts_], gs[i][:], R=[tgs[i]], W=[])
                            else:
                                P.op("dve", "tensor_copy", out=xr[c - 8][:, 2 + tt * TW:2 + (tt + 1) * TW], in_=PS[b][:, :],
                                     R=[PSt[b]], W=[txr[c - 8]])
                    P.barrier()
                    P.emit()
                import os as _os
                with ExitStack() as st:
                    _nchunk = int(_os.environ.get("DBG_O2_CHUNKS", "8"))
                    cw = sbuf(st, "cw", [128, 32])
                    cbias = sbuf(st, "cbias", [128, 8])
                    rab = sbuf(st, "rab", [128, 16])
                    ixb = sbuf(st, "ixb", [128, 16])
                    lam = sbuf(st, "lam", [128, 16])
                    nsp = sbuf(st, "nsp", [128, 16])
                    nsp2 = sbuf(st, "nsp2", [128, 16])
                    tp_ = Tok()
                    for (dst, src) in ((cw, convw_d[o]), (cbias, convb_d[o]), (rab, rab_d[o]), (ixb, ixb_d[o]), (lam, rlam_d[o])):
                        P.dma("sp", dst[:], src, W=[tp_])
                    rabd = sbuf(st, "rabd", [128, 16, 128], BF16)
                    ixbd = sbuf(st, "ixbd", [128, 16, 128], BF16)
                    for hf in range(2):
                        P.dma("pool", rabd[:, hf * 8:(hf + 1) * 8, :],
                              rabd_d[o].rearrange("p (c m) -> p c m", m=128)[:, hf * 8:(hf + 1) * 8, :], W=[tp_])
                        P.dma("pool", ixbd[:, hf * 8:(hf + 1) * 8, :],
                              ixbd_d[o].rearrange("p (c m) -> p c m", m=128)[:, hf * 8:(hf + 1) * 8, :], W=[tp_])
                    P.op("act", "activation", out=nsp[:], in_=lam[:], func=AF.Exp, scale=-1.0, R=[tp_], W=[tp_])
                    P.op("act", "activation", out=nsp[:], in_=nsp[:], func=AF.Ln, bias=one_c[:], R=[tp_] + CT, W=[tp_])
                    P.op("dve", "tensor_scalar", out=nsp2[:], in0=nsp[:], scalar1=-16.0, scalar2=None, op0=ALU.mult, R=[tp_], W=[tp_])
                    P.op("dve", "tensor_scalar", out=nsp[:], in0=nsp[:], scalar1=-8.0, scalar2=None, op0=ALU.mult, R=[tp_], W=[tp_])
                    xc = sbuf(st, "xc", [128, S])
                    xcb = sbuf(st, "xcb", [128, S], BF16)
                    Rb = sbuf(st, "Rb", [128, S])
                    Ib = sbuf(st, "Ib", [128, S])
                    Ab = sbuf(st, "Ab", [128, S])
                    Yb = sbuf(st, "Yb", [128, S])
                    txc, txcb = Tok(), toks(NT)
                    tR, tI, tA_, tY = toks(NT), toks(NT), toks(NT), toks(NT)
                    gtile = [sbuf(st, "gt%d" % i, [128, S], BF16) for i in range(2)]
                    tgt_ = toks(2)
                    ot = [sbuf(st, "ot%d" % i, [128, TW], BF16) for i in range(2)]
                    tot = toks(2)
                    gg = [sbuf(st, "gg%d" % i, [128, TW]) for i in range(2)]
                    tgg = toks(2)
                    n_ = 0
                    for c in range(_nchunk):
                        P.dma("sp", gtile[c % 2][:], gate_d[:, c, :], W=[tgt_[c % 2]])
                        P.op("dve", "tensor_scalar", out=xc[:], in0=xr[c][:, 0:S], scalar1=cw[:, c * 4:c * 4 + 1],
                             scalar2=cbias[:, c:c + 1], op0=ALU.mult, op1=ALU.add, R=[txr[c], tp_], W=[txc])
                        for j in range(1, 4):
                            P.op("dve", "scalar_tensor_tensor", out=xc[:], in0=xr[c][:, j:j + S], scalar=cw[:, c * 4 + j:c * 4 + j + 1],
                                 in1=xc[:], op0=ALU.mult, op1=ALU.add, R=[txr[c], tp_, txc], W=[txc])
                        for tt in range(NT):
                            ts_ = slice(tt * TW, (tt + 1) * TW)
                            P.op("act", "activation", out=xcb[:, ts_], in_=xc[:, ts_], func=AF.Copy, R=[txc], W=[txcb[tt]])
                        for dr in range(2):
                            dc = dr * 8 + c
                            for tt in range(NT):
                                ts_ = slice(tt * TW, (tt + 1) * TW)
                                b = 1 + (n_ % 3)
                                b2_ = 4 + (n_ % 3)
                                n_ += 1
                                P.op("pe", "matmul", PS[b][:, :], lhsT=rabd[:, dc, :], rhs=xcb[:, ts_], start=True, stop=True,
                                     R=[tp_, txcb[tt]], W=[PSt[b]])
                                P.op("pe", "matmul", PS[b2_][:, :], lhsT=ixbd[:, dc, :], rhs=xcb[:, ts_], start=True, stop=True,
                                     R=[tp_, txcb[tt]], W=[PSt[b2_]])
                                P.op("act", "activation", out=Rb[:, ts_], in_=PS[b][:, :], func=AF.Sigmoid, bias=rab[:, dc:dc + 1],
                                     R=[PSt[b], tp_], W=[tR[tt]])
                                P.op("act", "activation", out=Ib[:, ts_], in_=PS[b2_][:, :], func=AF.Sigmoid, bias=ixb[:, dc:dc + 1],
                                     R=[PSt[b2_], tp_], W=[tI[tt]])
                            for tt in range(NT):
                                ts_ = slice(tt * TW, (tt + 1) * TW)
                                P.op("act", "activation", out=Ab[:, ts_], in_=Rb[:, ts_], func=AF.Exp, scale=nsp[:, dc:dc + 1],
                                     R=[tR[tt], tp_], W=[tA_[tt]])
                                P.op("act", "activation", out=Rb[:, ts_], in_=Rb[:, ts_], func=AF.Exp, scale=nsp2[:, dc:dc + 1],
                                     R=[tR[tt], tp_], W=[tR[tt]])
                            for tt in range(NT):
                                ts_ = slice(tt * TW, (tt + 1) * TW)
                                P.op("act", "activation", out=Rb[:, ts_], in_=Rb[:, ts_], func=AF.Sqrt, scale=-1.0, bias=one_c[:],
                                     R=[tR[tt]] + CT, W=[tR[tt]])
                                P.op("pool", "tensor_tensor", out=Ib[:, ts_], in0=Ib[:, ts_], in1=xc[:, ts_], op=ALU.mult,
                                     R=[tI[tt], txc], W=[tI[tt]])
                                P.op("dve", "tensor_tensor", out=Ib[:, ts_], in0=Ib[:, ts_], in1=Rb[:, ts_], op=ALU.mult,
                                     R=[tI[tt], tR[tt]], W=[tI[tt]])
                            order = range(NT) if dr == 0 else range(NT - 1, -1, -1)
                            prev = None
                            dst = Yb if dr == 0 else Rb
                            tdst = tY if dr == 0 else tR
                            for tt in order:
                                ts_ = slice(tt * TW, (tt + 1) * TW)
                                if dr == 0:
                                    init = 0.0 if prev is None else dst[:, prev * TW + TW - 1:prev * TW + TW]
                                    P.op("dve", "tensor_tensor_scan", out=dst[:, ts_], data0=Ab[:, ts_], data1=Ib[:, ts_], initial=init,
                                         op0=ALU.mult, op1=ALU.add,
                                         R=[tA_[tt], tI[tt]] + ([tdst[prev]] if prev is not None else []), W=[tdst[tt]])
                                else:
                                    init = 0.0 if prev is None else dst[:, prev * TW:prev * TW + 1]
                                    P.op("dve", "tensor_tensor_scan", out=dst[:, ts_][:, ::-1], data0=Ab[:, ts_][:, ::-1],
                                         data1=Ib[:, ts_][:, ::-1], initial=init, op0=ALU.mult, op1=ALU.add,
                                         R=[tA_[tt], tI[tt]] + ([tdst[prev]] if prev is not None else []), W=[tdst[tt]])
                                prev = tt
                        for tt in range(NT):
                            ts_ = slice(tt * TW, (tt + 1) * TW)
                            i = tt % 2
                            P.op("act", "activation", out=gg[i][:], in_=gtile[c % 2][:, ts_], func=AF.Gelu_apprx_tanh,
                                 R=[tgt_[c % 2]], W=[tgg[i]])
                            P.op("pool", "tensor_tensor", out=Yb[:, ts_], in0=Yb[:, ts_], in1=Rb[:, ts_], op=ALU.add,
                                 R=[tY[tt], tR[tt]], W=[tY[tt]])
                            P.op("dve", "tensor_tensor", out=ot[i][:], in0=Yb[:, ts_], in1=gg[i][:], op=ALU.mult,
                                 R=[tY[tt], tgg[i]], W=[tot[i]])
                            P.dma("sp", mix_d[:, c, ts_], ot[i][:], R=[tot[i]], W=[])
                    P.barrier()
                    P.emit()
            phase_tail(l, rout_d[o])

        for l in range(n_layers):
            if l % 2 == 0:
                phase_even(l)
            else:
                phase_odd(l)

        with ExitStack() as st:
            hh = [sbuf(st, "fh%d" % i, [128, 8, TW]) for i in range(2)]
            thh = toks(2)
            hn = sbuf(st, "fhn", [128, 8, TW])
            thn = Tok()
            ob = [sbuf(st, "fo%d" % i, [128, D]) for i in range(2)]
            tob = toks(2)
            nsc = norm_scratch(st, TW)
            tfin = []
            n_ = 0
            for tt in range(NT):
                ts_ = slice(tt * TW, (tt + 1) * TW)
                h, th = hh[tt % 2], thh[tt % 2]
                P.dma("sp", h[:], h_d[:, :, ts_], W=[th])
                norm_tile(nsc, h, th, TW, hn, thn, lambda k: fnw[:, k:k + 1], None, PS[0], PSt[0])
                for j in range(4):
                    o_, to_ = ob[n_ % 2], tob[n_ % 2]
                    for b in range(2):
                        bank = 1 + 2 * (n_ % 2) + b
                        for kk in range(4):
                            k = 4 * b + kk
                            P.op("pe", "transpose", out=PS[bank][:, kk * 128:(kk + 1) * 128], in_=hn[:, k, j * 128:(j + 1) * 128],
                                 identity=ident, R=[thn] + CT, W=[PSt[bank]])
                        if b == 0:
                            P.op("act", "activation", out=o_[:, 0:512], in_=PS[bank][:, :], func=AF.Copy, R=[PSt[bank]], W=[to_])
                        else:
                            P.op("dve", "tensor_copy", out=o_[:, 512:1024], in_=PS[bank][:, :], R=[PSt[bank]], W=[to_])
                    n_ += 1
                    tk = Tok()
                    r0 = tt * TW + j * 128
                    P.dma("sp", out_d[r0:r0 + 128, :], o_[:], R=[to_], W=[tk])
                    tfin.append(tk)
            for tk in tfin:
                P._wait("sp", tk.w[0], tk.w[1])
            P.barrier()
            P.emit()
        print("[kernel] program built: %d instructions" % P.ninst)
    return nc


def _fm(v, k):
    v = np.asarray(v, np.float32)
    lead = v.shape[:-1]
    v = v.reshape(lead + (k, 128))
    v = np.moveaxis(v, -1, 0)
    return np.ascontiguousarray(v)


def _rope_tables():
    rows = S // 64
    row_idx = np.repeat(np.arange(rows, dtype=np.float64), 64)
    col_idx = np.tile(np.arange(64, dtype=np.float64), rows)
    inv_freq = 10000.0 ** (-np.arange(16, dtype=np.float64) / 16)
    cos = np.zeros((128, S), np.float64)
    sin = np.zeros((128, S), np.float64)
    for p in range(128):
        d = p % 64
        a = d // 32
        b = (d // 16) % 2
        f = d % 16
        ang = (row_idx if a == 0 else col_idx) * inv_freq[f]
        cos[p] = np.cos(ang)
        sin[p] = (-np.sin(ang)) if b == 0 else np.sin(ang)
    return cos.astype(np.float32), sin.astype(np.float32)


def _consts():
    c = np.zeros((128, 400), np.float32)
    c[:, 0:128] = np.eye(128, dtype=np.float32)
    c[0:64, 128:192] = 1.0
    c[64:128, 192:256] = 1.0
    for k in range(128):
        c[k, 256 + (k + 64) % 128] = 1.0
    c[0:64, 384] = -1.0
    c[64:128, 384] = 1.0
    c[0:64, 385] = 1.0
    c[64:128, 385] = -1.0
    for g in range(8):
        c[16 * g:16 * g + 16, 386 + g] = 1.0
    return c


def prepare_shared(inp):
    f = lambda a: np.ascontiguousarray(np.asarray(a, np.float32))
    sh = {}
    sh["ada_w"] = f(inp["ada_w"])
    sh["ada_b_fm"] = np.ascontiguousarray(f(inp["ada_b"]).reshape(4, 48, 128).transpose(2, 0, 1).reshape(128, 192))
    sh["norm_w_fm"] = np.ascontiguousarray(f(inp["norm_w"]).reshape(4, 2, 8, 128).transpose(3, 0, 1, 2).reshape(128, 64))
    sh["fnw_fm"] = np.ascontiguousarray(f(inp["final_norm_w"]).reshape(8, 128).T)
    sh["mlp_w1"] = f(inp["mlp_w1"])
    sh["mlp_w2"] = f(inp["mlp_w2"])
    idx = np.arange(128)
    perm = idx ^ 16
    cols = list(range(512))
    for j in range(4):
        cols += list(512 + 128 * j + idx)
    for j in range(4):
        cols += list(512 + 128 * j + perm)
    for kv in range(2):
        cols += list(1024 + 64 * kv + (idx % 64))
    for kv in range(2):
        cols += list(1024 + 64 * kv + (perm % 64))
    cols += list(range(1152, 1280))
    cols = np.asarray(cols)
    assert cols.shape[0] == 2176
    sh["hyb_w_in_ext"] = np.ascontiguousarray(f(inp["hyb_w_in"])[:, :, cols])
    sh["hyb_w_out"] = f(inp["hyb_w_out"])

    def stk(a):
        a = f(a).transpose(0, 3, 1, 2).reshape(2, 64, 64)
        return np.ascontiguousarray(np.concatenate([a, a], axis=1))
    sh["s5_lamre"] = stk(inp["s5_lam_re"])
    sh["s5_lamim"] = stk(inp["s5_lam_im"])
    sh["s5_logdt"] = np.ascontiguousarray(np.broadcast_to(f(inp["s5_log_dt"]).reshape(2, 1, 64), (2, 128, 64)))
    bre = f(inp["s5_b_re"]).transpose(0, 3, 1, 2, 4).reshape(2, 64, 1024)
    bim = f(inp["s5_b_im"]).transpose(0, 3, 1, 2, 4).reshape(2, 64, 1024)
    sh["s5_b1"] = np.ascontiguousarray(np.concatenate([bre, bim], axis=1))
    sh["s5_b2"] = np.ascontiguousarray(np.concatenate([bim, bre], axis=1))
    cre = f(inp["s5_c_re"]).transpose(0, 4, 1, 2, 3).reshape(2, 64, 1024)
    cim = f(inp["s5_c_im"]).transpose(0, 4, 1, 2, 3).reshape(2, 64, 1024)
    sh["s5_c1"] = np.ascontiguousarray(np.concatenate([cre, cim], axis=1))
    sh["s5_c2"] = np.ascontiguousarray(np.concatenate([cim, cre], axis=1))
    sh["s5_d_fm"] = np.ascontiguousarray(f(inp["s5_d"]).reshape(2, 4, 128).transpose(0, 2, 1))
    sh["s5_glu_w"] = f(inp["s5_glu_w"])
    sh["s5_glu_b_fm"] = np.ascontiguousarray(f(inp["s5_glu_b"]).reshape(2, 4, 128).transpose(0, 2, 1))
    qn = f(inp["attn_q_norm"])
    kn = f(inp["attn_k_norm"])
    d = idx % 64
    dp = (idx ^ 16) % 64
    sh["qkn_fm"] = np.ascontiguousarray(np.stack([qn[:, d], qn[:, dp], kn[:, d], kn[:, dp]], axis=-1))
    sh["rec_w_in"] = f(inp["rec_w_in"])
    sh["rec_w_out"] = f(inp["rec_w_out"])
    sh["rec_conv_fm"] = np.ascontiguousarray(f(inp["rec_conv_w"]).reshape(2, 4, 8, 128).transpose(0, 3, 2, 1).reshape(2, 128, 32))
    sh["rec_convb_fm"] = np.ascontiguousarray(f(inp["rec_conv_b"]).reshape(2, 8, 128).transpose(0, 2, 1))

    def bd(w):
        w = f(w)
        o = np.zeros((2, 128, 2, 8, 128), np.float32)
        for c in range(8):
            for hl in range(2):
                o[:, 64 * hl:64 * hl + 64, :, c, 64 * hl:64 * hl + 64] = w[:, :, 2 * c + hl].transpose(0, 2, 1, 3)
        return np.ascontiguousarray(o.reshape(2, 128, 2048))
    sh["rec_ra_bd"] = bd(inp["rec_ra_w"])
    sh["rec_ix_bd"] = bd(inp["rec_ix_w"])

    def fm2(a):
        return np.ascontiguousarray(f(a).reshape(2, 2, 8, 128).transpose(0, 3, 1, 2).reshape(2, 128, 16))
    sh["rec_rab_fm"] = fm2(inp["rec_ra_b"])
    sh["rec_ixb_fm"] = fm2(inp["rec_ix_b"])
    sh["rec_lam_fm"] = fm2(inp["rec_lam"])
    rc, rs = _rope_tables()
    sh["rope_cos"] = rc
    sh["rope_sin"] = rs
    sh["consts"] = _consts()
    return sh


_CACHE = {}


def run(inputs, core_batches, n_layers=4, trace=False):
    key = n_layers
    if key not in _CACHE:
        _CACHE[key] = build_program(n_layers)
    nc = _CACHE[key]
    sh = prepare_shared(inputs)
    x = np.asarray(inputs["x"], np.float32)
    c = np.asarray(inputs["c"], np.float32)
    in_maps = []
    for b in core_batches:
        m = dict(sh)
        m["x"] = np.ascontiguousarray(x[b])
        m["c_fm"] = np.ascontiguousarray(c[b].reshape(8, 128).T)
        in_maps.append(m)
    res = run_bass_kernel_spmd(nc, in_maps, core_ids=list(range(len(core_batches))), **({"trace": True} if trace else {}))
    outs = np.stack([np.asarray(r["out"], np.float32) for r in res.results], axis=0)
    return outs, res


def kernel(**inputs):
    outs, _ = run(inputs, list(range(8)), 4)
    return outs.astype(np.float32)
```

```python
import math
import numpy as np
from contextlib import ExitStack
import concourse.bass as bass
import concourse.mybir as mybir
from concourse.bass_utils import run_bass_kernel_spmd

F32 = mybir.dt.float32
BF16 = mybir.dt.bfloat16
I32 = mybir.dt.int32
AF = mybir.ActivationFunctionType
ALU = mybir.AluOpType

S = 4096
D = 1024
EPS = 1e-6
NT = 8
TW = 512
PI = math.pi


class Tok:
    __slots__ = ("w", "r", "x")

    def __init__(self, x=False):
        self.w = None
        self.r = {}
        self.x = x


def toks(n):
    return [Tok() for _ in range(n)]


class Prog:
    ENGS = ("pe", "act", "dve", "pool", "sp")

    def __init__(self, nc, n_dma_sems=12):
        self.nc = nc
        self.ops = {e: [] for e in self.ENGS}
        self.cnt = {}
        self.sems = {}
        self.waited = {e: {} for e in self.ENGS}
        self.n_dma_sems = n_dma_sems
        self.dma_rr = {}
        self.ninst = 0

    def alloc_sems(self, stack):
        for e in ("pe", "act", "dve", "pool"):
            self.sems[e] = stack.enter_context(self.nc.semaphore("s_" + e))
            self.cnt[e] = 0
        for q in ("sp", "pool"):
            self.dma_rr[q] = 0
            for i in range(self.n_dma_sems):
                k = "d_%s_%d" % (q, i)
                self.sems[k] = stack.enter_context(self.nc.semaphore(k))
                self.cnt[k] = 0

    def _wait(self, E, key, val):
        if val <= 0 or self.waited[E].get(key, 0) >= val:
            return
        self.waited[E][key] = val
        sem = self.sems[key]
        self.ops[E].append(lambda eng, sem=sem, val=val: eng.wait_ge(sem, val))
        self.ninst += 1

    @staticmethod
    def _deps(reads, writes):
        need = {}
        for t in reads:
            if t.w is not None:
                k, v = t.w
                if need.get(k, 0) < v:
                    need[k] = v
        for t in writes:
            if t.w is not None:
                k, v = t.w
                if need.get(k, 0) < v:
                    need[k] = v
            for k, v in t.r.items():
                if need.get(k, 0) < v:
                    need[k] = v
        return need

    def op(self, E, meth, *args, R=(), W=(), **kw):
        if any(t.x for t in R):
            W = list(W) + [t for t in R if t.x]
            R = [t for t in R if not t.x]
        need = self._deps(R, W)
        for k, v in need.items():
            if k == E and E == "pe":
                continue
            self._wait(E, k, v)
        self.cnt[E] += 1
        idx = self.cnt[E]
        sem = self.sems[E]
        self.ops[E].append(lambda eng, meth=meth, args=args, kw=kw, sem=sem:
                           getattr(eng, meth)(*args, **kw).then_inc(sem, 1))
        self.ninst += 1
        for t in R:
            t.r[E] = idx
        for t in W:
            t.w = (E, idx)
            t.r = {}

    def dma(self, Q, out, in_, R=(), W=()):
        need = self._deps(R, W)
        for k, v in need.items():
            self._wait(Q, k, v)
        i = self.dma_rr[Q]
        self.dma_rr[Q] = (i + 1) % self.n_dma_sems
        key = "d_%s_%d" % (Q, i)
        self._wait(Q, key, self.cnt[key])
        self.cnt[key] += 16
        val = self.cnt[key]
        sem = self.sems[key]
        self.ops[Q].append(lambda eng, out=out, in_=in_, sem=sem: eng.dma_start(out=out, in_=in_).then_inc(sem, 16))
        self.ninst += 1
        for t in R:
            t.r[key] = val
        for t in W:
            t.w = (key, val)
            t.r = {}

    def barrier(self):
        for E in self.ENGS:
            for key, v in self.cnt.items():
                if key != E:
                    self._wait(E, key, v)

    def emit(self, name=None):
        nc = self.nc
        ops = self.ops
        self.ops = {e: [] for e in self.ENGS}
        self.nphase = getattr(self, "nphase", 0) + 1
        with nc.named_scope("ph%02d" % self.nphase), nc.Block() as block:
            @block.tensor
            def _(eng):
                for f in ops["pe"]:
                    f(eng)

            @block.scalar
            def _(eng):
                for f in ops["act"]:
                    f(eng)

            @block.vector
            def _(eng):
                for f in ops["dve"]:
                    f(eng)

            @block.gpsimd
            def _(eng):
                for f in ops["pool"]:
                    f(eng)

            @block.sync
            def _(eng):
                for f in ops["sp"]:
                    f(eng)


def build_program(n_layers=4):
    nc = bass.Bass("TRN2", target_bir_lowering=False)

    def din(name, shape):
        return nc.dram_tensor(name, list(shape), F32, kind="ExternalInput").ap()

    x_d = din("x", [S, D])
    c_d = din("c_fm", [128, 8])
    adaw_d = din("ada_w", [4, 1024, 6144])
    adab_d = din("ada_b_fm", [128, 192])
    normw_d = din("norm_w_fm", [128, 64])
    fnw_d = din("fnw_fm", [128, 8])
    w1_d = din("mlp_w1", [4, 1024, 4096])
    w2_d = din("mlp_w2", [4, 4096, 1024])
    hin_d = din("hyb_w_in_ext", [2, 1024, 2176])
    hout_d = din("hyb_w_out", [2, 1024, 1024])
    lamre_d = din("s5_lamre", [2, 128, 64])
    lamim_d = din("s5_lamim", [2, 128, 64])
    logdt_d = din("s5_logdt", [2, 128, 64])
    sb1_d = din("s5_b1", [2, 128, 1024])
    sb2_d = din("s5_b2", [2, 128, 1024])
    sc1_d = din("s5_c1", [2, 128, 1024])
    sc2_d = din("s5_c2", [2, 128, 1024])
    s5d_d = din("s5_d_fm", [2, 128, 4])
    gluw_d = din("s5_glu_w", [2, 512, 512])
    glub_d = din("s5_glu_b_fm", [2, 128, 4])
    qkn_d = din("qkn_fm", [2, 128, 4])
    rin_d = din("rec_w_in", [2, 1024, 2048])
    rout_d = din("rec_w_out", [2, 1024, 1024])
    convw_d = din("rec_conv_fm", [2, 128, 32])
    convb_d = din("rec_convb_fm", [2, 128, 8])
    rabd_d = din("rec_ra_bd", [2, 128, 2048])
    ixbd_d = din("rec_ix_bd", [2, 128, 2048])
    rab_d = din("rec_rab_fm", [2, 128, 16])
    ixb_d = din("rec_ixb_fm", [2, 128, 16])
    rlam_d = din("rec_lam_fm", [2, 128, 16])
    rope_c_d = din("rope_cos", [128, S])
    rope_s_d = din("rope_sin", [128, S])
    const_d = din("consts", [128, 400])
    out_d = nc.dram_tensor("out", [S, D], F32, kind="ExternalOutput").ap()

    h_d = nc.dram_tensor("h_scr", [128, 8, S], F32, kind="Internal").ap()
    mix_d = nc.dram_tensor("mix_scr", [128, 8, S], BF16, kind="Internal").ap()
    gate_d = nc.dram_tensor("gate_scr", [128, 8, S], BF16, kind="Internal").ap()

    with ExitStack() as g:
        P = Prog(nc)
        P.alloc_sems(g)

        def gsb(name, shape, dt=F32):
            return g.enter_context(nc.sbuf_tensor(name, list(shape), dt))

        PS = [g.enter_context(nc.psum_tensor("ps%d" % i, [128, 512], F32)) for i in range(8)]
        PSt = [Tok(x=True) for _ in range(8)]

        cst = gsb("cst", [128, 400])
        t_cst = Tok()
        P.dma("sp", cst[:], const_d, W=[t_cst])
        ident = cst[:, 0:128]
        swapm = cst[:, 256:384]
        sgn = cst[:, 384:385]
        nsgn = cst[:, 385:386]
        gmask = cst[:, 386:394]
        ones_f = gsb("ones_f", [128, 128])
        ones_b = gsb("ones_b", [128, 128], BF16)
        bones_b = gsb("bones_b", [128, 128], BF16)
        eps_c = gsb("eps_c", [128, 1])
        one_c = gsb("one_c", [128, 1])
        t_c2 = Tok()
        P.op("pool", "memset", ones_f[:], 1.0, W=[t_c2])
        P.op("pool", "memset", ones_b[:], 1.0, W=[t_c2])
        P.op("pool", "memset", eps_c[:], EPS, W=[t_c2])
        P.op("pool", "memset", one_c[:], 1.0, W=[t_c2])
        P.op("pool", "tensor_copy", out=bones_b[:], in_=cst[:, 128:256], R=[t_cst], W=[t_c2])
        swapm_b = gsb("swapm_b", [128, 128], BF16)
        P.op("pool", "tensor_copy", out=swapm_b[:], in_=cst[:, 256:384], R=[t_cst], W=[t_c2])
        CT = [t_cst, t_c2]

        modall = gsb("modall", [128, 192])
        normw = gsb("normw", [128, 64])
        fnw = gsb("fnw", [128, 8])
        A1 = gsb("A1", [128, 32])
        A2 = gsb("A2", [128, 32])
        t_mod = Tok()

        uid = [0]

        def sbuf(st, name, shape, dt=F32):
            uid[0] += 1
            return st.enter_context(nc.sbuf_tensor("%s_u%d" % (name, uid[0]), list(shape), dt))

        with ExitStack() as st:
            cf = sbuf(st, "cf", [128, 8])
            cb = sbuf(st, "cb", [128, 8], BF16)
            adab = sbuf(st, "adab", [128, 192])
            t_cf, t_cb, t_ab, t_nw = toks(4)
            P.dma("sp", cf[:], c_d, W=[t_cf])
            P.dma("sp", adab[:], adab_d, W=[t_ab])
            P.dma("sp", normw[:], normw_d, W=[t_nw])
            P.dma("sp", fnw[:], fnw_d, W=[t_nw])
            P.op("act", "activation", out=cb[:], in_=cf[:], func=AF.Silu, R=[t_cf], W=[t_cb])
            NB = 3
            wt = [sbuf(st, "adaw%d" % i, [128, 8, 512], BF16) for i in range(NB)]
            twt = toks(NB)
            n = 0
            for l in range(4):
                src_l = adaw_d[l].rearrange("(k p) n -> p k n", p=128)
                for blk in range(12):
                    i = n % NB
                    n += 1
                    P.dma("pool", wt[i][:], src_l[:, :, blk * 512:(blk + 1) * 512], W=[twt[i]])
                    for j in range(4):
                        col = l * 48 + blk * 4 + j
                        for k in range(8):
                            P.op("pe", "matmul", PS[0][:, col:col + 1], lhsT=wt[i][:, k, j * 128:(j + 1) * 128],
                                 rhs=cb[:, k:k + 1], start=(k == 0), stop=(k == 7), R=[twt[i], t_cb], W=[PSt[0]])
            P.op("dve", "tensor_tensor", out=modall[:], in0=PS[0][:, 0:192], in1=adab[:], op=ALU.add,
                 R=[PSt[0], t_ab], W=[t_mod])
            for l in range(4):
                for (A, sc0, nw0) in ((A1, 8, 0), (A2, 32, 8)):
                    P.op("dve", "tensor_scalar", out=A[:, l * 8:(l + 1) * 8], in0=modall[:, l * 48 + sc0:l * 48 + sc0 + 8],
                         scalar1=1.0, scalar2=None, op0=ALU.add, R=[t_mod], W=[t_mod])
                    P.op("dve", "tensor_tensor", out=A[:, l * 8:(l + 1) * 8], in0=A[:, l * 8:(l + 1) * 8],
                         in1=normw[:, l * 16 + nw0:l * 16 + nw0 + 8], op=ALU.mult, R=[t_mod, t_nw], W=[t_mod])
            P.barrier()
            P.emit()

        def modcol(l, which, k):
            c = l * 48 + which * 8 + k
            return modall[:, c:c + 1]

        def norm_tile(st_tiles, h, th, N, u, tu, Acol, Bcol, pss, tpss):
            sq, tsq, srt, tsrt, rstd, trstd, tmp, ttmp = st_tiles
            for k in range(8):
                P.op("act", "activation", out=sq[k % 2][:, :N], in_=h[:, k, :N], func=AF.Square, R=[th], W=[tsq[k % 2]])
                P.op("pe", "matmul", pss[:, :N], lhsT=ones_b[:], rhs=sq[k % 2][:, :N], start=(k == 0), stop=(k == 7),
                     R=[tsq[k % 2]] + CT, W=[tpss])
            P.op("act", "activation", out=srt[:, :N], in_=pss[:, :N], func=AF.Sqrt, scale=1.0 / D, bias=eps_c[:],
                 R=[tpss] + CT, W=[tsrt])
            P.op("dve", "reciprocal", out=rstd[:, :N], in_=srt[:, :N], R=[tsrt], W=[trstd])
            for k in range(8):
                P.op("dve", "tensor_tensor", out=tmp[k % 2][:, :N], in0=h[:, k, :N], in1=rstd[:, :N], op=ALU.mult,
                     R=[th, trstd], W=[ttmp[k % 2]])
                if Bcol is None:
                    P.op("act", "activation", out=u[:, k, :N], in_=tmp[k % 2][:, :N], func=AF.Identity, scale=Acol(k),
                         R=[ttmp[k % 2], t_mod], W=[tu])
                else:
                    P.op("act", "activation", out=u[:, k, :N], in_=tmp[k % 2][:, :N], func=AF.Identity, scale=Acol(k),
                         bias=Bcol(k), R=[ttmp[k % 2], t_mod], W=[tu])

        def norm_scratch(st, N):
            sq = [sbuf(st, "n_sq%d" % i, [128, N], BF16) for i in range(2)]
            srt = sbuf(st, "n_srt", [128, N])
            rstd = sbuf(st, "n_rstd", [128, N])
            tmp = [sbuf(st, "n_tmp%d" % i, [128, N]) for i in range(2)]
            return (sq, toks(2), srt, Tok(), rstd, Tok(), tmp, toks(2))

        with ExitStack() as st:
            xt = [sbuf(st, "xt%d" % i, [128, D]) for i in range(2)]
            txt = toks(2)
            stg = [sbuf(st, "stg%d" % i, [128, 8, TW]) for i in range(2)]
            tstg = toks(2)
            for tt in range(NT):
                sg, tsg = stg[tt % 2], tstg[tt % 2]
                for j in range(4):
                    i = tt * 4 + j
                    xx, txx = xt[i % 2], txt[i % 2]
                    P.dma("sp", xx[:], x_d[i * 128:(i + 1) * 128, :], W=[txx])
                    for b in range(2):
                        bank = 2 * (i % 2) + b
                        for kk in range(4):
                            k = 4 * b + kk
                            P.op("pe", "transpose", out=PS[bank][:, kk * 128:(kk + 1) * 128], in_=xx[:, k * 128:(k + 1) * 128],
                                 identity=ident, R=[txx] + CT, W=[PSt[bank]])
                        P.op("act" if b == 0 else "dve", *(("activation",) if b == 0 else ("tensor_copy",)),
                             out=sg[:, 4 * b:4 * b + 4, j * 128:(j + 1) * 128],
                             in_=PS[bank][:, :].rearrange("p (k t) -> p k t", t=128),
                             **({"func": AF.Copy} if b == 0 else {}), R=[PSt[bank]], W=[tsg])
                P.dma("sp", h_d[:, :, tt * TW:(tt + 1) * TW], sg[:], R=[tsg], W=[])
            P.barrier()
            P.emit()

        def phase_tail(l, wout_src):
            TT = 256
            with ExitStack() as st:
                wo = sbuf(st, "wo", [128, 8, 1024], BF16)
                w1 = sbuf(st, "w1", [128, 8, 4096], BF16)
                w2 = sbuf(st, "w2", [128, 32, 1024], BF16)
                t_wo = Tok()
                t_w1, t_w2 = toks(4), toks(4)
                P.dma("pool", wo[:], wout_src.rearrange("(k p) n -> p k n", p=128), W=[t_wo])
                w1s = w1_d[l].rearrange("(k p) n -> p k n", p=128)
                for q4 in range(4):
                    P.dma("pool", w1[:, :, q4 * 1024:(q4 + 1) * 1024], w1s[:, :, q4 * 1024:(q4 + 1) * 1024], W=[t_w1[q4]])
                w2s = w2_d[l].rearrange("(k p) n -> p k n", p=128)
                for q4 in range(4):
                    P.dma("pool", w2[:, q4 * 8:(q4 + 1) * 8, :], w2s[:, q4 * 8:(q4 + 1) * 8, :], W=[t_w2[q4]])
                mx = [sbuf(st, "mx%d" % i, [128, 8, TT], BF16) for i in range(2)]
                tmx = toks(2)
                hh = [sbuf(st, "hh%d" % i, [128, 8, TT]) for i in range(2)]
                thh = toks(2)
                u2s = [sbuf(st, "u2_%d" % i, [128, 8, TT], BF16) for i in range(2)]
                tu2s = toks(2)
                ff = sbuf(st, "ff", [128, 32, TT], BF16)
                tff = toks(32)
                rl = [sbuf(st, "rl%d" % i, [128, TT], BF16) for i in range(3)]
                trl = toks(3)
                nsc = norm_scratch(st, TT)
                NIT = S // TT

                def stage_a1(it):
                    t0 = it * TT
                    m, tm = mx[it % 2], tmx[it % 2]
                    h, th = hh[it % 2], thh[it % 2]
                    P.dma("sp", m[:], mix_d[:, :, t0:t0 + TT], W=[tm])
                    P.dma("sp", h[:], h_d[:, :, t0:t0 + TT], W=[th])
                    for mo in range(8):
                        b = 1 + (mo % 2)
                        for k in range(8):
                            P.op("pe", "matmul", PS[b][:, :TT], lhsT=wo[:, k, mo * 128:(mo + 1) * 128], rhs=m[:, k, :],
                                 start=(k == 0), stop=(k == 7), R=[t_wo, tm], W=[PSt[b]])
                        P.op("dve", "scalar_tensor_tensor", out=h[:, mo, :], in0=PS[b][:, :TT], scalar=modcol(l, 2, mo),
                             in1=h[:, mo, :], op0=ALU.mult, op1=ALU.add, R=[PSt[b], t_mod, th], W=[th])

                def stage_a2(it):
                    norm_tile(nsc, hh[it % 2], thh[it % 2], TT, u2s[it % 2], tu2s[it % 2],
                              lambda k: A2[:, l * 8 + k:l * 8 + k + 1], lambda k: modcol(l, 3, k), PS[0], PSt[0])

                stage_a1(0)
                stage_a2(0)
                for it in range(NIT):
                    t0 = it * TT
                    h, th = hh[it % 2], thh[it % 2]
                    u2, tu2 = u2s[it % 2], tu2s[it % 2]
                    for f in range(32):
                        b = 3 + (f % 3)
                        for k in range(8):
                            P.op("pe", "matmul", PS[b][:, :TT], lhsT=w1[:, k, f * 128:(f + 1) * 128], rhs=u2[:, k, :],
                                 start=(k == 0), stop=(k == 7), R=[t_w1[f // 8], tu2], W=[PSt[b]])
                        r, tr = rl[f % 3], trl[f % 3]
                        P.op("act", "activation", out=r[:], in_=PS[b][:, :TT], func=AF.Relu, R=[PSt[b]], W=[tr])
                        P.op("pool", "tensor_tensor", out=ff[:, f, :], in0=r[:], in1=r[:], op=ALU.mult, R=[tr], W=[tff[f]])
                    if it + 1 < NIT:
                        stage_a1(it + 1)
                    for mo in range(8):
                        if mo == 4 and it + 1 < NIT:
                            stage_a2(it + 1)
                        b = 6 + (mo % 2)
                        for f in range(32):
                            P.op("pe", "matmul", PS[b][:, :TT], lhsT=w2[:, f, mo * 128:(mo + 1) * 128], rhs=ff[:, f, :],
                                 start=(f == 0), stop=(f == 31), R=[t_w2[f // 8], tff[f]], W=[PSt[b]])
                        P.op("dve", "scalar_tensor_tensor", out=h[:, mo, :], in0=PS[b][:, :TT], scalar=modcol(l, 5, mo),
                             in1=h[:, mo, :], op0=ALU.mult, op1=ALU.add, R=[PSt[b], t_mod, th], W=[th])
                    P.dma("sp", h_d[:, :, t0:t0 + TT], h[:], R=[th], W=[])
                P.barrier()
                P.emit()

        def phase_even(l):
            e = l // 2
            with ExitStack() as lst:
                zs5 = [sbuf(lst, "zs5_%d" % i, [128, S], BF16) for i in range(4)]
                tzs5 = [toks(NT) for _ in range(4)]
                with ExitStack() as ast:
                    qb = [sbuf(ast, "q%d" % i, [128, S], BF16) for i in range(4)]
                    tq = [toks(NT) for _ in range(4)]
                    kd = [sbuf(ast, "kd%d" % i, [128, S], BF16) for i in range(2)]
                    tkd = [Tok() for _ in range(2)]
                    Vx = sbuf(ast, "Vx", [128, 32, 2, 192], BF16)
                    tVx = Tok()
                    with ExitStack() as st:
                        win = sbuf(st, "win", [128, 8, 2176], BF16)
                        t_winp = toks(4)
                        wsrc = hin_d[e].rearrange("(k p) n -> p k n", p=128)
                        for q4 in range(4):
                            P.dma("pool", win[:, :, q4 * 544:(q4 + 1) * 544], wsrc[:, :, q4 * 544:(q4 + 1) * 544], W=[t_winp[q4]])

                        def wtok(c0, n=128):
                            return [t_winp[i] for i in range(c0 // 544, (c0 + n - 1) // 544 + 1)]
                        qkn = sbuf(st, "qkn", [128, 4])
                        t_qkn = Tok()
                        P.dma("sp", qkn[:], qkn_d[e], W=[t_qkn])
                        P.op("pool", "memset", Vx[:], 0.0, W=[tVx])
                        P.op("pool", "memset", Vx[:, :, :, 64:65], 1.0, W=[tVx])
                        hh = sbuf(st, "hh", [128, 8, TW])
                        thh = Tok()
                        u = sbuf(st, "u", [128, 8, TW], BF16)
                        tu = Tok()
                        rc = [sbuf(st, "rc%d" % i, [128, TW]) for i in range(2)]
                        rs = [sbuf(st, "rs%d" % i, [128, TW]) for i in range(2)]
                        trc = toks(2)
                        sqh_ = [sbuf(st, "sqh%d" % i, [128, TW], BF16) for i in range(2)]
                        srt_ = [sbuf(st, "srt%d" % i, [128, TW]) for i in range(2)]
                        rstd_ = [sbuf(st, "rstd%d" % i, [128, TW]) for i in range(2)]
                        t1_ = [sbuf(st, "t1_%d" % i, [128, TW]) for i in range(2)]
                        t2_ = [sbuf(st, "t2_%d" % i, [128, TW]) for i in range(2)]
                        tsqh_, tsrt_, trstd_, tt1_, tt2_ = toks(2), toks(2), toks(2), toks(2), toks(2)
                        nsc = norm_scratch(st, TW)
                        for tt in range(NT):
                            ts_ = slice(tt * TW, (tt + 1) * TW)
                            P.dma("sp", hh[:], h_d[:, :, ts_], W=[thh])
                            P.dma("sp", rc[tt % 2][:], rope_c_d[:, ts_], W=[trc[tt % 2]])
                            P.dma("sp", rs[tt % 2][:], rope_s_d[:, ts_], W=[trc[tt % 2]])
                            norm_tile(nsc, hh, thh, TW, u, tu, lambda k: A1[:, l * 8 + k:l * 8 + k + 1],
                                      lambda k: modcol(l, 0, k), PS[0], PSt[0])

                            def proj(bank, c0):
                                for k in range(8):
                                    P.op("pe", "matmul", PS[bank][:, :], lhsT=win[:, k, c0:c0 + 128], rhs=u[:, k, :],
                                         start=(k == 0), stop=(k == 7), R=wtok(c0) + [tu], W=[PSt[bank]])
                            for cs in range(4):
                                b = 1 + (cs % 2)
                                proj(b, cs * 128)
                                P.op("act", "activation", out=zs5[cs][:, ts_], in_=PS[b][:, :], func=AF.Copy,
                                     R=[PSt[b]], W=[tzs5[cs][tt]])
                            items = [(512 + 128 * j, 1024 + 128 * j, qb[j], tq[j][tt], 0, 1) for j in range(4)]
                            items += [(1536 + 128 * j, 1792 + 128 * j, kd[j], tkd[j], 2, 3) for j in range(2)]
                            for n_, (cz, cp, dst, tdst, wc, wpc) in enumerate(items):
                                bz = 3 + (n_ % 2)
                                bp = 5 + (n_ % 2)
                                sqh, srt, rstd, t1, t2 = sqh_[n_ % 2], srt_[n_ % 2], rstd_[n_ % 2], t1_[n_ % 2], t2_[n_ % 2]
                                tsqh, tsrt, trstd, tt1, tt2 = tsqh_[n_ % 2], tsrt_[n_ % 2], trstd_[n_ % 2], tt1_[n_ % 2], tt2_[n_ % 2]
                                proj(bz, cz)
                                proj(bp, cp)
                                P.op("act", "activation", out=sqh[:], in_=PS[bz][:, :], func=AF.Square, R=[PSt[bz]], W=[tsqh])
                                P.op("pe", "matmul", PS[7][:, :], lhsT=bones_b[:], rhs=sqh[:], start=True, stop=True,
                                     R=[tsqh] + CT, W=[PSt[7]])
                                P.op("act", "activation", out=srt[:], in_=PS[7][:, :], func=AF.Sqrt, scale=1.0 / 64, bias=eps_c[:],
                                     R=[PSt[7]] + CT, W=[tsrt])
                                P.op("dve", "reciprocal", out=rstd[:], in_=srt[:], R=[tsrt], W=[trstd])
                                P.op("dve", "scalar_tensor_tensor", out=t1[:], in0=PS[bz][:, :], scalar=qkn[:, wc:wc + 1],
                                     in1=rc[tt % 2][:], op0=ALU.mult, op1=ALU.mult, R=[PSt[bz], t_qkn, trc[tt % 2]], W=[tt1])
                                P.op("dve", "scalar_tensor_tensor", out=t2[:], in0=PS[bp][:, :], scalar=qkn[:, wpc:wpc + 1],
                                     in1=rs[tt % 2][:], op0=ALU.mult, op1=ALU.mult, R=[PSt[bp], t_qkn, trc[tt % 2]], W=[tt2])
                                P.op("pool", "tensor_tensor", out=t1[:], in0=t1[:], in1=t2[:], op=ALU.add, R=[tt1, tt2], W=[tt1])
                                P.op("dve", "tensor_tensor", out=dst[:, ts_], in0=t1[:], in1=rstd[:], op=ALU.mult,
                                     R=[tt1, trstd], W=[tdst])
                            b = 1 + (tt % 2)
                            for j in range(4):
                                for k in range(8):
                                    P.op("pe", "matmul", PS[b][:, j * 128:(j + 1) * 128], lhsT=u[:, k, j * 128:(j + 1) * 128],
                                         rhs=win[:, k, 2048:2176], start=(k == 0), stop=(k == 7), R=wtok(2048) + [tu], W=[PSt[b]])
                            src = PS[b][:, :].rearrange("p (j v d) -> p j v d", j=4, v=2)
                            P.op("act", "activation", out=Vx[:, tt * 4:(tt + 1) * 4, :, 0:64], in_=src, func=AF.Copy,
                                 R=[PSt[b]], W=[tVx])
                            P.op("dve", "tensor_copy", out=Vx[:, tt * 4:(tt + 1) * 4, :, 128:192], in_=src, R=[PSt[b]], W=[tVx])
                        P.barrier()
                        P.emit()
                    with ExitStack() as st:
                        NSB = 4
                        SBK = [0, 1, 2, 6]
                        pt = [sbuf(st, "pt%d" % i, [128, TW], BF16) for i in range(NSB)]
                        tpt = toks(NSB)
                        rec = [sbuf(st, "rec%d" % i, [128, TW]) for i in range(2)]
                        bcs = [sbuf(st, "bcs%d" % i, [128, TW]) for i in range(2)]
                        trec, tbcs = toks(2), toks(2)
                        oat = [sbuf(st, "oat%d" % i, [128, TW], BF16) for i in range(2)]
                        toat = toks(2)
                        steps = [(j, qt, hhf, kt) for j in range(4) for qt in range(NT) for hhf in range(2) for kt in range(32)]
                        NS = len(steps)
                        qz = [sbuf(st, "qz%d" % i, [128, S], BF16) for i in range(4)]
                        tqz = toks(4)
                        for j in range(4):
                            P.op("pool", "memset", qz[j][0:64, :], 0.0, W=[tqz[j]])
                            P.op("act" if j % 2 == 0 else "dve", *(("activation",) if j % 2 == 0 else ("tensor_copy",)),
                                 out=qz[j][64:128, :], in_=qb[j][64:128, :], **({"func": AF.Copy} if j % 2 == 0 else {}),
                                 R=tq[j], W=[tqz[j]])
                            P.op("pool", "memset", qb[j][64:128, :], 0.0, R=[tqz[j]], W=tq[j])

                        def emit_qk(s_):
                            j, qt, hhf, kt = steps[s_]
                            kv, bs = j // 2, s_ % NSB
                            qsrc = qb[j] if hhf == 0 else qz[j]
                            P.op("pe", "matmul", PS[SBK[bs]][:, :], lhsT=kd[kv][:, kt * 128:(kt + 1) * 128],
                                 rhs=qsrc[:, qt * TW:(qt + 1) * TW], start=True, stop=True,
                                 R=[tkd[kv], tq[j][qt], tqz[j]], W=[PSt[SBK[bs]]])

                        pending = []
                        emit_qk(0)
                        emit_qk(1)
                        emit_qk(2)
                        for s_ in range(NS):
                            j, qt, hhf, kt = steps[s_]
                            kv, bs = j // 2, s_ % NSB
                            un = s_ // 32
                            bo = 3 + (un % 2)
                            P.op("act", "activation", out=pt[bs][:], in_=PS[SBK[bs]][:, :], func=AF.Exp, scale=0.125,
                                 R=[PSt[SBK[bs]]], W=[tpt[bs]])
                            if s_ + 3 < NS:
                                emit_qk(s_ + 3)
                            if hhf == 0:
                                P.op("pe", "matmul", PS[bo][:, :], lhsT=Vx[:, kt, kv, 0:128], rhs=pt[bs][:],
                                     start=(kt == 0), stop=(kt == 31), R=[tVx, tpt[bs]], W=[PSt[bo]])
                            else:
                                P.op("pe", "matmul", PS[bo][:, :], lhsT=Vx[:, kt, kv, 64:192], rhs=pt[bs][:],
                                     start=(kt == 0), stop=(kt == 31), R=[tVx, tpt[bs]], W=[PSt[bo]])
                            if kt == 31:
                                ob_ = (j * NT + qt) % 2
                                o, to = oat[ob_], toat[ob_]
                                rr, trr, bb, tbb = rec[un % 2], trec[un % 2], bcs[un % 2], tbcs[un % 2]
                                qs = slice(qt * TW, (qt + 1) * TW)
                                if hhf == 0:
                                    P.op("dve", "reciprocal", out=rr[64:65, :], in_=PS[bo][64:65, :], R=[PSt[bo]], W=[trr])

                                    def f2(rr=rr, trr=trr):
                                        P.op("pe", "matmul", PS[5][0:64, :], lhsT=ones_f[64:65, 0:64], rhs=rr[64:65, :],
                                             start=True, stop=True, R=[trr] + CT, W=[PSt[5]])

                                    def f3(bb=bb, tbb=tbb, o=o, to=to, bo=bo):
                                        P.op("act", "activation", out=bb[0:64, :], in_=PS[5][0:64, :], func=AF.Copy,
                                             R=[PSt[5]], W=[tbb])
                                        P.op("dve", "tensor_tensor", out=o[0:64, :], in0=PS[bo][0:64, :], in1=bb[0:64, :],
                                             op=ALU.mult, R=[PSt[bo], tbb], W=[to])
                                else:
                                    P.op("dve", "reciprocal", out=rr[0:1, :], in_=PS[bo][0:1, :], R=[PSt[bo]], W=[trr])

                                    def f2(rr=rr, trr=trr):
                                        P.op("pe", "matmul", PS[5][:, :], lhsT=ones_f[0:1, :], rhs=rr[0:1, :],
                                             start=True, stop=True, R=[trr] + CT, W=[PSt[5]])

                                    def f3(bb=bb, tbb=tbb, o=o, to=to, bo=bo, j=j, qs=qs):
                                        P.op("act", "activation", out=bb[64:128, :], in_=PS[5][64:128, :], func=AF.Copy,
                                             R=[PSt[5]], W=[tbb])
                                        P.op("dve", "tensor_tensor", out=o[64:128, :], in0=PS[bo][64:128, :], in1=bb[64:128, :],
                                             op=ALU.mult, R=[PSt[bo], tbb], W=[to])
                                        P.dma("sp", mix_d[:, 4 + j, qs], o[:], R=[to], W=[])
                                pending.append((s_ + 3, f2))
                                pending.append((s_ + 6, f3))
                            while pending and pending[0][0] <= s_:
                                pending.pop(0)[1]()
                        for _, fn in pending:
                            fn()
                        P.barrier()
                        P.emit()
                with ExitStack() as st:
                    NG = 64
                    prm = {}
                    tprm = Tok()

                    def ptile(name, w=NG):
                        prm[name] = sbuf(st, "s5p_" + name, [128, w])
                        return prm[name]
                    lre, lim, ldt = ptile("lre"), ptile("lim"), ptile("ldt")
                    P.dma("sp", lre[:], lamre_d[e], W=[tprm])
                    P.dma("sp", lim[:], lamim_d[e], W=[tprm])
                    P.dma("sp", ldt[:], logdt_d[e], W=[tprm])

                    def dv(meth, **kw):
                        P.op("dve", meth, R=[tprm] + CT, W=[tprm], **kw)

                    def ac(**kw):
                        P.op("act", "activation", R=[tprm] + CT, W=[tprm], **kw)
                    lr, dt, x1, mag, th = ptile("lr"), ptile("dt"), ptile("x1"), ptile("mag"), ptile("th")
                    dv("tensor_scalar", out=lr[:], in0=lre[:], scalar1=-1e-4, scalar2=None, op0=ALU.min)
                    ac(out=dt[:], in_=ldt[:], func=AF.Exp)
                    dv("tensor_tensor", out=x1[:], in0=lr[:], in1=dt[:], op=ALU.mult)
                    ac(out=mag[:], in_=x1[:], func=AF.Exp)
                    dv("tensor_tensor", out=th[:], in0=lim[:], in1=dt[:], op=ALU.mult)
                    ti = sbuf(st, "s5p_ti", [128, NG], I32)
                    tf, red = ptile("tf"), ptile("red")

                    def sin_of(dst, src, shift):
                        dv("tensor_scalar", out=tf[:], in0=src[:], scalar1=shift, scalar2=1.0 / (2 * PI), op0=ALU.add, op1=ALU.mult)
                        dv("tensor_copy", out=ti[:], in_=tf[:])
                        dv("tensor_copy", out=tf[:], in_=ti[:])
                        dv("tensor_scalar", out=red[:], in0=src[:], scalar1=shift, scalar2=None, op0=ALU.add)
                        dv("scalar_tensor_tensor", out=red[:], in0=tf[:], scalar=-2 * PI, in1=red[:], op0=ALU.mult, op1=ALU.add)
                        dv("tensor_scalar", out=red[:], in0=red[:], scalar1=PI, scalar2=-PI, op0=ALU.min, op1=ALU.max)
                        ac(out=dst[:], in_=red[:], func=AF.Sin)
                    sn, cs_ = ptile("sn"), ptile("cs")
                    sin_of(sn, th, 0.0)
                    sin_of(cs_, th, PI / 2)
                    are, aim = ptile("are"), ptile("aim")
                    dv("tensor_tensor", out=are[:], in0=mag[:], in1=cs_[:], op=ALU.mult)
                    dv("tensor_tensor", out=aim[:], in0=mag[:], in1=sn[:], op=ALU.mult)
                    den, t_a, t_b = ptile("den"), ptile("ta"), ptile("tb")
                    dv("tensor_tensor", out=den[:], in0=lr[:], in1=lr[:], op=ALU.mult)
                    dv("tensor_tensor", out=t_a[:], in0=lim[:], in1=lim[:], op=ALU.mult)
                    dv("tensor_tensor", out=den[:], in0=den[:], in1=t_a[:], op=ALU.add)
                    dv("reciprocal", out=den[:], in_=den[:])
                    nr = ptile("nr")
                    dv("tensor_scalar", out=nr[:], in0=are[:], scalar1=-1.0, scalar2=None, op0=ALU.add)
                    fre, fim = ptile("fre"), ptile("fim")
                    dv("tensor_tensor", out=t_a[:], in0=nr[:], in1=lr[:], op=ALU.mult)
                    dv("tensor_tensor", out=t_b[:], in0=aim[:], in1=lim[:], op=ALU.mult)
                    dv("tensor_tensor", out=t_a[:], in0=t_a[:], in1=t_b[:], op=ALU.add)
                    dv("tensor_tensor", out=fre[:], in0=t_a[:], in1=den[:], op=ALU.mult)
                    dv("tensor_tensor", out=t_a[:], in0=aim[:], in1=lr[:], op=ALU.mult)
                    dv("tensor_tensor", out=t_b[:], in0=nr[:], in1=lim[:], op=ALU.mult)
                    dv("tensor_tensor", out=t_a[:], in0=t_a[:], in1=t_b[:], op=ALU.subtract)
                    dv("tensor_tensor", out=fim[:], in0=t_a[:], in1=den[:], op=ALU.mult)
                    F2, F3 = ptile("F2"), ptile("F3")
                    dv("tensor_scalar", out=F2[:], in0=fim[:], scalar1=sgn, scalar2=None, op0=ALU.mult)
                    dv("tensor_scalar", out=F3[:], in0=fre[:], scalar1=nsgn, scalar2=None, op0=ALU.mult)
                    CK = sbuf(st, "s5p_CK", [128, 12, NG])
                    SK = sbuf(st, "s5p_SK", [128, 12, NG])
                    dv("tensor_copy", out=CK[:, 0, :], in_=cs_[:])
                    dv("tensor_copy", out=SK[:, 0, :], in_=sn[:])
                    for k in range(11):
                        dv("tensor_tensor", out=t_a[:], in0=CK[:, k, :], in1=CK[:, k, :], op=ALU.mult)
                        dv("tensor_tensor", out=t_b[:], in0=SK[:, k, :], in1=SK[:, k, :], op=ALU.mult)
                        dv("tensor_tensor", out=CK[:, k + 1, :], in0=t_a[:], in1=t_b[:], op=ALU.subtract)
                        dv("tensor_tensor", out=t_a[:], in0=CK[:, k, :], in1=SK[:, k, :], op=ALU.mult)
                        dv("tensor_scalar", out=SK[:, k + 1, :], in0=t_a[:], scalar1=2.0, scalar2=None, op0=ALU.mult)
                    SKs9 = ptile("SKs9")
                    dv("tensor_scalar", out=SKs9[:], in0=SK[:, 9, :], scalar1=sgn, scalar2=None, op0=ALU.mult)
                    bbs = sbuf(st, "s5_bbs", [128, NG, 16])
                    bbw = sbuf(st, "s5_bbw", [128, NG, 16])

                    def bc16(a):
                        return a[:].unsqueeze(2).to_broadcast([128, NG, 16])
                    with ExitStack() as sub:
                        b1 = sbuf(sub, "s5_b1", [128, NG, 16])
                        b2 = sbuf(sub, "s5_b2", [128, NG, 16])
                        tmpb = sbuf(sub, "s5_tmpb", [128, NG, 16])
                        P.dma("sp", b1[:], sb1_d[e].rearrange("p (g c) -> p g c", c=16), W=[tprm])
                        P.dma("sp", b2[:], sb2_d[e].rearrange("p (g c) -> p g c", c=16), W=[tprm])
                        dv("tensor_tensor", out=bbs[:], in0=b1[:], in1=bc16(fre), op=ALU.mult)
                        dv("tensor_tensor", out=tmpb[:], in0=b2[:], in1=bc16(F2), op=ALU.mult)
                        dv("tensor_tensor", out=bbs[:], in0=bbs[:], in1=tmpb[:], op=ALU.add)
                        dv("tensor_tensor", out=bbw[:], in0=b2[:], in1=bc16(F3), op=ALU.mult)
                        dv("tensor_tensor", out=tmpb[:], in0=b1[:], in1=bc16(fim), op=ALU.mult)
                        dv("tensor_tensor", out=bbw[:], in0=bbw[:], in1=tmpb[:], op=ALU.add)
                        P.barrier()
                        P.emit()
                    c1 = sbuf(st, "s5_c1", [128, NG, 16])
                    c2 = sbuf(st, "s5_c2", [128, NG, 16])
                    P.dma("sp", c1[:], sc1_d[e].rearrange("p (g c) -> p g c", c=16), W=[tprm])
                    P.dma("sp", c2[:], sc2_d[e].rearrange("p (g c) -> p g c", c=16), W=[tprm])
                    s5d = sbuf(st, "s5_d", [128, 4])
                    glub = sbuf(st, "s5_glub", [128, 4])
                    P.dma("sp", s5d[:], s5d_d[e], W=[tprm])
                    P.dma("sp", glub[:], glub_d[e], W=[tprm])
                    gluw = sbuf(st, "s5_gluw", [128, 4, 512], BF16)
                    tgw = Tok()
                    P.dma("pool", gluw[:], gluw_d[e].rearrange("(k p) n -> p k n", p=128), W=[tgw])

                    bT = sbuf(st, "s5_bT", [128, 128])
                    bTw = sbuf(st, "s5_bTw", [128, 128])
                    tbT = Tok()
                    lb = [sbuf(st, "s5_lb%d" % i, [128, 8, 128], BF16) for i in range(2)]
                    lbw = [sbuf(st, "s5_lbw%d" % i, [128, 8, 128], BF16) for i in range(2)]
                    tlb = toks(2)
                    w1p = [sbuf(st, "s5_w1p%d" % i, [128, 1152], BF16) for i in range(2)]
                    w2p = [sbuf(st, "s5_w2p%d" % i, [128, 1152], BF16) for i in range(2)]
                    twp = toks(2)
                    for i in range(2):
                        P.op("pool", "memset", w1p[i][:], 0.0, W=[twp[i]])
                        P.op("pool", "memset", w2p[i][:], 0.0, W=[twp[i]])
                    COS = [sbuf(st, "s5_COS%d" % i, [128, TW]) for i in range(2)]
                    SIN = [sbuf(st, "s5_SIN%d" % i, [128, TW]) for i in range(2)]
                    tCOS, tSIN = toks(2), toks(2)
                    tA = sbuf(st, "s5_tA", [128, TW // 2])
                    tB = sbuf(st, "s5_tB", [128, TW // 2])
                    NCR = 4
                    crt = [sbuf(st, "s5_crt%d" % i, [128, 1]) for i in range(NCR)]
                    crc = [sbuf(st, "s5_crc%d" % i, [128, 1]) for i in range(NCR)]
                    tcrt, tcrc = toks(NCR), toks(NCR)
                    ttA, ttB = toks(2)
                    yacc = sbuf(st, "s5_yacc", [128, S])
                    tya = toks(NT)
                    gl, tgl = zs5, tzs5
                    NB = 3
                    vt = [sbuf(st, "s5_v%d" % i, [128, TW], BF16) for i in range(NB)]
                    tm_ = [sbuf(st, "s5_tm%d" % i, [128, TW], BF16) for i in range(NB)]
                    gt = [sbuf(st, "s5_g%d" % i, [128, TW], BF16) for i in range(NB)]
                    e1 = [sbuf(st, "s5_e1_%d" % i, [128, TW], BF16) for i in range(NB)]
                    e2 = [sbuf(st, "s5_e2_%d" % i, [128, TW], BF16) for i in range(NB)]
                    te1, te2 = toks(NB), toks(NB)
                    C16 = [sbuf(st, "s5_C16_%d" % i, [128, TW], BF16) for i in range(2)]
                    S16 = [sbuf(st, "s5_S16_%d" % i, [128, TW], BF16) for i in range(2)]
                    tC16, tS16 = toks(2), toks(2)
                    a1 = [sbuf(st, "s5_a1%d" % i, [128, TW], BF16) for i in range(NB)]
                    a2 = [sbuf(st, "s5_a2%d" % i, [128, TW], BF16) for i in range(NB)]
                    tvt, ttm, tgt, ta1, ta2 = toks(NB), toks(NB), toks(NB), toks(NB), toks(NB)
                    BUP = [(0, 1), (2, 3)]

                    gds = [(cs, dr, gq) for cs in range(4) for dr in range(2) for gq in range(8)]
                    steps = [(gi, tp) for gi in range(len(gds)) for tp in range(NT)]
                    NS = len(steps)

                    def gd_of(gi):
                        cs, dr, gq = gds[gi]
                        return cs, dr, gq, dr * 32 + cs * 8 + gq, (cs * 2 + dr) % 2

                    def gen_piece(gi, piece):
                        if gi >= len(gds):
                            return
                        gd = gd_of(gi)[3]
                        C_, S_, tC, tS = COS[gi % 2], SIN[gi % 2], tCOS[gi % 2], tSIN[gi % 2]
                        work = {0: [(0, 0, 1), (1, 0, 2)], 1: [(2, 0, 4), (3, 0, 8)], 2: [(4, 0, 16)], 3: [(5, 0, 32)],
                                4: [(6, 0, 64)], 5: [(7, 0, 128)], 6: [(8, 0, 256)], 7: []}[piece]
                        if piece == 0:
                            P.op("pool", "memset", C_[:, 0:1], 1.0, W=[tC])
                            P.op("pool", "memset", S_[:, 0:1], 0.0, W=[tS])
                        if piece == 7:
                            dr_ = gds[gi][1]
                            oc = C16[gi % 2][:, :] if dr_ == 0 else C16[gi % 2][:, ::-1]
                            os_ = S16[gi % 2][:, :] if dr_ == 0 else S16[gi % 2][:, ::-1]
                            P.op("act", "activation", out=oc, in_=C_[:, :], func=AF.Copy, R=[tC], W=[tC16[gi % 2]])
                            P.op("act", "activation", out=os_, in_=S_[:, :], func=AF.Copy, R=[tS], W=[tS16[gi % 2]])
                        for (k, c0, c1_) in work:
                            ln = 1 << k
                            n = c1_ - c0
                            ck = CK[:, k, gd:gd + 1]
                            sk = SK[:, k, gd:gd + 1]
                            P.op("act", "activation", out=C_[:, ln + c0:ln + c1_], in_=C_[:, c0:c1_], func=AF.Identity, scale=ck,
                                 R=[tC, tprm], W=[tC])
                            P.op("act", "activation", out=tA[:, 0:n], in_=S_[:, c0:c1_], func=AF.Identity, scale=sk,
                                 R=[tS, tprm], W=[ttA])
                            P.op("pool", "tensor_tensor", out=C_[:, ln + c0:ln + c1_], in0=C_[:, ln + c0:ln + c1_], in1=tA[:, 0:n],
                                 op=ALU.subtract, R=[tC, ttA], W=[tC])
                            P.op("act", "activation", out=S_[:, ln + c0:ln + c1_], in_=S_[:, c0:c1_], func=AF.Identity, scale=ck,
                                 R=[tS, tprm], W=[tS])
                            P.op("act", "activation", out=tB[:, 0:n], in_=C_[:, c0:c1_], func=AF.Identity, scale=sk,
                                 R=[tC, tprm], W=[ttB])
                            P.op("pool", "tensor_tensor", out=S_[:, ln + c0:ln + c1_], in0=S_[:, ln + c0:ln + c1_], in1=tB[:, 0:n],
                                 op=ALU.add, R=[tS, ttB], W=[tS])

                    def prep_group(cs, dr):
                        ws = (cs * 2 + dr) % 2
                        g0 = dr * 32 + cs * 8
                        P.op("pe", "transpose", out=PS[4][:, 0:128], in_=bbs[:, g0:g0 + 8, :].rearrange("p g c -> p (g c)"),
                             identity=ident, R=[tprm] + CT, W=[PSt[4]])
                        P.op("pe", "transpose", out=PS[4][:, 128:256], in_=bbw[:, g0:g0 + 8, :].rearrange("p g c -> p (g c)"),
                             identity=ident, R=[tprm] + CT, W=[PSt[4]])
                        P.op("act", "activation", out=bT[:], in_=PS[4][:, 0:128], func=AF.Copy, R=[PSt[4]], W=[tbT])
                        P.op("act", "activation", out=bTw[:], in_=PS[4][:, 128:256], func=AF.Copy, R=[PSt[4]], W=[tbT])
                        for gq in range(8):
                            P.op("dve", "tensor_scalar", out=lb[ws][:, gq, :], in0=bT[:], scalar1=gmask[:, gq:gq + 1], scalar2=None,
                                 op0=ALU.mult, R=[tbT] + CT, W=[tlb[ws]])
                            P.op("dve", "tensor_scalar", out=lbw[ws][:, gq, :], in0=bTw[:], scalar1=gmask[:, gq:gq + 1], scalar2=None,
                                 op0=ALU.mult, R=[tbT] + CT, W=[tlb[ws]])
                        P.op("dve", "tensor_scalar", out=w1p[ws][:].rearrange("p (g w) -> p g w", w=144)[:, :, 0:16],
                             in0=c1[:, g0:g0 + 8, :], scalar1=nsgn, scalar2=None, op0=ALU.mult, R=[tprm] + CT, W=[twp[ws]])
                        P.op("dve", "tensor_scalar", out=w2p[ws][:].rearrange("p (g w) -> p g w", w=144)[:, :, 0:16],
                             in0=c2[:, g0:g0 + 8, :], scalar1=-1.0, scalar2=None, op0=ALU.mult, R=[tprm] + CT, W=[twp[ws]])

                    def tile_of(s_):
                        gi, tp = steps[s_]
                        cs, dr, gq, gd, ws = gd_of(gi)
                        tt = tp if dr == 0 else NT - 1 - tp
                        return gi, tp, cs, dr, gq, gd, ws, tt

                    def tabs(s_):
                        gi, tp, cs, dr, gq, gd, ws, tt = tile_of(s_)
                        if dr == 0:
                            return COS[gi % 2][:, :], SIN[gi % 2][:, :]
                        return COS[gi % 2][:, ::-1], SIN[gi % 2][:, ::-1]

                    def st_bu(s_):
                        if s_ >= NS:
                            return
                        gi, tp, cs, dr, gq, gd, ws, tt = tile_of(s_)
                        if tp == 0 and gq == 0:
                            prep_group(cs, dr)
                        b1_, b2_ = BUP[s_ % 2]
                        ts_ = slice(tt * TW, (tt + 1) * TW)
                        P.op("pe", "matmul", PS[b1_][:, :], lhsT=lb[ws][:, gq, :], rhs=zs5[cs][:, ts_], start=True, stop=True,
                             R=[tlb[ws], tzs5[cs][tt]], W=[PSt[b1_]])
                        P.op("pe", "matmul", PS[b2_][:, :], lhsT=lbw[ws][:, gq, :], rhs=zs5[cs][:, ts_], start=True, stop=True,
                             R=[tlb[ws], tzs5[cs][tt]], W=[PSt[b2_]])

                    def st_a(s_):
                        if s_ >= NS:
                            return
                        gi = steps[s_][0]
                        b1_, b2_ = BUP[s_ % 2]
                        i = s_ % NB
                        P.op("act", "activation", out=e1[i][:], in_=PS[b1_][:, :], func=AF.Copy, R=[PSt[b1_]], W=[te1[i]])
                        P.op("act", "activation", out=e2[i][:], in_=PS[b2_][:, :], func=AF.Copy, R=[PSt[b2_]], W=[te2[i]])
                        P.op("dve", "tensor_tensor", out=tm_[i][:], in0=e1[i][:], in1=C16[gi % 2][:], op=ALU.mult,
                             R=[te1[i], tC16[gi % 2]], W=[ttm[i]])
                        P.op("dve", "tensor_tensor", out=vt[i][:], in0=e2[i][:], in1=S16[gi % 2][:], op=ALU.mult,
                             R=[te2[i], tS16[gi % 2]], W=[tvt[i]])
                        P.op("dve", "tensor_tensor", out=vt[i][:], in0=vt[i][:], in1=tm_[i][:], op=ALU.add,
                             R=[ttm[i], tvt[i]], W=[tvt[i]])

                    def st_swap(s_):
                        gi, tp, cs, dr, gq, gd, ws, tt = tile_of(s_)
                        if tp == NT - 1:
                            return
                        i = s_ % NB
                        col = 2 * (s_ % 128)
                        src = gt[i][:, TW - 2:TW] if dr == 0 else gt[i][:, 0:2]
                        P.op("pe", "matmul", PS[6][:, col:col + 2], lhsT=swapm_b[:], rhs=src, start=True, stop=True,
                             R=[tgt[i]] + CT, W=[PSt[6]])

                    def st_carry(s_):
                        gi, tp, cs, dr, gq, gd, ws, tt = tile_of(s_)
                        if tp == 0:
                            return
                        pv = (s_ - 1) % NB
                        cc = s_ % NCR
                        col = 2 * ((s_ - 1) % 128) + (1 if dr == 0 else 0)
                        glast = gt[pv][:, TW - 1:TW] if dr == 0 else gt[pv][:, 0:1]
                        P.op("dve", "tensor_scalar", out=crt[cc][:], in0=PS[6][:, col:col + 1], scalar1=SKs9[:, gd:gd + 1], scalar2=None,
                             op0=ALU.mult, R=[PSt[6], tprm], W=[tcrt[cc]])
                        P.op("dve", "scalar_tensor_tensor", out=crc[cc][:], in0=glast, scalar=CK[:, 9, gd:gd + 1], in1=crt[cc][:],
                             op0=ALU.mult, op1=ALU.add, R=[tgt[pv], tcrt[cc], tprm], W=[tcrc[cc]])

                    def st_b(s_):
                        gi, tp, cs, dr, gq, gd, ws, tt = tile_of(s_)
                        i = s_ % NB
                        cc = s_ % NCR
                        mcol = mag[:, gd:gd + 1].to_broadcast([128, TW])
                        extra = [tcrc[cc]] if tp > 0 else []
                        init = 0.0 if tp == 0 else crc[cc][:, 0:1]
                        if dr == 0:
                            P.op("dve", "tensor_tensor_scan", out=gt[i][:], data0=mcol, data1=vt[i][:], initial=init,
                                 op0=ALU.mult, op1=ALU.add, R=[tvt[i], tprm] + extra, W=[tgt[i]])
                        else:
                            P.op("dve", "tensor_tensor_scan", out=gt[i][:, ::-1], data0=mcol, data1=vt[i][:, ::-1], initial=init,
                                 op0=ALU.mult, op1=ALU.add, R=[tvt[i], tprm] + extra, W=[tgt[i]])

                    def st_c(s_):
                        gi, tp, cs, dr, gq, gd, ws, tt = tile_of(s_)
                        i = s_ % NB
                        by = 4 + (s_ % 2)
                        P.op("dve", "tensor_tensor", out=a1[i][:], in0=gt[i][:], in1=C16[gi % 2][:], op=ALU.mult,
                             R=[tgt[i], tC16[gi % 2]], W=[ta1[i]])
                        P.op("dve", "tensor_tensor", out=a2[i][:], in0=gt[i][:], in1=S16[gi % 2][:], op=ALU.mult,
                             R=[tgt[i], tS16[gi % 2]], W=[ta2[i]])
                        P.op("pe", "matmul", PS[by][:, :], lhsT=w1p[ws][:, gq * 128:(gq + 1) * 128], rhs=a1[i][:],
                             start=True, stop=False, R=[twp[ws], ta1[i]], W=[PSt[by]])
                        P.op("pe", "matmul", PS[by][:, :], lhsT=w2p[ws][:, gq * 128:(gq + 1) * 128], rhs=a2[i][:],
                             start=False, stop=True, R=[twp[ws], ta2[i]], W=[PSt[by]])

                    def st_acc(s_):
                        if s_ < 0:
                            return
                        gi, tp, cs, dr, gq, gd, ws, tt = tile_of(s_)
                        by = 4 + (s_ % 2)
                        ts_ = slice(tt * TW, (tt + 1) * TW)
                        P.op("dve", "tensor_tensor", out=yacc[:, ts_], in0=PS[by][:, :], in1=yacc[:, ts_], op=ALU.add,
                             R=[PSt[by], tya[tt]], W=[tya[tt]])
                        if s_ % 128 == 127:
                            for t2 in range(NT):
                                t2s = slice(t2 * TW, (t2 + 1) * TW)
                                P.op("act", "activation", out=gl[cs][:, t2s], in_=yacc[:, t2s], func=AF.Gelu_apprx_tanh,
                                     R=[tya[t2]], W=[tgl[cs][t2]])
                            if cs + 1 < 4:
                                y_init(cs + 1)

                    def y_init(cs):
                        for t2 in range(NT):
                            t2s = slice(t2 * TW, (t2 + 1) * TW)
                            P.op("act", "activation", out=yacc[:, t2s], in_=zs5[cs][:, t2s], func=AF.Identity, scale=s5d[:, cs:cs + 1],
                                 R=[tzs5[cs][t2], tprm], W=[tya[t2]])

                    for pc in range(8):
                        gen_piece(0, pc)
                    gen_piece(1, 0)
                    y_init(0)
                    st_bu(0)
                    st_bu(1)
                    st_a(0)
                    for s_ in range(NS):
                        gi_, tp_ = steps[s_]
                        st_bu(s_ + 2)
                        st_a(s_ + 1)
                        st_carry(s_)
                        st_b(s_)
                        st_swap(s_)
                        st_c(s_)
                        st_acc(s_ - 1)
                        if tp_ < NT - 1:
                            gen_piece(gi_ + 1, tp_ + 1)
                        else:
                            gen_piece(gi_ + 2, 0)
                    st_acc(NS - 1)
                    sg = [sbuf(st, "s5_sg%d" % i, [128, TW]) for i in range(2)]
                    ys = [sbuf(st, "s5_ys%d" % i, [128, TW], BF16) for i in range(2)]
                    tsg, tys = toks(2), toks(2)
                    n_ = 0
                    for tt in range(NT):
                        ts_ = slice(tt * TW, (tt + 1) * TW)
                        for mo in range(4):
                            b = 6 + (n_ % 2)
                            i = n_ % 2
                            n_ += 1
                            for k in range(4):
                                P.op("pe", "matmul", PS[b][:, :], lhsT=gluw[:, k, mo * 128:(mo + 1) * 128], rhs=gl[k][:, ts_],
                                     start=(k == 0), stop=(k == 3), R=[tgw, tgl[k][tt]], W=[PSt[b]])
                            P.op("act", "activation", out=sg[i][:], in_=PS[b][:, :], func=AF.Sigmoid, bias=glub[:, mo:mo + 1],
                                 R=[PSt[b], tprm], W=[tsg[i]])
                            P.op("dve", "tensor_tensor", out=ys[i][:], in0=sg[i][:], in1=gl[mo][:, ts_], op=ALU.mult,
                                 R=[tsg[i], tgl[mo][tt]], W=[tys[i]])
                            P.dma("sp", mix_d[:, mo, ts_], ys[i][:], R=[tys[i]], W=[])
                    P.barrier()
                    P.emit()
            phase_tail(l, hout_d[e])

        def phase_odd(l):
            o = l // 2
            with ExitStack() as lst:
                XW = S + 4
                xr = [sbuf(lst, "xr%d" % i, [128, XW], BF16) for i in range(8)]
                txr = [Tok() for _ in range(8)]
                with ExitStack() as st:
                    win = sbuf(st, "rwin", [128, 8, 2048], BF16)
                    t_winp = toks(4)
                    wsrc = rin_d[o].rearrange("(k p) n -> p k n", p=128)
                    for q4 in range(4):
                        P.dma("pool", win[:, :, q4 * 512:(q4 + 1) * 512], wsrc[:, :, q4 * 512:(q4 + 1) * 512], W=[t_winp[q4]])
                    for c in range(8):
                        P.op("pool", "memset", xr[c][:, 0:2], 0.0, W=[txr[c]])
                        P.op("pool", "memset", xr[c][:, S + 2:S + 4], 0.0, W=[txr[c]])
                    hh = sbuf(st, "hh", [128, 8, TW])
                    thh = Tok()
                    u = sbuf(st, "u", [128, 8, TW], BF16)
                    tu = Tok()
                    gs = [sbuf(st, "gs%d" % i, [128, TW], BF16) for i in range(3)]
                    tgs = toks(3)
                    nsc = norm_scratch(st, TW)
                    n_ = 0
                    for tt in range(NT):
                        ts_ = slice(tt * TW, (tt + 1) * TW)
                        P.dma("sp", hh[:], h_d[:, :, ts_], W=[thh])
                        norm_tile(nsc, hh, thh, TW, u, tu, lambda k: A1[:, l * 8 + k:l * 8 + k + 1],
                                  lambda k: modcol(l, 0, k), PS[0], PSt[0])
                        for c in range(16):
                            b = 1 + (c % 4)
                            for k in range(8):
                                P.op("pe", "matmul", PS[b][:, :], lhsT=win[:, k, c * 128:(c + 1) * 128], rhs=u[:, k, :],
                                     start=(k == 0), stop=(k == 7), R=[t_winp[c // 4], tu], W=[PSt[b]])
                            if c < 8:
                                i = n_ % 3
                                n_ += 1
                                P.op("act", "activation", out=gs[i][:], in_=PS[b][:, :], func=AF.Copy, R=[PSt[b]], W=[tgs[i]])
                                P.dma("sp", gate_d[:, c, ts_], gs[i][:], R=[tgs[i]], W=[])
                            else:
                                P.op("dve", "tensor_copy", out=xr[c - 8][:, 2 + tt * TW:2 + (tt + 1) * TW], in_=PS[b][:, :],
                                     R=[PSt[b]], W=[txr[c - 8]])
                    P.barrier()
                    P.emit()
                import os as _os
                with ExitStack() as st:
                    _nchunk = int(_os.environ.get("DBG_O2_CHUNKS", "8"))
                    cw = sbuf(st, "cw", [128, 32])
                    cbias = sbuf(st, "cbias", [128, 8])
                    rab = sbuf(st, "rab", [128, 16])
                    ixb = sbuf(st, "ixb", [128, 16])
                    lam = sbuf(st, "lam", [128, 16])
                    nsp = sbuf(st, "nsp", [128, 16])
                    nsp2 = sbuf(st, "nsp2", [128, 16])
                    tp_ = Tok()
                    for (dst, src) in ((cw, convw_d[o]), (cbias, convb_d[o]), (rab, rab_d[o]), (ixb, ixb_d[o]), (lam, rlam_d[o])):
                        P.dma("sp", dst[:], src, W=[tp_])
                    rabd = sbuf(st, "rabd", [128, 16, 128], BF16)
                    ixbd = sbuf(st, "ixbd", [128, 16, 128], BF16)
                    for hf in range(2):
                        P.dma("pool", rabd[:, hf * 8:(hf + 1) * 8, :],
                              rabd_d[o].rearrange("p (c m) -> p c m", m=128)[:, hf * 8:(hf + 1) * 8, :], W=[tp_])
                        P.dma("pool", ixbd[:, hf * 8:(hf + 1) * 8, :],
                              ixbd_d[o].rearrange("p (c m) -> p c m", m=128)[:, hf * 8:(hf + 1) * 8, :], W=[tp_])
                    P.op("act", "activation", out=nsp[:], in_=lam[:], func=AF.Exp, scale=-1.0, R=[tp_], W=[tp_])
                    P.op("act", "activation", out=nsp[:], in_=nsp[:], func=AF.Ln, bias=one_c[:], R=[tp_] + CT, W=[tp_])
                    P.op("dve", "tensor_scalar", out=nsp2[:], in0=nsp[:], scalar1=-16.0, scalar2=None, op0=ALU.mult, R=[tp_], W=[tp_])
                    P.op("dve", "tensor_scalar", out=nsp[:], in0=nsp[:], scalar1=-8.0, scalar2=None, op0=ALU.mult, R=[tp_], W=[tp_])
                    xc = sbuf(st, "xc", [128, S])
                    xcb = sbuf(st, "xcb", [128, S], BF16)
                    Rb = sbuf(st, "Rb", [128, S])
                    Ib = sbuf(st, "Ib", [128, S])
                    Ab = sbuf(st, "Ab", [128, S])
                    Yb = sbuf(st, "Yb", [128, S])
                    txc, txcb = Tok(), toks(NT)
                    tR, tI, tA_, tY = toks(NT), toks(NT), toks(NT), toks(NT)
                    gtile = [sbuf(st, "gt%d" % i, [128, S], BF16) for i in range(2)]
                    tgt_ = toks(2)
                    ot = [sbuf(st, "ot%d" % i, [128, TW], BF16) for i in range(2)]
                    tot = toks(2)
                    gg = [sbuf(st, "gg%d" % i, [128, TW]) for i in range(2)]
                    tgg = toks(2)
                    n_ = 0
                    for c in range(_nchunk):
                        P.dma("sp", gtile[c % 2][:], gate_d[:, c, :], W=[tgt_[c % 2]])
                        P.op("dve", "tensor_scalar", out=xc[:], in0=xr[c][:, 0:S], scalar1=cw[:, c * 4:c * 4 + 1],
                             scalar2=cbias[:, c:c + 1], op0=ALU.mult, op1=ALU.add, R=[txr[c], tp_], W=[txc])
                        for j in range(1, 4):
                            P.op("dve", "scalar_tensor_tensor", out=xc[:], in0=xr[c][:, j:j + S], scalar=cw[:, c * 4 + j:c * 4 + j + 1],
                                 in1=xc[:], op0=ALU.mult, op1=ALU.add, R=[txr[c], tp_, txc], W=[txc])
                        for tt in range(NT):
                            ts_ = slice(tt * TW, (tt + 1) * TW)
                            P.op("act", "activation", out=xcb[:, ts_], in_=xc[:, ts_], func=AF.Copy, R=[txc], W=[txcb[tt]])
                        for dr in range(2):
                            dc = dr * 8 + c
                            for tt in range(NT):
                                ts_ = slice(tt * TW, (tt + 1) * TW)
                                b = 1 + (n_ % 3)
                                b2_ = 4 + (n_ % 3)
                                n_ += 1
                                P.op("pe", "matmul", PS[b][:, :], lhsT=rabd[:, dc, :], rhs=xcb[:, ts_], start=True, stop=True,
                                     R=[tp_, txcb[tt]], W=[PSt[b]])
                                P.op("pe", "matmul", PS[b2_][:, :], lhsT=ixbd[:, dc, :], rhs=xcb[:, ts_], start=True, stop=True,
                                     R=[tp_, txcb[tt]], W=[PSt[b2_]])
                                P.op("act", "activation", out=Rb[:, ts_], in_=PS[b][:, :], func=AF.Sigmoid, bias=rab[:, dc:dc + 1],
                                     R=[PSt[b], tp_], W=[tR[tt]])
                                P.op("act", "activation", out=Ib[:, ts_], in_=PS[b2_][:, :], func=AF.Sigmoid, bias=ixb[:, dc:dc + 1],
                                     R=[PSt[b2_], tp_], W=[tI[tt]])
                            for tt in range(NT):
                                ts_ = slice(tt * TW, (tt + 1) * TW)
                                P.op("act", "activation", out=Ab[:, ts_], in_=Rb[:, ts_], func=AF.Exp, scale=nsp[:, dc:dc + 1],
                                     R=[tR[tt], tp_], W=[tA_[tt]])
                                P.op("act", "activation", out=Rb[:, ts_], in_=Rb[:, ts_], func=AF.Exp, scale=nsp2[:, dc:dc + 1],
                                     R=[tR[tt], tp_], W=[tR[tt]])
                            for tt in range(NT):
                                ts_ = slice(tt * TW, (tt + 1) * TW)
                                P.op("act", "activation", out=Rb[:, ts_], in_=Rb[:, ts_], func=AF.Sqrt, scale=-1.0, bias=one_c[:],
                                     R=[tR[tt]] + CT, W=[tR[tt]])
                                P.op("pool", "tensor_tensor", out=Ib[:, ts_], in0=Ib[:, ts_], in1=xc[:, ts_], op=ALU.mult,
                                     R=[tI[tt], txc], W=[tI[tt]])
                                P.op("dve", "tensor_tensor", out=Ib[:, ts_], in0=Ib[:, ts_], in1=Rb[:, ts_], op=ALU.mult,
                                     R=[tI[tt], tR[tt]], W=[tI[tt]])
                            order = range(NT) if dr == 0 else range(NT - 1, -1, -1)
                            prev = None
                            dst = Yb if dr == 0 else Rb
                            tdst = tY if dr == 0 else tR
                            for tt in order:
                                ts_ = slice(tt * TW, (tt + 1) * TW)
                                if dr == 0:
                                    init = 0.0 if prev is None else dst[:, prev * TW + TW - 1:prev * TW + TW]
                                    P.op("dve", "tensor_tensor_scan", out=dst[:, ts_], data0=Ab[:, ts_], data1=Ib[:, ts_], initial=init,
                                         op0=ALU.mult, op1=ALU.add,
                                         R=[tA_[tt], tI[tt]] + ([tdst[prev]] if prev is not None else []), W=[tdst[tt]])
                                else:
                                    init = 0.0 if prev is None else dst[:, prev * TW:prev * TW + 1]
                                    P.op("dve", "tensor_tensor_scan", out=dst[:, ts_][:, ::-1], data0=Ab[:, ts_][:, ::-1],
                                         data1=Ib[:, ts_][:, ::-1], initial=init, op0=ALU.mult, op1=ALU.add,
                                         R=[tA_[tt], tI[tt]] + ([tdst[prev]] if prev is not None else []), W=[tdst[tt]])
                                prev = tt
                        for tt in range(NT):
                            ts_ = slice(tt * TW, (tt + 1) * TW)
                            i = tt % 2
                            P.op("act", "activation", out=gg[i][:], in_=gtile[c % 2][:, ts_], func=AF.Gelu_apprx_tanh,
                                 R=[tgt_[c % 2]], W=[tgg[i]])
                            P.op("pool", "tensor_tensor", out=Yb[:, ts_], in0=Yb[:, ts_], in1=Rb[:, ts_], op=ALU.add,
                                 R=[tY[tt], tR[tt]], W=[tY[tt]])
                            P.op("dve", "tensor_tensor", out=ot[i][:], in0=Yb[:, ts_], in1=gg[i][:], op=ALU.mult,
                                 R=[tY[tt], tgg[i]], W=[tot[i]])
                            P.dma("sp", mix_d[:, c, ts_], ot[i][:], R=[tot[i]], W=[])
                    P.barrier()
                    P.emit()
            phase_tail(l, rout_d[o])

        for l in range(n_layers):
            if l % 2 == 0:
                phase_even(l)
            else:
                phase_odd(l)

        with ExitStack() as st:
            hh = [sbuf(st, "fh%d" % i, [128, 8, TW]) for i in range(2)]
            thh = toks(2)
            hn = sbuf(st, "fhn", [128, 8, TW])
            thn = Tok()
            ob = [sbuf(st, "fo%d" % i, [128, D]) for i in range(2)]
            tob = toks(2)
            nsc = norm_scratch(st, TW)
            tfin = []
            n_ = 0
            for tt in range(NT):
                ts_ = slice(tt * TW, (tt + 1) * TW)
                h, th = hh[tt % 2], thh[tt % 2]
                P.dma("sp", h[:], h_d[:, :, ts_], W=[th])
                norm_tile(nsc, h, th, TW, hn, thn, lambda k: fnw[:, k:k + 1], None, PS[0], PSt[0])
                for j in range(4):
                    o_, to_ = ob[n_ % 2], tob[n_ % 2]
                    for b in range(2):
                        bank = 1 + 2 * (n_ % 2) + b
                        for kk in range(4):
                            k = 4 * b + kk
                            P.op("pe", "transpose", out=PS[bank][:, kk * 128:(kk + 1) * 128], in_=hn[:, k, j * 128:(j + 1) * 128],
                                 identity=ident, R=[thn] + CT, W=[PSt[bank]])
                        if b == 0:
                            P.op("act", "activation", out=o_[:, 0:512], in_=PS[bank][:, :], func=AF.Copy, R=[PSt[bank]], W=[to_])
                        else:
                            P.op("dve", "tensor_copy", out=o_[:, 512:1024], in_=PS[bank][:, :], R=[PSt[bank]], W=[to_])
                    n_ += 1
                    tk = Tok()
                    r0 = tt * TW + j * 128
                    P.dma("sp", out_d[r0:r0 + 128, :], o_[:], R=[to_], W=[tk])
                    tfin.append(tk)
            for tk in tfin:
                P._wait("sp", tk.w[0], tk.w[1])
            P.barrier()
            P.emit()
        print("[kernel] program built: %d instructions" % P.ninst)
    return nc


def _fm(v, k):
    v = np.asarray(v, np.float32)
    lead = v.shape[:-1]
    v = v.reshape(lead + (k, 128))
    v = np.moveaxis(v, -1, 0)
    return np.ascontiguousarray(v)


def _rope_tables():
    rows = S // 64
    row_idx = np.repeat(np.arange(rows, dtype=np.float64), 64)
    col_idx = np.tile(np.arange(64, dtype=np.float64), rows)
    inv_freq = 10000.0 ** (-np.arange(16, dtype=np.float64) / 16)
    cos = np.zeros((128, S), np.float64)
    sin = np.zeros((128, S), np.float64)
    for p in range(128):
        d = p % 64
        a = d // 32
        b = (d // 16) % 2
        f = d % 16
        ang = (row_idx if a == 0 else col_idx) * inv_freq[f]
        cos[p] = np.cos(ang)
        sin[p] = (-np.sin(ang)) if b == 0 else np.sin(ang)
    return cos.astype(np.float32), sin.astype(np.float32)


def _consts():
    c = np.zeros((128, 400), np.float32)
    c[:, 0:128] = np.eye(128, dtype=np.float32)
    c[0:64, 128:192] = 1.0
    c[64:128, 192:256] = 1.0
    for k in range(128):
        c[k, 256 + (k + 64) % 128] = 1.0
    c[0:64, 384] = -1.0
    c[64:128, 384] = 1.0
    c[0:64, 385] = 1.0
    c[64:128, 385] = -1.0
    for g in range(8):
        c[16 * g:16 * g + 16, 386 + g] = 1.0
    return c


def prepare_shared(inp):
    f = lambda a: np.ascontiguousarray(np.asarray(a, np.float32))
    sh = {}
    sh["ada_w"] = f(inp["ada_w"])
    sh["ada_b_fm"] = np.ascontiguousarray(f(inp["ada_b"]).reshape(4, 48, 128).transpose(2, 0, 1).reshape(128, 192))
    sh["norm_w_fm"] = np.ascontiguousarray(f(inp["norm_w"]).reshape(4, 2, 8, 128).transpose(3, 0, 1, 2).reshape(128, 64))
    sh["fnw_fm"] = np.ascontiguousarray(f(inp["final_norm_w"]).reshape(8, 128).T)
    sh["mlp_w1"] = f(inp["mlp_w1"])
    sh["mlp_w2"] = f(inp["mlp_w2"])
    idx = np.arange(128)
    perm = idx ^ 16
    cols = list(range(512))
    for j in range(4):
        cols += list(512 + 128 * j + idx)
    for j in range(4):
        cols += list(512 + 128 * j + perm)
    for kv in range(2):
        cols += list(1024 + 64 * kv + (idx % 64))
    for kv in range(2):
        cols += list(1024 + 64 * kv + (perm % 64))
    cols += list(range(1152, 1280))
    cols = np.asarray(cols)
    assert cols.shape[0] == 2176
    sh["hyb_w_in_ext"] = np.ascontiguousarray(f(inp["hyb_w_in"])[:, :, cols])
    sh["hyb_w_out"] = f(inp["hyb_w_out"])

    def stk(a):
        a = f(a).transpose(0, 3, 1, 2).reshape(2, 64, 64)
        return np.ascontiguousarray(np.concatenate([a, a], axis=1))
    sh["s5_lamre"] = stk(inp["s5_lam_re"])
    sh["s5_lamim"] = stk(inp["s5_lam_im"])
    sh["s5_logdt"] = np.ascontiguousarray(np.broadcast_to(f(inp["s5_log_dt"]).reshape(2, 1, 64), (2, 128, 64)))
    bre = f(inp["s5_b_re"]).transpose(0, 3, 1, 2, 4).reshape(2, 64, 1024)
    bim = f(inp["s5_b_im"]).transpose(0, 3, 1, 2, 4).reshape(2, 64, 1024)
    sh["s5_b1"] = np.ascontiguousarray(np.concatenate([bre, bim], axis=1))
    sh["s5_b2"] = np.ascontiguousarray(np.concatenate([bim, bre], axis=1))
    cre = f(inp["s5_c_re"]).transpose(0, 4, 1, 2, 3).reshape(2, 64, 1024)
    cim = f(inp["s5_c_im"]).transpose(0, 4, 1, 2, 3).reshape(2, 64, 1024)
    sh["s5_c1"] = np.ascontiguousarray(np.concatenate([cre, cim], axis=1))
    sh["s5_c2"] = np.ascontiguousarray(np.concatenate([cim, cre], axis=1))
    sh["s5_d_fm"] = np.ascontiguousarray(f(inp["s5_d"]).reshape(2, 4, 128).transpose(0, 2, 1))
    sh["s5_glu_w"] = f(inp["s5_glu_w"])
    sh["s5_glu_b_fm"] = np.ascontiguousarray(f(inp["s5_glu_b"]).reshape(2, 4, 128).transpose(0, 2, 1))
    qn = f(inp["attn_q_norm"])
    kn = f(inp["attn_k_norm"])
    d = idx % 64
    dp = (idx ^ 16) % 64
    sh["qkn_fm"] = np.ascontiguousarray(np.stack([qn[:, d], qn[:, dp], kn[:, d], kn[:, dp]], axis=-1))
    sh["rec_w_in"] = f(inp["rec_w_in"])
    sh["rec_w_out"] = f(inp["rec_w_out"])
    sh["rec_conv_fm"] = np.ascontiguousarray(f(inp["rec_conv_w"]).reshape(2, 4, 8, 128).transpose(0, 3, 2, 1).reshape(2, 128, 32))
    sh["rec_convb_fm"] = np.ascontiguousarray(f(inp["rec_conv_b"]).reshape(2, 8, 128).transpose(0, 2, 1))

    def bd(w):
        w = f(w)
        o = np.zeros((2, 128, 2, 8, 128), np.float32)
        for c in range(8):
            for hl in range(2):
                o[:, 64 * hl:64 * hl + 64, :, c, 64 * hl:64 * hl + 64] = w[:, :, 2 * c + hl].transpose(0, 2, 1, 3)
        return np.ascontiguousarray(o.reshape(2, 128, 2048))
    sh["rec_ra_bd"] = bd(inp["rec_ra_w"])
    sh["rec_ix_bd"] = bd(inp["rec_ix_w"])

    def fm2(a):
        return np.ascontiguousarray(f(a).reshape(2, 2, 8, 128).transpose(0, 3, 1, 2).reshape(2, 128, 16))
    sh["rec_rab_fm"] = fm2(inp["rec_ra_b"])
    sh["rec_ixb_fm"] = fm2(inp["rec_ix_b"])
    sh["rec_lam_fm"] = fm2(inp["rec_lam"])
    rc, rs = _rope_tables()
    sh["rope_cos"] = rc
    sh["rope_sin"] = rs
    sh["consts"] = _consts()
    return sh


_CACHE = {}


def run(inputs, core_batches, n_layers=4, trace=False):
    key = n_layers
    if key not in _CACHE:
        _CACHE[key] = build_program(n_layers)
    nc = _CACHE[key]
    sh = prepare_shared(inputs)
    x = np.asarray(inputs["x"], np.float32)
    c = np.asarray(inputs["c"], np.float32)
    in_maps = []
    for b in core_batches:
        m = dict(sh)
        m["x"] = np.ascontiguousarray(x[b])
        m["c_fm"] = np.ascontiguousarray(c[b].reshape(8, 128).T)
        in_maps.append(m)
    res = run_bass_kernel_spmd(nc, in_maps, core_ids=list(range(len(core_batches))), **({"trace": True} if trace else {}))
    outs = np.stack([np.asarray(r["out"], np.float32) for r in res.results], axis=0)
    return outs, res


def kernel(**inputs):
    outs, _ = run(inputs, list(range(8)), 4)
    return outs.astype(np.float32)
```
